# Optimizing a Trainium2 kernel written in Bass

```python
import math
import jax, jax.numpy as jnp
from jax import lax
import numpy as np


D_MODEL = 1024
BATCH = 1
SEQ = 16384
DEPTH = 2
DEC_BATCH = 2
DEC_SEQ = 16384
PAST_LEN = 128

HEAD_DIM = 64
CONV_CH = 384
CONV_WIDTH = 3
DIFF_HEADS = 4
DIFF_QK_DIM = 32
DIFF_V_DIM = 2 * DIFF_QK_DIM
DIL_HEADS = 6
DIL_PAIRS = ((128, 1), (512, 4), (2048, 16))
DIL_HALF = 64
N_BUCKETS = 32
MAX_DISTANCE = 1024
D_FF = 2816
Q_BLOCK = 128
EPS = 1e-6
MIX_WIDTH = CONV_CH + DIFF_HEADS * DIFF_V_DIM + DIL_HEADS * HEAD_DIM
W_CONV = 3 * CONV_CH
W_DIFF_QK = DIFF_HEADS * 2 * DIFF_QK_DIM
W_DIFF_V = DIFF_HEADS * DIFF_V_DIM
W_DIL = DIL_HEADS * HEAD_DIM
IN_WIDTH = W_CONV + 2 * W_DIFF_QK + W_DIFF_V + 3 * W_DIL
SPLITS = (CONV_CH, 2 * CONV_CH, W_CONV,
          W_CONV + W_DIFF_QK, W_CONV + 2 * W_DIFF_QK, W_CONV + 2 * W_DIFF_QK + W_DIFF_V,
          W_CONV + 2 * W_DIFF_QK + W_DIFF_V + W_DIL, W_CONV + 2 * W_DIFF_QK + W_DIFF_V + 2 * W_DIL)

kernel_name = "hybrid_parallel_encoder_two_batches"


def rms_norm(x, g):
    xf = x.astype(jnp.float32)
    y = xf * lax.rsqrt(jnp.mean(xf * xf, axis=-1, keepdims=True) + EPS)
    return (y * g.astype(jnp.float32)).astype(x.dtype)


def rel_bucket(rel):
    nb = N_BUCKETS // 2
    max_exact = nb // 2
    ret = jnp.where(rel > 0, nb, 0)
    n = jnp.abs(rel)
    nf = jnp.maximum(n, 1).astype(jnp.float32)
    large = max_exact + (jnp.log(nf / max_exact) / math.log(MAX_DISTANCE / max_exact)
                         * (nb - max_exact)).astype(jnp.int32)
    large = jnp.minimum(large, nb - 1)
    return ret + jnp.where(n < max_exact, n, large)


def swiglu(h, w_gu, w_down):
    g, u = jnp.split(h @ w_gu, 2, axis=-1)
    return (jax.nn.silu(g) * u) @ w_down


def short_conv(u, w):
    c = u.shape[-1]
    return lax.conv_general_dilated(u, w[:, None, :].astype(u.dtype), window_strides=(1,),
                                    padding=((CONV_WIDTH // 2, CONV_WIDTH // 2),),
                                    dimension_numbers=('NWC', 'WIO', 'NWC'), feature_group_count=c)


def diff_attention(q, k, v, bias_tab, lam, lam_init, sub_g):
    b, s_len, h, _, dq = q.shape
    dv = v.shape[-1]
    nblk = s_len // Q_BLOCK
    qb = q.reshape(b, nblk, Q_BLOCK, h, 2, dq).swapaxes(0, 1)
    kpos = jnp.arange(s_len)

    def block(args):
        qblk, i = args
        qpos = i * Q_BLOCK + jnp.arange(Q_BLOCK)
        bias = bias_tab[rel_bucket(kpos[None, :] - qpos[:, None])].astype(jnp.float32)
        s = jnp.einsum('bqhcd,bkhcd->bchqk', qblk, k).astype(jnp.float32) \
            + jnp.transpose(bias, (2, 0, 1))[None, None]
        p = jax.nn.softmax(s, axis=-1)
        w = p[:, 0] - lam * p[:, 1]
        return jnp.einsum('bhqk,bkhd->bqhd', w.astype(v.dtype), v)

    out = lax.map(block, (qb, jnp.arange(nblk)))
    out = out.swapaxes(0, 1).reshape(b, s_len, h, dv)
    out = rms_norm(out, sub_g) * (1.0 - lam_init)
    return out.reshape(b, s_len, h * dv)


def dilated_attention(q, k, v, bias_tab):
    b, s_len, h, d = q.shape
    nblk = s_len // Q_BLOCK
    m = jnp.arange(-DIL_HALF, DIL_HALF + 1)
    offs = [m * r for (_, r) in DIL_PAIRS]
    biases = [bias_tab[rel_bucket(o)].T.astype(jnp.float32) for o in offs]
    qb = q.reshape(b, nblk, Q_BLOCK, h, d).swapaxes(0, 1)

    def block(args):
        qblk, i = args
        qpos = i * Q_BLOCK + jnp.arange(Q_BLOCK)
        outs, lses = [], []
        for off, bias in zip(offs, biases):
            idx = qpos[:, None] + off[None, :]
            valid = (idx >= 0) & (idx < s_len)
            idx = jnp.clip(idx, 0, s_len - 1)
            kg = k[:, idx]
            vg = v[:, idx]
            s = jnp.einsum('bqhd,bqjhd->bhqj', qblk, kg).astype(jnp.float32) + bias[None, :, None, :]
            s = jnp.where(valid[None, None], s, -jnp.inf)
            lse = jax.nn.logsumexp(s, axis=-1, keepdims=True)
            p = jnp.exp(s - lse)
            outs.append(jnp.einsum('bhqj,bqjhd->bqhd', p.astype(v.dtype), vg).astype(jnp.float32))
            lses.append(lse[..., 0])
        alpha = jax.nn.softmax(jnp.stack(lses), axis=0)
        alpha = jnp.swapaxes(alpha, 2, 3)[..., None]
        return jnp.sum(alpha * jnp.stack(outs), axis=0).astype(v.dtype)

    out = lax.map(block, (qb, jnp.arange(nblk)))
    return out.swapaxes(0, 1).reshape(b, s_len, h * d)


def trunk(x, p):
    b, s_len, _ = x.shape
    for l in range(DEPTH):
        lam_init = 0.8 - 0.6 * math.exp(-0.3 * l)
        x = x + 0.5 * swiglu(rms_norm(x, p['ffn1_norm'][l]), p['ffn1_w_gu'][l], p['ffn1_w_down'][l])
        h = rms_norm(x, p['mix_norm'][l])
        proj = h @ p['w_in'][l]
        (u, bg, cg, dq, dk, dv, lq, lk, lv) = jnp.split(proj, SPLITS, axis=-1)
        y_a = bg * short_conv(cg * u, p['conv_w'][l])
        dq = rms_norm(dq.reshape(b, s_len, DIFF_HEADS, 2, DIFF_QK_DIM), p['diff_q_norm'][l]) * (DIFF_QK_DIM ** -0.5)
        dk = rms_norm(dk.reshape(b, s_len, DIFF_HEADS, 2, DIFF_QK_DIM), p['diff_k_norm'][l])
        dv = dv.reshape(b, s_len, DIFF_HEADS, DIFF_V_DIM)
        lq1 = p['lambda_q1'][l].astype(jnp.float32)
        lk1 = p['lambda_k1'][l].astype(jnp.float32)
        lq2 = p['lambda_q2'][l].astype(jnp.float32)
        lk2 = p['lambda_k2'][l].astype(jnp.float32)
        lam = jnp.exp(jnp.sum(lq1 * lk1)) - jnp.exp(jnp.sum(lq2 * lk2)) + lam_init
        y_b = diff_attention(dq, dk, dv, p['rel_bias'][:, :DIFF_HEADS], lam, lam_init, p['diff_sub_norm'][l])
        lq = rms_norm(lq.reshape(b, s_len, DIL_HEADS, HEAD_DIM), p['dil_q_norm'][l]) * (HEAD_DIM ** -0.5)
        lk = rms_norm(lk.reshape(b, s_len, DIL_HEADS, HEAD_DIM), p['dil_k_norm'][l])
        lv = lv.reshape(b, s_len, DIL_HEADS, HEAD_DIM)
        y_c = dilated_attention(lq, lk, lv, p['rel_bias'][:, DIFF_HEADS:])
        x = x + jnp.concatenate([y_a, y_b.astype(x.dtype), y_c], axis=-1) @ p['w_out'][l]
        x = x + 0.5 * swiglu(rms_norm(x, p['ffn2_norm'][l]), p['ffn2_w_gu'][l], p['ffn2_w_down'][l])
        x = rms_norm(x, p['final_norm'][l])
    return x


def setup_inputs(seed: int = 0) -> dict:
    key = jax.random.key(seed)
    ks = jax.random.split(key, 24)
    f32 = jnp.float32

    def nrm(k, shape, scale):
        return jax.random.normal(k, shape, f32) * scale

    def gain(k, shape):
        return 1.0 + 0.02 * jax.random.normal(k, shape, f32)

    return {
        'x_prompt': nrm(ks[0], (BATCH, SEQ, D_MODEL), 1.0),
        'x_sample': nrm(ks[1], (DEC_BATCH, DEC_SEQ, D_MODEL), 1.0),
        'ffn1_norm': gain(ks[2], (DEPTH, D_MODEL)),
        'ffn1_w_gu': nrm(ks[3], (DEPTH, D_MODEL, 2 * D_FF), D_MODEL ** -0.5),
        'ffn1_w_down': nrm(ks[4], (DEPTH, D_FF, D_MODEL), D_FF ** -0.5),
        'mix_norm': gain(ks[5], (DEPTH, D_MODEL)),
        'w_in': nrm(ks[6], (DEPTH, D_MODEL, IN_WIDTH), D_MODEL ** -0.5),
        'conv_w': nrm(ks[7], (DEPTH, CONV_WIDTH, CONV_CH), CONV_WIDTH ** -0.5),
        'diff_q_norm': gain(ks[8], (DEPTH, DIFF_QK_DIM)),
        'diff_k_norm': gain(ks[9], (DEPTH, DIFF_QK_DIM)),
        'lambda_q1': nrm(ks[10], (DEPTH, DIFF_QK_DIM), 0.1),
        'lambda_k1': nrm(ks[11], (DEPTH, DIFF_QK_DIM), 0.1),
        'lambda_q2': nrm(ks[12], (DEPTH, DIFF_QK_DIM), 0.1),
        'lambda_k2': nrm(ks[13], (DEPTH, DIFF_QK_DIM), 0.1),
        'diff_sub_norm': gain(ks[14], (DEPTH, DIFF_V_DIM)),
        'dil_q_norm': gain(ks[15], (DEPTH, HEAD_DIM)),
        'dil_k_norm': gain(ks[16], (DEPTH, HEAD_DIM)),
        'w_out': nrm(ks[17], (DEPTH, MIX_WIDTH, D_MODEL), MIX_WIDTH ** -0.5),
        'ffn2_norm': gain(ks[18], (DEPTH, D_MODEL)),
        'ffn2_w_gu': nrm(ks[19], (DEPTH, D_MODEL, 2 * D_FF), D_MODEL ** -0.5),
        'ffn2_w_down': nrm(ks[20], (DEPTH, D_FF, D_MODEL), D_FF ** -0.5),
        'final_norm': gain(ks[21], (DEPTH, D_MODEL)),
        'rel_bias': nrm(ks[22], (N_BUCKETS, DIFF_HEADS + DIL_HEADS), 0.2),
    }


def reference(x_prompt, x_sample, ffn1_norm, ffn1_w_gu, ffn1_w_down, mix_norm, w_in, conv_w,
              diff_q_norm, diff_k_norm, lambda_q1, lambda_k1, lambda_q2, lambda_k2, diff_sub_norm,
              dil_q_norm, dil_k_norm, w_out, ffn2_norm, ffn2_w_gu, ffn2_w_down, final_norm, rel_bias):
    p = dict(ffn1_norm=ffn1_norm, ffn1_w_gu=ffn1_w_gu, ffn1_w_down=ffn1_w_down, mix_norm=mix_norm,
             w_in=w_in, conv_w=conv_w, diff_q_norm=diff_q_norm, diff_k_norm=diff_k_norm,
             lambda_q1=lambda_q1, lambda_k1=lambda_k1, lambda_q2=lambda_q2, lambda_k2=lambda_k2,
             diff_sub_norm=diff_sub_norm, dil_q_norm=dil_q_norm, dil_k_norm=dil_k_norm, w_out=w_out,
             ffn2_norm=ffn2_norm, ffn2_w_gu=ffn2_w_gu, ffn2_w_down=ffn2_w_down, final_norm=final_norm,
             rel_bias=rel_bias)
    y_prompt = trunk(x_prompt, p)
    y_sample = trunk(x_sample, p)
    return (y_prompt, y_sample)
```

```python
import math
import numpy as np
import ml_dtypes
import concourse.bass as bass
import concourse.mybir as mybir
from concourse.bass_utils import run_bass_kernel_spmd

F32 = mybir.dt.float32
BF16 = mybir.dt.bfloat16
AF = mybir.ActivationFunctionType
ALU = mybir.AluOpType
NPBF = ml_dtypes.bfloat16

NCORE = 8
S = 16384
SL = S // NCORE
NT = 3 * SL
D = 1024
DFF = 2816
EPS = 1e-6
TW = 3327
HW_ = 3200
CC = 1663


class Op:
    __slots__ = ("fn", "waits", "signal", "dkey", "dcnt")

    def __init__(self, fn):
        self.fn = fn
        self.waits = []
        self.signal = False
        self.dkey = None
        self.dcnt = 0


class Prog:
    ENG = ("pe", "act", "dve", "pool", "sp")

    def __init__(self, nc):
        self.nc = nc
        self.ops = {e: [] for e in self.ENG}
        self.res = {}
        self.dcount = {}
        self.waited = {e: {} for e in self.ENG}

    def _st(self, k):
        st = self.res.get(k)
        if st is None:
            st = {"w": {}, "r": {}, "wd": {}, "rd": {}}
            self.res[k] = st
        return st

    def op(self, eng, fn, reads=(), writes=(), dkey=None):
        lst = self.ops[eng]
        idx = len(lst)
        o = Op(fn)
        edeps = {}
        ddeps = {}

        def add(dst, src):
            for k, v in src.items():
                if dst.get(k, -1) < v:
                    dst[k] = v

        for k in reads:
            st = self._st(k)
            add(edeps, st["w"])
            add(ddeps, st["wd"])
        for k in writes:
            st = self._st(k)
            add(edeps, st["w"])
            add(edeps, st["r"])
            add(ddeps, st["wd"])
            add(ddeps, st["rd"])
        wd = self.waited[eng]
        for e2, i2 in edeps.items():
            if e2 == eng and eng in ("pe", "sp"):
                continue
            if wd.get(("e", e2), -1) >= i2:
                continue
            wd[("e", e2)] = i2
            self.ops[e2][i2].signal = True
            o.waits.append(("e", e2, i2))
        for dk, c in ddeps.items():
            if wd.get(("d", dk), -1) >= c:
                continue
            wd[("d", dk)] = c
            o.waits.append(("d", dk, c))
        if dkey is not None:
            c = self.dcount.get(dkey, 0) + 16
            self.dcount[dkey] = c
            o.dkey = dkey
            o.dcnt = c
            for k in reads:
                self._st(k)["rd"][dkey] = c
            for k in writes:
                self._st(k)["wd"][dkey] = c
        else:
            for k in reads:
                self._st(k)["r"][eng] = idx
            for k in writes:
                self._st(k)["w"][eng] = idx
        lst.append(o)
        return o

    def dma(self, out, in_, reads, writes, dkey, eng="sp"):
        return self.op(eng, lambda e: e.dma_start(out=out, in_=in_), reads, writes, dkey=dkey)

    def final_wait(self):
        o = Op(None)
        for e in self.ENG:
            if e != "sp" and self.ops[e]:
                self.ops[e][-1].signal = True
                o.waits.append(("e", e, len(self.ops[e]) - 1))
        for dk, c in self.dcount.items():
            o.waits.append(("d", dk, c))
        self.ops["sp"].append(o)

    def emit(self):
        nc = self.nc
        import contextlib
        with contextlib.ExitStack() as es:
            esem = {e: es.enter_context(nc.semaphore("s_" + e)) for e in self.ENG}
            dsem = {dk: es.enter_context(nc.semaphore("d_%d" % i)) for i, dk in enumerate(self.dcount)}
            cnt = {}
            for e in self.ENG:
                c = 0
                arr = []
                for o in self.ops[e]:
                    if o.signal and o.dkey is None:
                        c += 1
                    arr.append(c)
                cnt[e] = arr
            block = es.enter_context(nc.Block())

            def run(eng_name, eng):
                for o in self.ops[eng_name]:
                    for w in o.waits:
                        if w[0] == "e":
                            eng.wait_ge(esem[w[1]], cnt[w[1]][w[2]])
                        else:
                            eng.wait_ge(dsem[w[1]], w[2])
                    if o.fn is None:
                        continue
                    ins = o.fn(eng)
                    if o.dkey is not None:
                        ins.then_inc(dsem[o.dkey], 16)
                    elif o.signal:
                        ins.then_inc(esem[eng_name], 1)

            @block.tensor
            def _(e):
                run("pe", e)

            @block.scalar
            def _(e):
                run("act", e)

            @block.vector
            def _(e):
                run("dve", e)

            @block.gpsimd
            def _(e):
                run("pool", e)

            @block.sync
            def _(e):
                run("sp", e)


class Ctx:
    pass


def load_cols(P, c, name, dram_vec_ap, ncols_chunks):
    pass


def norm_T(P, c, xt, gcols, tag):
    nc = P.nc
    for j in range(4):
        P.op("act", lambda e, j=j: e.activation(out=c.junk[:, :], in_=xt[:, j, :], func=AF.Square,
                                                accum_out=c.ss[:, j:j + 1]),
             reads=["xt"], writes=["junk", "ss"])
    P.op("act", lambda e: e.activation(out=c.rs[:, 0:4], in_=c.ss[:, 0:4], func=AF.Sqrt,
                                       bias=c.epsc[:, 0:1], scale=1.0 / D),
         reads=["ss"], writes=["rs"])
    P.op("dve", lambda e: e.reciprocal(out=c.rs[:, 0:4], in_=c.rs[:, 0:4]), reads=["rs"], writes=["rs"])
    for j in range(4):
        P.op("dve", lambda e, j=j: e.tensor_scalar(out=c.h[:, j, :], in0=xt[:, j, :], scalar1=c.rs[:, j:j + 1],
                                                   scalar2=None, op0=ALU.mult),
             reads=["xt", "rs"], writes=["h"])
    for kc in range(8):
        pst = c.pst[kc % 2]
        pk = "pst%d" % (kc % 2)
        for j in range(4):
            P.op("pe", lambda e, j=j, kc=kc, pst=pst: e.transpose(out=pst[:, j * 128:(j + 1) * 128],
                                                                  in_=c.h[:, j, kc * 128:(kc + 1) * 128],
                                                                  identity=c.ident[:, :]),
                 reads=["h"], writes=[pk])
        if kc % 2 == 0:
            P.op("act", lambda e, kc=kc, pst=pst: e.activation(out=c.hT[:, kc, :], in_=pst[:, 0:512], func=AF.Copy,
                                                               scale=gcols[:, kc:kc + 1]),
                 reads=[pk], writes=["hT"])
        else:
            P.op("dve", lambda e, kc=kc, pst=pst: e.tensor_scalar(out=c.hT[:, kc, :], in0=pst[:, 0:512],
                                                                  scalar1=gcols[:, kc:kc + 1], scalar2=None,
                                                                  op0=ALU.mult),
                 reads=[pk], writes=["hT"])


def ffn(P, c, xt, wgu, wdn_sb, cnt):
    wgu_v = wgu.rearrange("(kc p) n -> p kc n", p=128)
    for j in range(22):
        slot = cnt[0] % 2
        cnt[0] += 1
        wk = "wup%d" % slot
        wt = c.wup[slot]
        P.dma(wt[:, :, 0:128], wgu_v[:, :, j * 128:(j + 1) * 128], reads=[], writes=[wk], dkey=wk, eng="pool")
        P.dma(wt[:, :, 128:256], wgu_v[:, :, DFF + j * 128:DFF + (j + 1) * 128], reads=[], writes=[wk], dkey=wk,
              eng="pool")
        pg, pu = c.ps[0 + 2 * (j % 2)], c.ps[1 + 2 * (j % 2)]
        kg, ku = "ps%d" % (2 * (j % 2)), "ps%d" % (1 + 2 * (j % 2))
        for kc in range(8):
            P.op("pe", lambda e, kc=kc, wt=wt, pg=pg: e.matmul(pg[:, :], lhsT=wt[:, kc, 0:128], rhs=c.hT[:, kc, :],
                                                               start=(kc == 0), stop=(kc == 7)),
                 reads=[wk, "hT"], writes=[kg])
        for kc in range(8):
            P.op("pe", lambda e, kc=kc, wt=wt, pu=pu: e.matmul(pu[:, :], lhsT=wt[:, kc, 128:256], rhs=c.hT[:, kc, :],
                                                               start=(kc == 0), stop=(kc == 7)),
                 reads=[wk, "hT"], writes=[ku])
        sg = c.sg[j % 2]
        sk = "sg%d" % (j % 2)
        P.op("act", lambda e, pg=pg, sg=sg: e.activation(out=sg[:, :], in_=pg[:, :], func=AF.Silu),
             reads=[kg], writes=[sk])
        P.op("dve", lambda e, j=j, pu=pu, sg=sg: e.tensor_tensor(out=c.actT[:, j, :], in0=sg[:, :], in1=pu[:, :],
                                                                 op=ALU.mult),
             reads=[sk, ku], writes=["actT"])
    n = 0
    for j in range(4):
        for hf in range(2):
            pd = c.ps[4 + n % 2]
            pk = "ps%d" % (4 + n % 2)
            n += 1
            for jj in range(22):
                P.op("pe", lambda e, j=j, hf=hf, jj=jj, pd=pd: e.matmul(
                    pd[:, :], lhsT=c.actT[:, jj, j * 128:(j + 1) * 128], rhs=wdn_sb[:, jj, hf * 512:(hf + 1) * 512],
                    start=(jj == 0), stop=(jj == 21)), reads=["actT", "wdn"], writes=[pk])
            P.op("dve", lambda e, j=j, hf=hf, pd=pd: e.scalar_tensor_tensor(
                out=xt[:, j, hf * 512:(hf + 1) * 512], in0=pd[:, :], scalar=0.5, in1=xt[:, j, hf * 512:(hf + 1) * 512],
                op0=ALU.mult, op1=ALU.add), reads=[pk, "xt"], writes=["xt"])


def load_wdn(P, c, wdn):
    v = wdn.rearrange("(kc p) n -> p kc n", p=128)
    for q in range(2):
        P.dma(c.wdn[:, q * 11:(q + 1) * 11, :], v[:, q * 11:(q + 1) * 11, :], reads=[], writes=["wdn"], dkey="wdn",
              eng="pool")


def gcol_load(P, dst, vec_ap, nch, key):
    P.dma(dst[:, 0:nch], vec_ap[:, :], reads=[], writes=[key], dkey=key)


def alloc_common(nc, es, c):
    def sb(name, shape, dt):
        return es.enter_context(nc.sbuf_tensor(name, shape, dt))

    c.xt = sb("xt", [128, 4, 1024], F32)
    c.h = sb("h", [128, 4, 1024], BF16)
    c.hT = sb("hT", [128, 8, 512], BF16)
    c.actT = sb("actT", [128, 22, 512], BF16)
    c.junk = sb("junk", [128, 1024], BF16)
    c.ss = sb("ss", [128, 8], F32)
    c.rs = sb("rs", [128, 8], F32)
    c.epsc = sb("epsc", [128, 1], F32)
    c.ident = sb("idents", [128, 128], BF16)
    c.wup = [sb("wup%d" % i, [128, 8, 256], BF16) for i in range(2)]
    c.wdn = sb("wdn", [128, 22, 1024], BF16)
    c.sg = [sb("sg%d" % i, [128, 512], F32) for i in range(2)]
    c.ps = [es.enter_context(nc.psum_tensor("ps%d" % i, [128, 512], F32)) for i in range(6)]
    c.pst = [es.enter_context(nc.psum_tensor("pst%d" % i, [128, 1024], BF16)) for i in range(2)]


def build_phase(mode, layer=0):
    import contextlib
    nc = bass.Bass("TRN2", target_bir_lowering=False)
    P = Prog(nc)
    c = Ctx()
    doC = mode in ("CA", "C")
    doA = mode in ("A", "CA")

    def din(name, shape, dt=F32):
        return nc.dram_tensor(name, shape, dt, kind="ExternalInput").ap()

    def dout(name, shape, dt=F32):
        return nc.dram_tensor(name, shape, dt, kind="ExternalOutput").ap()

    ident_d = din("ident", [128, 128], BF16)
    xin = din("x", [NT, D])
    if doC:
        od = din("od", [NT, 8 * 64])
        ol = din("ol", [NT, 6 * 64])
        zh = din("zh", [384, 12, 514])
        bgd = din("bg", [384, NT])
        wout = din("w_out", [D, D])
        convw = din("conv_w", [128, 9])
        lamv = din("lamv", [1, 128])
        gsub = din("gsub", [1, 64])
        n2 = din("n2", [128, 8])
        nf = din("nf", [1, D])
        wgu2 = din("w_gu2", [D, 2 * DFF])
        wdn2 = din("w_dn2", [DFF, D])
    if doA:
        n1 = din("n1", [128, 8])
        nm = din("nm", [128, 8])
        wgu1 = din("w_gu1", [D, 2 * DFF])
        wdn1 = din("w_dn1", [DFF, D])
        win = din("w_in", [D, 3072])
        gq = din("gq", [128, 4])
        bd32 = din("bd32", [128, 128], BF16)
        bd64 = din("bd64", [128, 128], BF16)
        x1o = dout("x1", [NT, D])
        zo = dout("z", [384, NT])
        bgo = dout("bgo", [384, NT])
        qko = dout("qk", [1280, NT], BF16)
        vo = dout("v", [NT, 640], BF16)
    if mode == "C":
        yo = dout("y", [NT, D])

    with contextlib.ExitStack() as es:
        alloc_common(nc, es, c)

        def sb(name, shape, dt):
            return es.enter_context(nc.sbuf_tensor(name, shape, dt))

        gcols = sb("gcols", [128, 4, 8], F32)
        if doC:
            c.yT = sb("yT", [128, 8, 512], BF16)
            c.ytm = sb("ytm", [128, 4, 640], BF16)
            c.od = sb("odt", [128, 4, 512], F32)
            c.ol = sb("olt", [128, 4, 384], F32)
            c.ob = sb("ob", [128, 256], F32)
            c.wout = sb("woutsb", [128, 8, 1024], BF16)
            c.zh = sb("zhs", [128, 3, 514], F32)
            c.bg = sb("bgs", [128, 3, 512], F32)
            c.cw = sb("cw", [128, 3, 3], F32)
            c.ca = sb("ca", [128, 512], F32)
            c.nlam = sb("nlams", [128, 1], F32)
            c.lamv = sb("lamvs", [128, 128], F32)
            c.lt = sb("lts", [128, 4], F32)
            c.gsub = sb("gsubs", [128, 64], F32)
            c.nf = sb("nfs", [128, 1024], F32)
        if doA:
            c.wch = [sb("wch%d" % i, [128, 8, 128], BF16) for i in range(2)]
            c.wv = sb("wv", [128, 8, 640], BF16)
            if doC:
                c.usb = c.od[:, 0:3, :]
                c.zt = c.zh[:, :, 0:512]
                c.bgt = c.bg
                c.k = {"usb": "od", "zt": "zh", "bgt": "bg"}
            else:
                c.usb = sb("usb", [128, 3, 512], F32)
                c.zt = sb("zt", [128, 3, 512], F32)
                c.bgt = sb("bgt", [128, 3, 512], F32)
                c.k = {"usb": "usb", "zt": "zt", "bgt": "bgt"}
            c.sq = [sb("sq%d" % i, [128, 512], BF16) for i in range(2)]
            c.rq = [sb("rq%d" % i, [128, 512], F32) for i in range(2)]
            c.qk = [sb("qks%d" % i, [128, 512], BF16) for i in range(2)]
            c.vt = sb("vt", [128, 4, 640], BF16)
            c.gq = sb("gqs", [128, 4], F32)
            c.bd32 = sb("bd32s", [128, 128], BF16)
            c.bd64 = sb("bd64s", [128, 128], BF16)

        P.dma(c.ident[:, :], ident_d[:, :], reads=[], writes=["ident"], dkey="cst1")
        P.op("dve", lambda e: e.memset(c.epsc[:, :], EPS), reads=[], writes=["epsc"])
        if doA:
            gcol_load(P, gcols[:, 0, :], n1, 8, "gcols")
            gcol_load(P, gcols[:, 1, :], nm, 8, "gcols")
            P.dma(c.gq[:, :], gq[:, :], reads=[], writes=["gq"], dkey="cst2")
            P.op("dve", lambda e: e.tensor_scalar(out=c.gq[:, 0:1], in0=c.gq[:, 0:1], scalar1=32 ** -0.5, scalar2=None,
                                                  op0=ALU.mult), reads=["gq"], writes=["gq"])
            P.op("dve", lambda e: e.tensor_scalar(out=c.gq[:, 2:3], in0=c.gq[:, 2:3], scalar1=64 ** -0.5, scalar2=None,
                                                  op0=ALU.mult), reads=["gq"], writes=["gq"])
            P.dma(c.bd32[:, :], bd32[:, :], reads=[], writes=["bd"], dkey="cst3")
            P.dma(c.bd64[:, :], bd64[:, :], reads=[], writes=["bd"], dkey="cst4")
        if doC:
            gcol_load(P, gcols[:, 2, :], n2, 8, "gcols")
            P.dma(c.cw[:, :, :], convw.rearrange("p (c k) -> p c k", k=3), reads=[], writes=["cw"], dkey="cst5")
            lam_init = 0.8 - 0.6 * math.exp(-0.3 * layer)
            P.dma(c.lamv[:, :], lamv[0:1, :].partition_broadcast(128), reads=[], writes=["lamv"], dkey="cst6")
            for q in range(2):
                P.op("dve", lambda e, q=q: e.tensor_tensor(out=c.lamv[:, q * 64:q * 64 + 32],
                                                           in0=c.lamv[:, q * 64:q * 64 + 32],
                                                           in1=c.lamv[:, q * 64 + 32:q * 64 + 64], op=ALU.mult),
                     reads=["lamv"], writes=["lamv"])
                P.op("dve", lambda e, q=q: e.tensor_reduce(out=c.lt[:, q:q + 1], in_=c.lamv[:, q * 64:q * 64 + 32],
                                                           axis=mybir.AxisListType.X, op=ALU.add),
                     reads=["lamv"], writes=["lt"])
            P.op("act", lambda e: e.activation(out=c.lt[:, 2:4], in_=c.lt[:, 0:2], func=AF.Exp), reads=["lt"],
                 writes=["lt"])
            P.op("dve", lambda e: e.tensor_tensor(out=c.nlam[:, :], in0=c.lt[:, 3:4], in1=c.lt[:, 2:3],
                                                  op=ALU.subtract), reads=["lt"], writes=["nlam"])
            P.op("dve", lambda e: e.tensor_scalar(out=c.nlam[:, :], in0=c.nlam[:, :], scalar1=-lam_init, scalar2=None,
                                                  op0=ALU.add), reads=["nlam"], writes=["nlam"])
            P.dma(c.gsub[:, :], gsub[0:1, :].partition_broadcast(128), reads=[], writes=["gsub"], dkey="cst7")
            P.op("dve", lambda e: e.tensor_scalar(out=c.gsub[:, :], in0=c.gsub[:, :], scalar1=1.0 - lam_init,
                                                  scalar2=None, op0=ALU.mult), reads=["gsub"], writes=["gsub"])
            P.dma(c.nf[:, :], nf[0:1, :].partition_broadcast(128), reads=[], writes=["nf"], dkey="cst8")

        cnt = [0]
        wcnt = [0]

        if doC:
            wo_v = wout.rearrange("(kc p) n -> p kc n", p=128)
            for q in range(2):
                P.dma(c.wout[:, q * 4:(q + 1) * 4, :], wo_v[:, q * 4:(q + 1) * 4, :], reads=[], writes=["wout"],
                      dkey="wout", eng="pool")
            load_wdn(P, c, wdn2)
            for ti in range(12):
                g0 = ti * 512
                P.dma(c.xt[:, :, :], xin[g0:g0 + 512, :].rearrange("(j p) n -> p j n", p=128), reads=[], writes=["xt"],
                      dkey="xt")
                P.dma(c.od[:, :, :], od[g0:g0 + 512, :].rearrange("(j p) n -> p j n", p=128), reads=[], writes=["od"],
                      dkey="od")
                P.dma(c.ol[:, :, :], ol[g0:g0 + 512, :].rearrange("(j p) n -> p j n", p=128), reads=[], writes=["ol"],
                      dkey="ol")
                P.dma(c.zh[:, :, :], zh[:, ti, :].rearrange("(c p) n -> p c n", p=128), reads=[], writes=["zh"],
                      dkey="zh")
                P.dma(c.bg[:, :, :], bgd[:, g0:g0 + 512].rearrange("(c p) n -> p c n", p=128), reads=[],
                      writes=["bg"], dkey="bg")
                for ch in range(3):
                    P.op("dve", lambda e, ch=ch: e.tensor_scalar(out=c.ca[:, :], in0=c.zh[:, ch, 0:512],
                                                                 scalar1=c.cw[:, ch, 0:1], scalar2=None, op0=ALU.mult),
                         reads=["zh", "cw"], writes=["ca"])
                    P.op("dve", lambda e, ch=ch: e.scalar_tensor_tensor(out=c.ca[:, :], in0=c.zh[:, ch, 1:513],
                                                                        scalar=c.cw[:, ch, 1:2], in1=c.ca[:, :],
                                                                        op0=ALU.mult, op1=ALU.add),
                         reads=["zh", "cw", "ca"], writes=["ca"])
                    P.op("dve", lambda e, ch=ch: e.scalar_tensor_tensor(out=c.ca[:, :], in0=c.zh[:, ch, 2:514],
                                                                        scalar=c.cw[:, ch, 2:3], in1=c.ca[:, :],
                                                                        op0=ALU.mult, op1=ALU.add),
                         reads=["zh", "cw", "ca"], writes=["ca"])
                    P.op("dve", lambda e, ch=ch: e.tensor_tensor(out=c.yT[:, ch, :], in0=c.ca[:, :], in1=c.bg[:, ch, :],
                                                                 op=ALU.mult),
                         reads=["ca", "bg"], writes=["yT"])
                for j in range(4):
                    for hh in range(4):
                        o1 = c.od[:, j, (2 * hh) * 64:(2 * hh + 1) * 64]
                        o2 = c.od[:, j, (2 * hh + 1) * 64:(2 * hh + 2) * 64]
                        obv = c.ob[:, hh * 64:(hh + 1) * 64]
                        P.op("dve", lambda e, o1=o1, o2=o2, obv=obv: e.scalar_tensor_tensor(
                            out=obv, in0=o2, scalar=c.nlam[:, 0:1], in1=o1, op0=ALU.mult, op1=ALU.add),
                             reads=["od", "nlam"], writes=["ob"])
                        P.op("act", lambda e, obv=obv, hh=hh: e.activation(out=c.junk[:, 0:64], in_=obv, func=AF.Square,
                                                                           accum_out=c.ss[:, 4 + hh:5 + hh]),
                             reads=["ob"], writes=["junk", "ss"])
                    P.op("act", lambda e: e.activation(out=c.rs[:, 4:8], in_=c.ss[:, 4:8], func=AF.Sqrt,
                                                       bias=c.epsc[:, 0:1], scale=1.0 / 64),
                         reads=["ss"], writes=["rs"])
                    P.op("dve", lambda e: e.reciprocal(out=c.rs[:, 4:8], in_=c.rs[:, 4:8]), reads=["rs"],
                         writes=["rs"])
                    for hh in range(4):
                        obv = c.ob[:, hh * 64:(hh + 1) * 64]
                        P.op("dve", lambda e, obv=obv, hh=hh, j=j: e.scalar_tensor_tensor(
                            out=c.ytm[:, j, hh * 64:(hh + 1) * 64], in0=obv, scalar=c.rs[:, 4 + hh:5 + hh],
                            in1=c.gsub[:, :], op0=ALU.mult, op1=ALU.mult), reads=["ob", "rs", "gsub"], writes=["ytm"])
                    P.op("act", lambda e, j=j: e.activation(out=c.ytm[:, j, 256:640], in_=c.ol[:, j, :], func=AF.Copy),
                         reads=["ol"], writes=["ytm"])
                for kc in range(5):
                    pst = c.pst[kc % 2]
                    pk = "pst%d" % (kc % 2)
                    for j in range(4):
                        P.op("pe", lambda e, j=j, kc=kc, pst=pst: e.transpose(
                            out=pst[:, j * 128:(j + 1) * 128], in_=c.ytm[:, j, kc * 128:(kc + 1) * 128],
                            identity=c.ident[:, :]), reads=["ytm"], writes=[pk])
                    P.op("act" if kc % 2 == 0 else "dve",
                         (lambda e, kc=kc, pst=pst: e.activation(out=c.yT[:, 3 + kc, :], in_=pst[:, 0:512],
                                                                 func=AF.Copy)) if kc % 2 == 0 else
                         (lambda e, kc=kc, pst=pst: e.tensor_copy(out=c.yT[:, 3 + kc, :], in_=pst[:, 0:512])),
                         reads=[pk], writes=["yT"])
                n = 0
                for j in range(4):
                    for hf in range(2):
                        pd = c.ps[4 + n % 2]
                        pk = "ps%d" % (4 + n % 2)
                        n += 1
                        for kc in range(8):
                            P.op("pe", lambda e, j=j, hf=hf, kc=kc, pd=pd: e.matmul(
                                pd[:, :], lhsT=c.yT[:, kc, j * 128:(j + 1) * 128],
                                rhs=c.wout[:, kc, hf * 512:(hf + 1) * 512], start=(kc == 0), stop=(kc == 7)),
                                 reads=["yT", "wout"], writes=[pk])
                        P.op("dve", lambda e, j=j, hf=hf, pd=pd: e.tensor_tensor(
                            out=c.xt[:, j, hf * 512:(hf + 1) * 512], in0=pd[:, :],
                            in1=c.xt[:, j, hf * 512:(hf + 1) * 512], op=ALU.add), reads=[pk, "xt"], writes=["xt"])
                norm_T(P, c, c.xt, gcols[:, 2, :], "n2")
                ffn(P, c, c.xt, wgu2, c.wdn, cnt)
                for j in range(4):
                    P.op("act", lambda e, j=j: e.activation(out=c.junk[:, :], in_=c.xt[:, j, :], func=AF.Square,
                                                            accum_out=c.ss[:, j:j + 1]),
                         reads=["xt"], writes=["junk", "ss"])
                P.op("act", lambda e: e.activation(out=c.rs[:, 0:4], in_=c.ss[:, 0:4], func=AF.Sqrt,
                                                   bias=c.epsc[:, 0:1], scale=1.0 / D), reads=["ss"], writes=["rs"])
                P.op("dve", lambda e: e.reciprocal(out=c.rs[:, 0:4], in_=c.rs[:, 0:4]), reads=["rs"], writes=["rs"])
                for j in range(4):
                    P.op("dve", lambda e, j=j: e.scalar_tensor_tensor(out=c.xt[:, j, :], in0=c.xt[:, j, :],
                                                                      scalar=c.rs[:, j:j + 1], in1=c.nf[:, :],
                                                                      op0=ALU.mult, op1=ALU.mult),
                         reads=["xt", "rs", "nf"], writes=["xt"])
                if mode == "C":
                    P.dma(yo[g0:g0 + 512, :].rearrange("(j p) n -> p j n", p=128), c.xt[:, :, :], reads=["xt"],
                          writes=["yo"], dkey="xo")
                if doA:
                    phaseA_tile(P, c, ti, None, gcols, wgu1, win, x1o, zo, bgo, qko, vo, cnt, wcnt, first=(ti == 0),
                                wdn1=wdn1, reload_wdn2=(wdn2 if ti < 11 else None))
        elif doA:
            load_wdn(P, c, wdn1)
            for ti in range(12):
                g0 = ti * 512
                P.dma(c.xt[:, :, :], xin[g0:g0 + 512, :].rearrange("(j p) n -> p j n", p=128), reads=[], writes=["xt"],
                      dkey="xt")
                phaseA_tile(P, c, ti, None, gcols, wgu1, win, x1o, zo, bgo, qko, vo, cnt, wcnt, first=(ti == 0))
        P.final_wait()
        P.emit()
    return nc


def phaseA_tile(P, c, ti, _unused, gcols, wgu1, win, x1o, zo, bgo, qko, vo, cnt, wcnt, first, wdn1=None,
                reload_wdn2=None):
    g0 = ti * 512
    if wdn1 is not None:
        load_wdn(P, c, wdn1)
    if first:
        wv_v = win.rearrange("(kc p) n -> p kc n", p=128)
        P.dma(c.wv[:, :, 0:256], wv_v[:, :, 1664:1920], reads=[], writes=["wv"], dkey="wv", eng="pool")
        P.dma(c.wv[:, :, 256:640], wv_v[:, :, 2688:3072], reads=[], writes=["wv"], dkey="wv", eng="pool")
    norm_T(P, c, c.xt, gcols[:, 0, :], "n1")
    ffn(P, c, c.xt, wgu1, c.wdn, cnt)
    if reload_wdn2 is not None:
        load_wdn(P, c, reload_wdn2)
    P.dma(x1o[g0:g0 + 512, :].rearrange("(j p) n -> p j n", p=128), c.xt[:, :, :], reads=["xt"], writes=["x1o"],
          dkey="xo")
    norm_T(P, c, c.xt, gcols[:, 1, :], "nm")
    win_v = win.rearrange("(kc p) n -> p kc n", p=128)
    chunks = [(0, "u", 0), (128, "u", 1), (256, "u", 2), (768, "cg", 0), (896, "cg", 1), (1024, "cg", 2),
              (384, "bg", 0), (512, "bg", 1), (640, "bg", 2),
              (1152, "dq", 0), (1280, "dq", 1), (1408, "dk", 2), (1536, "dk", 3),
              (1920, "lq", 4), (2048, "lq", 5), (2176, "lq", 6), (2304, "lk", 7), (2432, "lk", 8), (2560, "lk", 9)]
    n = 0
    for (col, kind, idx) in chunks:
        slot = wcnt[0] % 2
        wcnt[0] += 1
        wk = "wch%d" % slot
        wt = c.wch[slot]
        P.dma(wt[:, :, :], win_v[:, :, col:col + 128], reads=[], writes=[wk], dkey=wk, eng="pool")
        pp = c.ps[n % 2]
        pk = "ps%d" % (n % 2)
        n += 1
        for kc in range(8):
            P.op("pe", lambda e, kc=kc, wt=wt, pp=pp: e.matmul(pp[:, :], lhsT=wt[:, kc, :], rhs=c.hT[:, kc, :],
                                                               start=(kc == 0), stop=(kc == 7)),
                 reads=[wk, "hT"], writes=[pk])
        if kind == "u":
            P.op("act", lambda e, idx=idx, pp=pp: e.activation(out=c.usb[:, idx, :], in_=pp[:, :], func=AF.Copy),
                 reads=[pk], writes=[c.k["usb"]])
        elif kind == "cg":
            P.op("dve", lambda e, idx=idx, pp=pp: e.tensor_tensor(out=c.zt[:, idx, :], in0=pp[:, :],
                                                                  in1=c.usb[:, idx, :], op=ALU.mult),
                 reads=[pk, c.k["usb"]], writes=[c.k["zt"]])
        elif kind == "bg":
            P.op("act", lambda e, idx=idx, pp=pp: e.activation(out=c.bgt[:, idx, :], in_=pp[:, :], func=AF.Copy),
                 reads=[pk], writes=[c.k["bgt"]])
        else:
            is32 = kind in ("dq", "dk")
            gi = {"dq": 0, "dk": 1, "lq": 2, "lk": 3}[kind]
            b = n % 2
            sq, rq, qk = c.sq[b], c.rq[b], c.qk[b]
            pq = c.ps[2 + b]
            pqk = "ps%d" % (2 + b)
            P.op("act", lambda e, sq=sq, pp=pp: e.activation(out=sq[:, :], in_=pp[:, :], func=AF.Square),
                 reads=[pk], writes=["sq%d" % b])
            bd = c.bd32 if is32 else c.bd64
            P.op("pe", lambda e, sq=sq, pq=pq, bd=bd: e.matmul(pq[:, :], lhsT=bd[:, :], rhs=sq[:, :], start=True,
                                                               stop=True), reads=["sq%d" % b, "bd"], writes=[pqk])
            P.op("act", lambda e, rq=rq, pq=pq, is32=is32: e.activation(out=rq[:, :], in_=pq[:, :], func=AF.Sqrt,
                                                                        bias=c.epsc[:, 0:1],
                                                                        scale=1.0 / (32 if is32 else 64)),
                 reads=[pqk], writes=["rq%d" % b])
            P.op("dve", lambda e, rq=rq: e.reciprocal(out=rq[:, :], in_=rq[:, :]), reads=["rq%d" % b],
                 writes=["rq%d" % b])
            P.op("dve", lambda e, rq=rq, qk=qk, pp=pp, gi=gi: e.scalar_tensor_tensor(
                out=qk[:, :], in0=pp[:, :], scalar=c.gq[:, gi:gi + 1], in1=rq[:, :], op0=ALU.mult, op1=ALU.mult),
                 reads=[pk, "gq", "rq%d" % b], writes=["qk%d" % b])
            r0 = (idx) * 128
            P.dma(qko[r0:r0 + 128, g0:g0 + 512], qk[:, :], reads=["qk%d" % b], writes=["qko"], dkey="qko%d" % b)
    P.dma(zo[:, g0:g0 + 512].rearrange("(c p) n -> p c n", p=128), c.zt[:, :, :], reads=[c.k["zt"]], writes=["zo"],
          dkey="zo")
    P.dma(bgo[:, g0:g0 + 512].rearrange("(c p) n -> p c n", p=128), c.bgt[:, :, :], reads=[c.k["bgt"]], writes=["bgo"],
          dkey="bgo")
    for j in range(4):
        pa, pb = c.ps[4], c.ps[5]
        for kc in range(8):
            P.op("pe", lambda e, j=j, kc=kc: e.matmul(pa[:, 0:256], lhsT=c.hT[:, kc, j * 128:(j + 1) * 128],
                                                      rhs=c.wv[:, kc, 0:256], start=(kc == 0), stop=(kc == 7)),
                 reads=["hT", "wv"], writes=["ps4"])
        for kc in range(8):
            P.op("pe", lambda e, j=j, kc=kc: e.matmul(pb[:, 0:384], lhsT=c.hT[:, kc, j * 128:(j + 1) * 128],
                                                      rhs=c.wv[:, kc, 256:640], start=(kc == 0), stop=(kc == 7)),
                 reads=["hT", "wv"], writes=["ps5"])
        P.op("act", lambda e, j=j: e.activation(out=c.vt[:, j, 0:256], in_=pa[:, 0:256], func=AF.Copy),
             reads=["ps4"], writes=["vt"])
        P.op("dve", lambda e, j=j: e.tensor_copy(out=c.vt[:, j, 256:640], in_=pb[:, 0:384]), reads=["ps5"],
             writes=["vt"])
    P.dma(vo[g0:g0 + 512, :].rearrange("(j p) n -> p j n", p=128), c.vt[:, :, :], reads=["vt"], writes=["vo"],
          dkey="vo")


DIL = ((1, -128, 512, 1152), (4, -256, 640, 1408), (16, -1024, 1408, 2944))


def build_attn():
    import contextlib
    nc = bass.Bass("TRN2", target_bir_lowering=False)
    P = Prog(nc)

    def din(name, shape, dt=F32):
        return nc.dram_tensor(name, shape, dt, kind="ExternalInput").ap()

    def dout(name, shape, dt=F32):
        return nc.dram_tensor(name, shape, dt, kind="ExternalOutput").ap()

    dq = din("dq", [3, 32, S], BF16)
    dk = din("dk", [3, 32, S], BF16)
    dv = din("dv", [3, 128, 128, 64], BF16)
    lq = din("lq", [3, 64, S], BF16)
    lk = din("lk", [3, 64, S], BF16)
    lv = din("lv", [3, 128, 128, 64], BF16)
    tabc = din("tabc", [32, 6])
    ohrev = din("ohrev", [32, 3328])
    Jd = din("J", [128, 128], BF16)
    e15 = din("e15", [32, 128])
    e31 = din("e31", [32, 128])
    idf = din("identf", [128, 128])
    mds = [din("m%d" % r, [128, w], BF16) for (r, _, _, w) in DIL]
    odu = dout("odu", [3, S, 64])
    olu = dout("olu", [3, S, 64])
    T = nc.dram_tensor("Ttab", [6, 3328], F32).ap()

    with contextlib.ExitStack() as es:
        def sb(name, shape, dt):
            return es.enter_context(nc.sbuf_tensor(name, shape, dt))

        qT = sb("qT", [64, S], BF16)
        kT = sb("kT", [64, S], BF16)
        V = sb("V", [128, 128, 65], BF16)
        H = sb("H", [128, HW_], BF16)
        J = sb("Jsb", [128, 128], BF16)
        identf = sb("identfs", [128, 128], F32)
        ms = [sb("ms%d" % r, [128, w], BF16) for (r, _, _, w) in DIL]
        pT = [sb("pT%d" % i, [128, 512], BF16) for i in range(3)]
        osb = sb("osb", [65, 512], F32)
        rc = sb("rc", [128, 4], F32)
        on = sb("on", [128, 4, 64], F32)
        tab = sb("tab", [32, 6], F32)
        oh = sb("oh", [32, 3328], F32)
        e15s = sb("e15s", [32, 128], F32)
        e31s = sb("e31s", [32, 128], F32)
        tsb = sb("tsb", [6, 3328], F32)
        cb = sb("cb", [128, 12], F32)
        pss = [es.enter_context(nc.psum_tensor("pss%d" % i, [128, 512], F32)) for i in range(2)]
        pso = es.enter_context(nc.psum_tensor("pso", [128, 512], F32))
        ptr = es.enter_context(nc.psum_tensor("ptr", [128, 512], F32))
        ptb = es.enter_context(nc.psum_tensor("ptb", [128, 512], F32))

        P.dma(J[:, :], Jd[:, :], [], ["J"], "cst9")
        P.dma(identf[:, :], idf[:, :], [], ["identf"], "cst10")
        P.dma(tab[:, :], tabc[:, :], [], ["tab"], "cst11")
        P.dma(oh[:, :], ohrev[:, :], [], ["oh"], "cst12")
        P.dma(e15s[:, :], e15[:, :], [], ["e15"], "cst13")
        P.dma(e31s[:, :], e31[:, :], [], ["e31"], "cst14")
        for i in range(3):
            P.dma(ms[i][:, :], mds[i][:, :], [], ["ms"], "cst15")
        P.op("dve", lambda e: e.memset(V[:, :, 64:65], 1.0), [], ["Vones"])
        for i in range(7):
            n = 512 if i < 6 else 256
            P.op("pe", lambda e, i=i, n=n: e.matmul(ptb[0:6, 0:n], lhsT=tab[:, 0:6], rhs=oh[:, i * 512:i * 512 + n],
                                                   start=True, stop=True), ["tab", "oh"], ["ptb"])
            P.op("act", lambda e, i=i, n=n: e.activation(out=tsb[0:6, i * 512:i * 512 + n], in_=ptb[0:6, 0:n],
                                                         func=AF.Copy), ["ptb"], ["tsb"])
        P.dma(T[:, :], tsb[:, :], ["tsb"], ["T"], "T")
        P.op("pe", lambda e: e.matmul(ptb[:, 0:6], lhsT=e15s[:, :], rhs=tab[:, 0:6], start=True, stop=True),
             ["e15", "tab"], ["ptb"])
        P.op("act", lambda e: e.activation(out=cb[:, 0:6], in_=ptb[:, 0:6], func=AF.Copy), ["ptb"], ["cb"])
        P.op("pe", lambda e: e.matmul(ptb[:, 0:6], lhsT=e31s[:, :], rhs=tab[:, 0:6], start=True, stop=True),
             ["e31", "tab"], ["ptb"])
        P.op("act", lambda e: e.activation(out=cb[:, 6:12], in_=ptb[:, 0:6], func=AF.Copy), ["ptb"], ["cb"])

        cnt = [0]

        def unit(slot, is_dil):
            K = 64 if is_dil else 32
            qsrc, ksrc, vsrc, osrc = (lq, lk, lv, olu) if is_dil else (dq, dk, dv, odu)
            k = slot - 3 if is_dil else slot
            for hh in range(4):
                P.dma(qT[0:K, hh * 4096:(hh + 1) * 4096], qsrc[k, :, hh * 4096:(hh + 1) * 4096], [], ["qT"], "qT")
                P.dma(kT[0:K, hh * 4096:(hh + 1) * 4096], ksrc[k, :, hh * 4096:(hh + 1) * 4096], [], ["kT"], "kT")
                P.dma(V[:, hh * 32:(hh + 1) * 32, 0:64], vsrc[k, :, hh * 32:(hh + 1) * 32, :], [], ["V"], "V")
            hsrc = bass.AP(tensor=T.tensor, offset=slot * 3328, ap=[[1, 128], [1, HW_]])
            P.dma(H[:, :], hsrc, ["T"], ["H"], "H", eng="pool")
            for qt in range(32):
                qb = qt * 512
                tiles = []
                if is_dil:
                    for bi, (r, dmin, dmax, w) in enumerate(DIL):
                        for kt in range(max(0, (qb + dmin) // 128), min(127, (qb + dmax) // 128) + 1):
                            tiles.append((kt, bi))
                else:
                    tiles = [(kt, None) for kt in range(128)]
                for ti, (kt, bi) in enumerate(tiles):
                    d = kt * 128 - qb
                    near = is_dil or (-640 <= d <= 1024)
                    i = cnt[0]
                    cnt[0] += 1
                    ps = pss[i % 2]
                    pk = "pss%d" % (i % 2)
                    pt = pT[i % 3]
                    ptk = "pT%d" % (i % 3)
                    P.op("pe", lambda e, kt=kt, qb=qb, ps=ps, near=near: e.matmul(
                        ps[:, :], lhsT=kT[0:K, kt * 128:(kt + 1) * 128], rhs=qT[0:K, qb:qb + 512], start=True,
                        stop=(not near)), ["kT", "qT"], [pk])
                    if near:
                        off = 1536 - d
                        P.op("pe", lambda e, ps=ps, off=off: e.matmul(ps[:, :], lhsT=J[:, :], rhs=H[:, off:off + 512],
                                                                      start=False, stop=True), ["J", "H"], [pk])
                        P.op("act", lambda e, ps=ps, pt=pt: e.activation(out=pt[:, :], in_=ps[:, :], func=AF.Exp),
                             [pk], [ptk])
                    else:
                        col = slot if d < 0 else 6 + slot
                        P.op("act", lambda e, ps=ps, pt=pt, col=col: e.activation(out=pt[:, :], in_=ps[:, :],
                                                                                  func=AF.Exp,
                                                                                  bias=cb[:, col:col + 1], scale=1.0),
                             [pk, "cb"], [ptk])
                    if is_dil:
                        r, dmin, dmax, w = DIL[bi]
                        o = dmax - d
                        P.op("dve", lambda e, pt=pt, bi=bi, o=o: e.tensor_tensor(out=pt[:, :], in0=pt[:, :],
                                                                                 in1=ms[bi][:, o:o + 512],
                                                                                 op=ALU.mult), [ptk, "ms"], [ptk])
                    P.op("pe", lambda e, kt=kt, pt=pt, ti=ti, nt=len(tiles): e.matmul(
                        pso[0:65, :], lhsT=V[:, kt, 0:65], rhs=pt[:, :], start=(ti == 0), stop=(ti == nt - 1)),
                         ["V", "Vones", ptk], ["pso"])
                P.op("act", lambda e: e.activation(out=osb[:, :], in_=pso[0:65, :], func=AF.Copy), ["pso"], ["osb"])
                for j in range(4):
                    P.op("pe", lambda e, j=j: e.transpose(out=ptr[:, j * 65:(j + 1) * 65],
                                                          in_=osb[:, j * 128:(j + 1) * 128],
                                                          identity=identf[0:65, 0:65]), ["osb", "identf"], ["ptr"])
                ptv = ptr[:, 0:260].rearrange("p (j d) -> p j d", d=65)
                P.op("dve", lambda e, ptv=ptv: e.reciprocal(out=rc[:, :], in_=ptv[:, :, 64]), ["ptr"], ["rc"])
                for j in range(4):
                    P.op("dve", lambda e, j=j, ptv=ptv: e.tensor_scalar(out=on[:, j, :], in0=ptv[:, j, 0:64],
                                                                        scalar1=rc[:, j:j + 1], scalar2=None,
                                                                        op0=ALU.mult), ["ptr", "rc"], ["on"])
                P.dma(osrc[k, qb:qb + 512, :].rearrange("(j p) d -> p j d", p=128), on[:, :, :], ["on"], ["osrc"],
                      "on")

        for slot in range(3):
            unit(slot, False)
        for slot in range(3, 6):
            unit(slot, True)
        P.final_wait()
        P.emit()
    return nc


def _rel_bucket(rel):
    nb, me = 16, 8
    rel = np.asarray(rel, np.int64)
    ret = np.where(rel > 0, nb, 0)
    n = np.abs(rel)
    nf = np.maximum(n, 1).astype(np.float32)
    large = me + (np.log(nf / np.float32(me)) / np.float32(math.log(1024 / me)) * np.float32(nb - me)).astype(np.int32)
    large = np.minimum(large, nb - 1)
    return ret + np.where(n < me, n, large)


def _consts():
    cst = {}
    cst["ident"] = np.eye(128, dtype=np.float32).astype(NPBF)
    cst["identf"] = np.eye(128, dtype=np.float32)
    cst["J"] = np.eye(128, dtype=np.float32)[::-1].copy().astype(NPBF)
    bd32 = np.kron(np.eye(4, dtype=np.float32), np.ones((32, 32), np.float32))
    bd64 = np.kron(np.eye(2, dtype=np.float32), np.ones((64, 64), np.float32))
    cst["bd32"] = bd32.astype(NPBF)
    cst["bd64"] = bd64.astype(NPBF)
    e15 = np.zeros((32, 128), np.float32); e15[15] = 1
    e31 = np.zeros((32, 128), np.float32); e31[31] = 1
    cst["e15"], cst["e31"] = e15, e31
    m = np.arange(3328)
    bk = _rel_bucket(CC - m)
    oh = (bk[None, :] == np.arange(32)[:, None]).astype(np.float32)
    oh[:, 3327:] = 0
    cst["ohrev"] = oh
    for (r, dmin, dmax, w) in DIL:
        rel = np.arange(128)[:, None] - np.arange(w)[None, :] + dmax
        cst["m%d" % r] = ((rel % r == 0) & (np.abs(rel) <= 64 * r)).astype(np.float32).astype(NPBF)
    return cst


def _col(v):
    return np.ascontiguousarray(np.asarray(v, np.float32).reshape(-1, 128).T)


_PROGS = {}
_DBG = None


def _prog(key, fn):
    if key not in _PROGS:
        _PROGS[key] = fn()
    return _PROGS[key]


def kernel(x_prompt, x_sample, ffn1_norm, ffn1_w_gu, ffn1_w_down, mix_norm, w_in, conv_w, diff_q_norm, diff_k_norm,
           lambda_q1, lambda_k1, lambda_q2, lambda_k2, diff_sub_norm, dil_q_norm, dil_k_norm, w_out, ffn2_norm,
           ffn2_w_gu, ffn2_w_down, final_norm, rel_bias):
    f = lambda a: np.ascontiguousarray(np.asarray(a, np.float32))
    cst = _consts()
    xs = [f(x_prompt)[0], f(x_sample)[0], f(x_sample)[1]]
    cores = list(range(NCORE))
    xc = [np.concatenate([xs[s][c * SL:(c + 1) * SL] for s in range(3)], 0) for c in cores]
    rel_bias = f(rel_bias)

    def a_inputs(l):
        gq = np.stack([np.tile(f(diff_q_norm)[l], 4), np.tile(f(diff_k_norm)[l], 4), np.tile(f(dil_q_norm)[l], 2),
                       np.tile(f(dil_k_norm)[l], 2)], 1)
        return {"n1": _col(ffn1_norm[l]), "nm": _col(mix_norm[l]), "w_gu1": f(ffn1_w_gu[l]), "w_dn1": f(ffn1_w_down[l]),
                "w_in": f(w_in[l]), "gq": np.ascontiguousarray(gq.astype(np.float32)), "bd32": cst["bd32"],
                "bd64": cst["bd64"]}

    def c_inputs(l):
        cw = f(conv_w[l])
        cwl = np.ascontiguousarray(cw.reshape(3, 3, 128).transpose(2, 1, 0).reshape(128, 9))
        lamv = np.concatenate([f(lambda_q1)[l], f(lambda_k1)[l], f(lambda_q2)[l], f(lambda_k2)[l]])[None, :]
        return {"w_out": f(w_out[l]), "conv_w": cwl, "lamv": np.ascontiguousarray(lamv),
                "gsub": f(diff_sub_norm[l])[None, :], "n2": _col(ffn2_norm[l]), "nf": f(final_norm[l])[None, :],
                "w_gu2": f(ffn2_w_gu[l]), "w_dn2": f(ffn2_w_down[l])}

    def shuffle_fwd(resA):
        qk = [np.asarray(r["qk"]) for r in resA]
        v = [np.asarray(r["v"]) for r in resA]
        z = [np.asarray(r["z"]) for r in resA]
        Q = [np.concatenate([qk[c][:, s * SL:(s + 1) * SL] for c in cores], 1) for s in range(3)]
        Vf = [np.concatenate([v[c][s * SL:(s + 1) * SL] for c in cores], 0) for s in range(3)]
        Z = [np.concatenate([z[c][:, s * SL:(s + 1) * SL] for c in cores], 1) for s in range(3)]
        maps = []

        def pm(a):
            return np.ascontiguousarray(a.reshape(128, 128, 64).transpose(1, 0, 2))

        for j in cores:
            dq = np.zeros((3, 32, S), NPBF); dk = np.zeros((3, 32, S), NPBF); dv = np.zeros((3, 128, 128, 64), NPBF)
            lq = np.zeros((3, 64, S), NPBF); lk = np.zeros((3, 64, S), NPBF); lv = np.zeros((3, 128, 128, 64), NPBF)
            cols = []
            for k in range(3):
                u = 3 * j + k
                s, h, cc = u // 8, (u % 8) // 2, u % 2
                r0 = h * 64 + cc * 32
                dq[k] = Q[s][r0:r0 + 32]
                dk[k] = Q[s][256 + r0:256 + r0 + 32]
                dv[k] = pm(Vf[s][:, h * 64:(h + 1) * 64])
                cols.append(h)
            for k in range(3):
                vv = 3 * j + k
                if vv < 18:
                    s, h6 = vv // 6, vv % 6
                    lq[k] = Q[s][512 + h6 * 64:512 + (h6 + 1) * 64]
                    lk[k] = Q[s][896 + h6 * 64:896 + (h6 + 1) * 64]
                    lv[k] = pm(Vf[s][:, 256 + h6 * 64:256 + (h6 + 1) * 64])
                    cols.append(4 + h6)
                else:
                    cols.append(4)
            m = {"dq": dq, "dk": dk, "dv": dv, "lq": lq, "lk": lk, "lv": lv,
                 "tabc": np.ascontiguousarray(rel_bias[:, cols]), "ohrev": cst["ohrev"], "J": cst["J"],
                 "e15": cst["e15"], "e31": cst["e31"], "identf": cst["identf"]}
            for (r, _, _, _) in DIL:
                m["m%d" % r] = cst["m%d" % r]
            maps.append(m)
        return maps, Z

    def shuffle_bwd(resB, Z):
        odu = [np.asarray(r["odu"]) for r in resB]
        olu = [np.asarray(r["olu"]) for r in resB]
        outs = []
        for c in cores:
            od = np.zeros((NT, 8, 64), np.float32)
            ol = np.zeros((NT, 6, 64), np.float32)
            zh = np.zeros((384, 12, 514), np.float32)
            for s in range(3):
                for hc in range(8):
                    u = s * 8 + hc
                    od[s * SL:(s + 1) * SL, hc] = odu[u // 3][u % 3][c * SL:(c + 1) * SL]
                for h6 in range(6):
                    vv = s * 6 + h6
                    ol[s * SL:(s + 1) * SL, h6] = olu[vv // 3][vv % 3][c * SL:(c + 1) * SL]
                for q in range(4):
                    t0 = c * SL + q * 512
                    lo, hi = max(0, t0 - 1), min(S, t0 + 513)
                    zh[:, s * 4 + q, lo - (t0 - 1):hi - (t0 - 1)] = Z[s][:, lo:hi]
            outs.append({"od": od.reshape(NT, 512), "ol": ol.reshape(NT, 384), "zh": zh})
        return outs

    ncA = _prog("A", lambda: build_phase("A"))
    ai = a_inputs(0)
    resA = run_bass_kernel_spmd(ncA, [dict(ai, ident=cst["ident"], x=xc[c]) for c in cores], core_ids=cores).results
    ncB = _prog("B", build_attn)
    y = None
    if _DBG:
        _DBG("A0", resA, None)
    for l in range(2):
        mapsB, Z = shuffle_fwd(resA)
        resB = run_bass_kernel_spmd(ncB, mapsB, core_ids=cores).results
        cin = shuffle_bwd(resB, Z)
        if _DBG:
            _DBG("B%d" % l, resB, (mapsB, cin))
        ci = c_inputs(l)
        if l == 0:
            ncC = _prog("CA0", lambda: build_phase("CA", layer=0))
            ai = a_inputs(1)
            maps = [dict(ci, **ai, **cin[c], ident=cst["ident"], x=np.asarray(resA[c]["x1"]),
                         bg=np.asarray(resA[c]["bgo"])) for c in cores]
            resA = run_bass_kernel_spmd(ncC, maps, core_ids=cores).results
            if _DBG:
                _DBG("CA0", resA, maps)
        else:
            ncC = _prog("C1", lambda: build_phase("C", layer=1))
            maps = [dict(ci, **cin[c], ident=cst["ident"], x=np.asarray(resA[c]["x1"]),
                         bg=np.asarray(resA[c]["bgo"])) for c in cores]
            resC = run_bass_kernel_spmd(ncC, maps, core_ids=cores).results
            y = [np.asarray(r["y"]) for r in resC]
    ys = [np.concatenate([y[c][s * SL:(s + 1) * SL] for c in cores], 0) for s in range(3)]
    y_prompt = ys[0][None].astype(np.float32)
    y_sample = np.stack([ys[1], ys[2]], 0).astype(np.float32)
    return (y_prompt, y_sample)
```

```python
import math
import numpy as np
import ml_dtypes
import concourse.bass as bass
import concourse.mybir as mybir
from concourse.bass_utils import run_bass_kernel_spmd

F32 = mybir.dt.float32
I32 = mybir.dt.int32
BF16 = mybir.dt.bfloat16
AF = mybir.ActivationFunctionType
ALU = mybir.AluOpType
NPBF = ml_dtypes.bfloat16

NCORE = 8
S = 16384
SL = S // NCORE
NT = 3 * SL
D = 1024
DFF = 2816
EPS = 1e-6
TW = 3327
HW_ = 3200
CC = 1663


class Op:
    __slots__ = ("fn", "waits", "signal", "dkey", "dcnt", "dinc")

    def __init__(self, fn):
        self.fn = fn
        self.waits = []
        self.signal = False
        self.dkey = None
        self.dcnt = 0
        self.dinc = 16


class Prog:
    ENG = ("pe", "act", "dve", "pool", "sp")

    def __init__(self, nc):
        self.nc = nc
        self.ops = {e: [] for e in self.ENG}
        self.res = {}
        self.dcount = {}
        self.waited = {e: {} for e in self.ENG}

    def _st(self, k):
        st = self.res.get(k)
        if st is None:
            st = {"w": {}, "r": {}, "wd": {}, "rd": {}}
            self.res[k] = st
        return st

    def op(self, eng, fn, reads=(), writes=(), dkey=None, dinc=16):
        lst = self.ops[eng]
        idx = len(lst)
        o = Op(fn)
        edeps = {}
        ddeps = {}

        def add(dst, src):
            for k, v in src.items():
                if dst.get(k, -1) < v:
                    dst[k] = v

        for k in reads:
            st = self._st(k)
            add(edeps, st["w"])
            add(ddeps, st["wd"])
        for k in writes:
            st = self._st(k)
            add(edeps, st["w"])
            add(edeps, st["r"])
            add(ddeps, st["wd"])
            add(ddeps, st["rd"])
        wd = self.waited[eng]
        for e2, i2 in edeps.items():
            if e2 == eng and eng in ("pe", "sp"):
                continue
            if wd.get(("e", e2), -1) >= i2:
                continue
            wd[("e", e2)] = i2
            self.ops[e2][i2].signal = True
            o.waits.append(("e", e2, i2))
        for dk, c in ddeps.items():
            c = self.dcount[dk]
            if wd.get(("d", dk), -1) >= c:
                continue
            wd[("d", dk)] = c
            o.waits.append(("d", dk, c))
        if dkey is not None:
            c = self.dcount.get(dkey, 0) + dinc
            o.dinc = dinc
            self.dcount[dkey] = c
            o.dkey = dkey
            o.dcnt = c
            for k in reads:
                self._st(k)["rd"][dkey] = c
            for k in writes:
                self._st(k)["wd"][dkey] = c
        else:
            for k in reads:
                self._st(k)["r"][eng] = idx
            for k in writes:
                self._st(k)["w"][eng] = idx
        lst.append(o)
        return o

    def dma(self, out, in_, reads, writes, dkey, eng="sp"):
        return self.op(eng, lambda e: e.dma_start(out=out, in_=in_), reads, writes, dkey=dkey)

    def _last_real(self, e):
        lst = self.ops[e]
        for i in range(len(lst) - 1, -1, -1):
            if lst[i].fn is not None and lst[i].dkey is None:
                return i
        return None

    def barrier(self):
        lasts = {}
        for e in ("pe", "act", "dve"):
            i = self._last_real(e)
            if i is not None:
                lasts[e] = i
                self.ops[e][i].signal = True
        dk = dict(self.dcount)
        for e in self.ENG:
            o = Op(None)
            for e2, i2 in lasts.items():
                if e2 != e:
                    o.waits.append(("e", e2, i2))
                    self.waited[e][("e", e2)] = i2
            for k, c in dk.items():
                o.waits.append(("d", k, c))
                self.waited[e][("d", k)] = c
            self.ops[e].append(o)

    def final_wait(self):
        o = Op(None)
        for e in ("pe", "act", "dve"):
            i = self._last_real(e)
            if i is not None:
                self.ops[e][i].signal = True
                o.waits.append(("e", e, i))
        for dk, c in self.dcount.items():
            o.waits.append(("d", dk, c))
        self.ops["sp"].append(o)

    def emit(self):
        nc = self.nc
        import contextlib
        with contextlib.ExitStack() as es:
            esem = {e: es.enter_context(nc.semaphore("s_" + e)) for e in self.ENG}
            dsem = {dk: es.enter_context(nc.semaphore("d_%d" % i)) for i, dk in enumerate(self.dcount)}
            cnt = {}
            for e in self.ENG:
                c = 0
                arr = []
                for o in self.ops[e]:
                    if o.signal and o.dkey is None:
                        c += 1
                    arr.append(c)
                cnt[e] = arr
            block = es.enter_context(nc.Block())

            def run(eng_name, eng):
                for o in self.ops[eng_name]:
                    for w in o.waits:
                        if w[0] == "e":
                            eng.wait_ge(esem[w[1]], cnt[w[1]][w[2]])
                        else:
                            eng.wait_ge(dsem[w[1]], w[2])
                    if o.fn is None:
                        continue
                    ins = o.fn(eng)
                    if o.dkey is not None:
                        if o.dinc == 16:
                            ins.then_inc(dsem[o.dkey], 16)
                        else:
                            ins.then_inc(dsem[o.dkey])
                    elif o.signal:
                        ins.then_inc(esem[eng_name], 1)

            @block.tensor
            def _(e):
                run("pe", e)

            @block.scalar
            def _(e):
                run("act", e)

            @block.vector
            def _(e):
                run("dve", e)

            @block.gpsimd
            def _(e):
                run("pool", e)

            @block.sync
            def _(e):
                run("sp", e)


class Ctx:
    pass


def load_cols(P, c, name, dram_vec_ap, ncols_chunks):
    pass


def norm_T(P, c, xt, gcols, tag):
    nc = P.nc
    for j in range(4):
        P.op("act", lambda e, j=j: e.activation(out=c.junk[:, :], in_=xt[:, j, :], func=AF.Square,
                                                accum_out=c.ss[:, j:j + 1]),
             reads=["xt"], writes=["junk", "ss"])
    P.op("act", lambda e: e.activation(out=c.rs[:, 0:4], in_=c.ss[:, 0:4], func=AF.Sqrt,
                                       bias=c.epsc[:, 0:1], scale=1.0 / D),
         reads=["ss"], writes=["rs"])
    P.op("dve", lambda e: e.reciprocal(out=c.rs[:, 0:4], in_=c.rs[:, 0:4]), reads=["rs"], writes=["rs"])
    for j in range(4):
        P.op("dve", lambda e, j=j: e.tensor_scalar(out=c.h[:, j, :], in0=xt[:, j, :], scalar1=c.rs[:, j:j + 1],
                                                   scalar2=None, op0=ALU.mult),
             reads=["xt", "rs"], writes=["h"])
    for kc in range(8):
        pst = c.pst[kc % 2]
        pk = "pst%d" % (kc % 2)
        for j in range(4):
            P.op("pe", lambda e, j=j, kc=kc, pst=pst: e.transpose(out=pst[:, j * 128:(j + 1) * 128],
                                                                  in_=c.h[:, j, kc * 128:(kc + 1) * 128],
                                                                  identity=c.ident[:, :]),
                 reads=["h"], writes=[pk])
        if kc % 2 == 0:
            P.op("act", lambda e, kc=kc, pst=pst: e.activation(out=c.hT[:, kc, :], in_=pst[:, 0:512], func=AF.Copy,
                                                               scale=gcols[:, kc:kc + 1]),
                 reads=[pk], writes=["hT"])
        else:
            P.op("dve", lambda e, kc=kc, pst=pst: e.tensor_scalar(out=c.hT[:, kc, :], in0=pst[:, 0:512],
                                                                  scalar1=gcols[:, kc:kc + 1], scalar2=None,
                                                                  op0=ALU.mult),
                 reads=[pk], writes=["hT"])


def ffn(P, c, xt, wgu, wdn_sb, cnt):
    wgu_v = wgu.rearrange("(kc p) n -> p kc n", p=128)
    for j in range(22):
        slot = cnt[0] % 2
        cnt[0] += 1
        wk = "wup%d" % slot
        wt = c.wup[slot]
        P.dma(wt[:, :, 0:128], wgu_v[:, :, j * 128:(j + 1) * 128], reads=[], writes=[wk], dkey=wk, eng="pool")
        P.dma(wt[:, :, 128:256], wgu_v[:, :, DFF + j * 128:DFF + (j + 1) * 128], reads=[], writes=[wk], dkey=wk,
              eng="pool")
        pg, pu = c.ps[0 + 2 * (j % 2)], c.ps[1 + 2 * (j % 2)]
        kg, ku = "ps%d" % (2 * (j % 2)), "ps%d" % (1 + 2 * (j % 2))
        for kc in range(8):
            P.op("pe", lambda e, kc=kc, wt=wt, pg=pg: e.matmul(pg[:, :], lhsT=wt[:, kc, 0:128], rhs=c.hT[:, kc, :],
                                                               start=(kc == 0), stop=(kc == 7)),
                 reads=[wk, "hT"], writes=[kg])
        for kc in range(8):
            P.op("pe", lambda e, kc=kc, wt=wt, pu=pu: e.matmul(pu[:, :], lhsT=wt[:, kc, 128:256], rhs=c.hT[:, kc, :],
                                                               start=(kc == 0), stop=(kc == 7)),
                 reads=[wk, "hT"], writes=[ku])
        sg = c.sg[j % 2]
        sk = "sg%d" % (j % 2)
        P.op("act", lambda e, pg=pg, sg=sg: e.activation(out=sg[:, :], in_=pg[:, :], func=AF.Silu),
             reads=[kg], writes=[sk])
        P.op("dve", lambda e, j=j, pu=pu, sg=sg: e.tensor_tensor(out=c.actT[:, j, :], in0=sg[:, :], in1=pu[:, :],
                                                                 op=ALU.mult),
             reads=[sk, ku], writes=["actT"])
    n = 0
    for j in range(4):
        for hf in range(2):
            pd = c.ps[4 + n % 2]
            pk = "ps%d" % (4 + n % 2)
            n += 1
            for jj in range(22):
                P.op("pe", lambda e, j=j, hf=hf, jj=jj, pd=pd: e.matmul(
                    pd[:, :], lhsT=c.actT[:, jj, j * 128:(j + 1) * 128], rhs=wdn_sb[:, jj, hf * 512:(hf + 1) * 512],
                    start=(jj == 0), stop=(jj == 21)), reads=["actT", "wdn"], writes=[pk])
            P.op("dve", lambda e, j=j, hf=hf, pd=pd: e.scalar_tensor_tensor(
                out=xt[:, j, hf * 512:(hf + 1) * 512], in0=pd[:, :], scalar=0.5, in1=xt[:, j, hf * 512:(hf + 1) * 512],
                op0=ALU.mult, op1=ALU.add), reads=[pk, "xt"], writes=["xt"])


def load_wdn(P, c, wdn):
    v = wdn.rearrange("(kc p) n -> p kc n", p=128)
    for q in range(2):
        P.dma(c.wdn[:, q * 11:(q + 1) * 11, :], v[:, q * 11:(q + 1) * 11, :], reads=[], writes=["wdn"], dkey="wdn",
              eng="pool")


def gcol_load(P, dst, vec_ap, nch, key):
    P.dma(dst[:, 0:nch], vec_ap[:, :], reads=[], writes=[key], dkey=key)


DIL = ((1, -128, 512, 1152), (4, -256, 640, 1408), (16, -1024, 1408, 2944))
RPD = 1024
SRC1 = 8 * RPD * 2048
R2D = 3072
SRC2 = 8 * R2D * 256
QD0, KD0, QL0, KL0 = 0, 128, 256, 448
VD0, VL0 = 1280, 1664


class Arena:
    def __init__(self, ap):
        self.a = ap
        self.off = 0

    def reset(self):
        self.off = 0

    def take(self, shape, dt):
        n = 1
        for v in shape[1:]:
            n *= v
        ne = n * 2 if dt in (F32, I32) else n
        ne = (ne + 1) // 2 * 2
        v = self.a[0:shape[0], self.off:self.off + ne]
        self.off += ne
        assert self.off <= self.a.shape[1], ("arena overflow", self.off)
        if dt in (F32, I32):
            v = v.bitcast(dt)
        if len(shape) == 3:
            v = v.rearrange("p (a b) -> p a b", b=shape[2])
        return v


def build_fused(dbg=False):
    import contextlib
    nc = bass.Bass("TRN2", target_bir_lowering=False)
    P = Prog(nc)
    c = Ctx()

    def din(name, shape, dt=F32):
        return nc.dram_tensor(name, shape, dt, kind="ExternalInput").ap()

    def dscr(name, shape, dt=F32):
        return nc.dram_tensor(name, shape, dt).ap()

    xin = din("x", [NT, D])
    yo = nc.dram_tensor("y", [NT, D], F32, kind="ExternalOutput").ap()
    idxd = din("idx", [128, 4], I32)
    tabc = din("tabc", [32, 6])
    n1 = din("n1", [2, 128, 8]); nm = din("nm", [2, 128, 8]); n2 = din("n2", [2, 128, 8])
    nf = din("nf", [2, 1, D])
    wshapes = {"w_gu1": (D, 2 * DFF), "w_dn1": (DFF, D), "w_gu2": (D, 2 * DFF), "w_dn2": (DFF, D),
               "w_in": (D, 3072), "w_out": (D, D)}
    wsl = {k: din(k, [2, kk // 8, nn]) for k, (kk, nn) in wshapes.items()}
    wloc = {k: [dscr("%s_loc%d" % (k, l), [kk // 8, nn], BF16) for l in range(2)] for k, (kk, nn) in wshapes.items()}
    wful = {k: [dscr("%s_bf%d" % (k, l), [kk, nn], BF16) for l in range(2)] for k, (kk, nn) in wshapes.items()}
    wgu1, wdn1, wgu2, wdn2, win, wout = (wful[k] for k in ("w_gu1", "w_dn1", "w_gu2", "w_dn2", "w_in", "w_out"))
    gqd = din("gq", [2, 128, 4]); convw = din("conv_w", [2, 128, 9])
    lamv = din("lamv", [2, 1, 128]); gsub = din("gsub", [2, 1, 64])
    ident_d = din("ident", [128, 128], BF16); idf = din("identf", [128, 128])
    Jd = din("J", [128, 128], BF16)
    bd32 = din("bd32", [128, 128], BF16); bd64 = din("bd64", [128, 128], BF16)
    e15 = din("e15", [32, 128]); e31 = din("e31", [32, 128]); ohrev = din("ohrev", [32, 3328])
    mds = [din("m%d" % r, [128, w], BF16) for (r, _, _, w) in DIL]

    x1s_l = [dscr("x1s%d" % l, [NT, D]) for l in range(2)]
    zs_l = [dscr("zs%d" % l, [384, NT]) for l in range(2)]
    bgs_l = [dscr("bgs%d" % l, [384, NT]) for l in range(2)]
    send1 = dscr("send1", [8 * RPD, 2048], BF16); G1 = dscr("G1", [64 * RPD, 2048], BF16)
    zb = dscr("zb", [7, 384]); G3 = dscr("G3", [56, 384])
    send2 = dscr("send2", [8 * R2D, 256]); G2 = dscr("G2", [64 * R2D, 256])
    T = dscr("Ttab", [6, 3328]); cbd = dscr("cbd", [128, 12])
    send1v = send1.rearrange("r (h e) -> (r h) e", h=2)
    G1v = G1.rearrange("r (h e) -> (r h) e", h=2).rearrange("r (k d) -> r k d", d=64)
    G2v = G2.rearrange("r (j d) -> r j d", d=64)

    if dbg == 2:
        dsend1 = nc.dram_tensor("dsend1", [8 * RPD, 2048], BF16, kind="ExternalOutput").ap()
        dqT = nc.dram_tensor("dqT", [128, 16384], BF16, kind="ExternalOutput").ap()
        dkT = nc.dram_tensor("dkT", [128, 16384], BF16, kind="ExternalOutput").ap()
        dV = nc.dram_tensor("dV", [128, 128 * 65], BF16, kind="ExternalOutput").ap()
    if dbg:
        dsend2 = nc.dram_tensor("dsend2", [8 * R2D, 256], F32, kind="ExternalOutput").ap()
    with contextlib.ExitStack() as es:
        arena_t = es.enter_context(nc.sbuf_tensor("arena", [128, 106200], BF16))
        idx = es.enter_context(nc.sbuf_tensor("idxs", [128, 4], I32))
        banks = [es.enter_context(nc.psum_tensor("bk%d" % i, [128, 512], F32)) for i in range(6)]
        pstb = [es.enter_context(nc.psum_tensor("pstb%d" % i, [128, 1024], BF16)) for i in range(2)]
        A = Arena(arena_t[:, :])
        P.dma(idx[:, :], idxd[:, :], [], ["idx"], "idx")
        c.ps = banks
        c.pst = pstb
        cnt = [0]
        wcnt = [0]
        uq = [0]

        grp = ["cP"]

        def ukey(prefix):
            if prefix == "c":
                return grp[0]
            return prefix

        for k in ("w_gu1", "w_dn1", "w_in", "w_out", "w_gu2", "w_dn2"):
            for l in range(2):
                kk = wshapes[k][0] // 8
                hk = kk // 2
                for q in range(2):
                    P.dma(wloc[k][l][q * hk:(q + 1) * hk, :], wsl[k][l, q * hk:(q + 1) * hk, :], [],
                          ["wloc"], "wc", eng="pool")
        for k in ("w_gu1", "w_dn1", "w_in", "w_out", "w_gu2", "w_dn2"):
            for l in range(2):
                P.op("pool", lambda e, k=k, l=l: e.collective_compute(
                    "AllGather", ALU.bypass, replica_groups=[list(range(8))],
                    ins=[wloc[k][l].tensor.ap().opt()], outs=[wful[k][l].tensor.ap().opt()]),
                     ["wloc"], ["wful"], dkey="agw", dinc=1)
        tab = A.take([32, 6], F32); oh = A.take([32, 3328], F32)
        e15s = A.take([32, 128], F32); e31s = A.take([32, 128], F32)
        tsb = A.take([6, 3328], F32); cb = A.take([128, 12], F32); zrow = A.take([1, 384], F32)
        P.dma(tab[:, :], tabc[:, :], [], ["tab"], ukey("c"))
        P.dma(oh[:, :], ohrev[:, :], [], ["oh"], ukey("c"))
        P.dma(e15s[:, :], e15[:, :], [], ["e15"], ukey("c"))
        P.dma(e31s[:, :], e31[:, :], [], ["e31"], ukey("c"))
        ptb = banks[0]
        for i in range(7):
            n = 512 if i < 6 else 256
            P.op("pe", lambda e, i=i, n=n: e.matmul(ptb[0:6, 0:n], lhsT=tab[:, 0:6], rhs=oh[:, i * 512:i * 512 + n],
                                                   start=True, stop=True), ["tab", "oh"], ["ptb"])
            P.op("act", lambda e, i=i, n=n: e.activation(out=tsb[0:6, i * 512:i * 512 + n], in_=ptb[0:6, 0:n],
                                                         func=AF.Copy), ["ptb"], ["tsb"])
        P.dma(T[:, :], tsb[:, :], ["tsb"], ["T"], "T")
        P.op("pe", lambda e: e.matmul(ptb[:, 0:6], lhsT=e15s[:, :], rhs=tab[:, 0:6], start=True, stop=True),
             ["e15", "tab"], ["ptb"])
        P.op("act", lambda e: e.activation(out=cb[:, 0:6], in_=ptb[:, 0:6], func=AF.Copy), ["ptb"], ["cb"])
        P.op("pe", lambda e: e.matmul(ptb[:, 0:6], lhsT=e31s[:, :], rhs=tab[:, 0:6], start=True, stop=True),
             ["e31", "tab"], ["ptb"])
        P.op("act", lambda e: e.activation(out=cb[:, 6:12], in_=ptb[:, 0:6], func=AF.Copy), ["ptb"], ["cb"])
        P.dma(cbd[:, :], cb[:, :], ["cb"], ["cbd"], "cbd")
        P.op("dve", lambda e: e.memset(zrow[:, :], 0.0), [], ["zrow"])
        P.dma(zb[6:7, :], zrow[:, :], ["zrow"], ["zb"], "zb6")

        def carve_AC():
            A.reset()
            c.xt = A.take([128, 4, 1024], F32)
            c.h = A.take([128, 4, 1024], BF16)
            c.hT = A.take([128, 8, 512], BF16)
            c.actT = A.take([128, 22, 512], BF16)
            c.junk = A.take([128, 1024], BF16)
            c.ss = A.take([128, 8], F32); c.rs = A.take([128, 8], F32); c.epsc = A.take([128, 2], F32)
            c.ident = A.take([128, 128], BF16)
            c.wup = [A.take([128, 8, 256], BF16) for _ in range(2)]
            c.wdn = A.take([128, 22, 1024], BF16)
            c.sg = [A.take([128, 512], F32) for _ in range(2)]
            c.gcols = A.take([128, 3, 8], F32)
            c.yT = A.take([128, 8, 512], BF16); c.ytm = A.take([128, 4, 640], BF16)
            odf = A.take([128, 2048], F32)
            c.od = odf.rearrange("p (u n) -> p u n", n=256)
            c.usb_view = odf[:, 0:1536].rearrange("p (a b) -> p a b", b=512)
            olf = A.take([128, 1536], F32)
            c.ol = olf.rearrange("p (u n) -> p u n", n=256)
            c.ob = A.take([128, 256], F32); c.wout = A.take([128, 8, 1024], BF16)
            c.zh = A.take([128, 3, 514], F32); c.bg = A.take([128, 3, 512], F32)
            c.cw = A.take([128, 3, 3], F32); c.ca = A.take([128, 512], F32)
            c.nlam = A.take([128, 2], F32); c.gsub = A.take([128, 64], F32); c.nf = A.take([128, 1024], F32)
            c.lamv = A.take([128, 128], F32); c.lt = A.take([128, 4], F32)
            c.hal = A.take([128, 384], F32); c.halT = A.take([128, 3, 8], F32); c.identf = A.take([128, 128], F32)
            c.usb = c.usb_view; c.zt = c.zh[:, :, 0:512]; c.bgt = c.bg
            c.k = {"usb": "od", "zt": "zh", "bgt": "bg"}
            c.wch = [A.take([128, 8, 128], BF16) for _ in range(2)]
            c.wv = A.take([128, 8, 640], BF16)
            c.sq = [A.take([128, 512], BF16) for _ in range(2)]
            c.rq = [A.take([128, 512], F32) for _ in range(2)]
            c.qk = [A.take([128, 512], BF16) for _ in range(2)]
            c.vt = A.take([128, 4, 640], BF16)
            c.gq = A.take([128, 4], F32); c.bd32 = A.take([128, 128], BF16); c.bd64 = A.take([128, 128], BF16)

        def consts_A(l):
            grp[0] = "cA"
            P.dma(c.ident[:, :], ident_d[:, :], [], ["ident"], ukey("c"))
            P.dma(c.gcols[:, 0, :], n1[l], [], ["gcols"], ukey("c"))
            P.dma(c.gcols[:, 1, :], nm[l], [], ["gcols"], ukey("c"))
            P.dma(c.gq[:, :], gqd[l], [], ["gq"], ukey("c"))
            P.dma(c.bd32[:, :], bd32[:, :], [], ["bd"], ukey("c"))
            P.dma(c.bd64[:, :], bd64[:, :], [], ["bd"], ukey("c"))
            P.op("dve", lambda e: e.memset(c.epsc[:, :], EPS), [], ["epsc"])
            P.op("dve", lambda e: e.tensor_scalar(out=c.gq[:, 0:1], in0=c.gq[:, 0:1], scalar1=32 ** -0.5, scalar2=None,
                                                  op0=ALU.mult), ["gq"], ["gq"])
            P.op("dve", lambda e: e.tensor_scalar(out=c.gq[:, 2:3], in0=c.gq[:, 2:3], scalar1=64 ** -0.5, scalar2=None,
                                                  op0=ALU.mult), ["gq"], ["gq"])
            wv_v = win[l].rearrange("(kc p) n -> p kc n", p=128)
            P.dma(c.wv[:, :, 0:256], wv_v[:, :, 1664:1920], [], ["wv"], "wv", eng="pool")
            P.dma(c.wv[:, :, 256:640], wv_v[:, :, 2688:3072], [], ["wv"], "wv", eng="pool")

        def consts_C(l):
            lam_init = 0.8 - 0.6 * math.exp(-0.3 * l)
            grp[0] = "cC"
            P.dma(c.ident[:, :], ident_d[:, :], [], ["ident"], ukey("c"))
            P.dma(c.identf[:, :], idf[:, :], [], ["identf"], ukey("c"))
            P.dma(c.gcols[:, 2, :], n2[l], [], ["gcols"], ukey("c"))
            P.dma(c.cw[:, :, :], convw[l].rearrange("p (c k) -> p c k", k=3), [], ["cw"], ukey("c"))
            P.dma(c.lamv[:, :], lamv[l].partition_broadcast(128), [], ["lamv"], ukey("c"))
            P.dma(c.gsub[:, :], gsub[l].partition_broadcast(128), [], ["gsub"], ukey("c"))
            P.dma(c.nf[:, :], nf[l].partition_broadcast(128), [], ["nf"], ukey("c"))
            P.op("dve", lambda e: e.memset(c.epsc[:, :], EPS), [], ["epsc"])
            for q in range(2):
                P.op("dve", lambda e, q=q: e.tensor_tensor(out=c.lamv[:, q * 64:q * 64 + 32],
                                                           in0=c.lamv[:, q * 64:q * 64 + 32],
                                                           in1=c.lamv[:, q * 64 + 32:q * 64 + 64], op=ALU.mult),
                     ["lamv"], ["lamv"])
                P.op("dve", lambda e, q=q: e.tensor_reduce(out=c.lt[:, q:q + 1], in_=c.lamv[:, q * 64:q * 64 + 32],
                                                           axis=mybir.AxisListType.X, op=ALU.add), ["lamv"], ["lt"])
            P.op("act", lambda e: e.activation(out=c.lt[:, 2:4], in_=c.lt[:, 0:2], func=AF.Exp), ["lt"], ["lt"])
            P.op("dve", lambda e: e.tensor_tensor(out=c.nlam[:, 0:1], in0=c.lt[:, 3:4], in1=c.lt[:, 2:3],
                                                  op=ALU.subtract), ["lt"], ["nlam"])
            P.op("dve", lambda e: e.tensor_scalar(out=c.nlam[:, 0:1], in0=c.nlam[:, 0:1], scalar1=-lam_init,
                                                  scalar2=None, op0=ALU.add), ["nlam"], ["nlam"])
            P.op("dve", lambda e: e.tensor_scalar(out=c.gsub[:, :], in0=c.gsub[:, :], scalar1=1.0 - lam_init,
                                                  scalar2=None, op0=ALU.mult), ["gsub"], ["gsub"])
            wo_v = wout[l].rearrange("(kc p) n -> p kc n", p=128)
            for q in range(2):
                P.dma(c.wout[:, q * 4:(q + 1) * 4, :], wo_v[:, q * 4:(q + 1) * 4, :], [], ["wout"], "wout", eng="pool")
            P.op("pool", lambda e: e.indirect_dma_start(
                out=c.hal[:, :], out_offset=None, in_=G3[:, :],
                in_offset=bass.IndirectOffsetOnAxis(ap=idx[:, 3:4], axis=0)), ["G3", "idx"], ["hal"], dkey="hal")
            for ch in range(3):
                P.op("pe", lambda e, ch=ch: e.transpose(out=banks[0][:, ch * 8:(ch + 1) * 8],
                                                        in_=c.hal[0:8, ch * 128:(ch + 1) * 128],
                                                        identity=c.identf[0:8, 0:8]), ["hal", "identf"], ["ps0"])
            P.op("act", lambda e: e.activation(out=c.halT[:, :, :],
                                               in_=banks[0][:, 0:24].rearrange("p (c k) -> p c k", k=8),
                                               func=AF.Copy), ["ps0"], ["halT"])

        def phaseA_tile(l, ti, wdn_reload_first, wdn_reload_after):
            x1s, zs, bgs = x1s_l[l], zs_l[l], bgs_l[l]
            g0 = ti * 512
            s, tq = ti // 4, ti % 4
            if wdn_reload_first is not None:
                load_wdn(P, c, wdn_reload_first)
            norm_T(P, c, c.xt, c.gcols[:, 0, :], "n1")
            ffn(P, c, c.xt, wgu1[l], c.wdn, cnt)
            if wdn_reload_after is not None:
                load_wdn(P, c, wdn_reload_after)
            P.dma(x1s[g0:g0 + 512, :].rearrange("(j p) n -> p j n", p=128), c.xt[:, :, :], ["xt"], ["x1s"], "xo")
            norm_T(P, c, c.xt, c.gcols[:, 1, :], "nm")
            win_v = win[l].rearrange("(kc p) n -> p kc n", p=128)
            chunks = [(0, "u", 0), (128, "u", 1), (256, "u", 2), (768, "cg", 0), (896, "cg", 1), (1024, "cg", 2),
                      (384, "bg", 0), (512, "bg", 1), (640, "bg", 2),
                      (1152, "dq", 0), (1280, "dq", 1), (1408, "dk", 0), (1536, "dk", 1),
                      (1920, "lq", 0), (2048, "lq", 1), (2176, "lq", 2), (2304, "lk", 0), (2432, "lk", 1),
                      (2560, "lk", 2)]
            n = 0
            for (col, kind, ci) in chunks:
                slot = wcnt[0] % 2
                wcnt[0] += 1
                wk = "wch%d" % slot
                wt = c.wch[slot]
                P.dma(wt[:, :, :], win_v[:, :, col:col + 128], [], [wk], wk, eng="pool")
                pp = c.ps[n % 2]
                pk = "ps%d" % (n % 2)
                n += 1
                for kc in range(8):
                    P.op("pe", lambda e, kc=kc, wt=wt, pp=pp: e.matmul(pp[:, :], lhsT=wt[:, kc, :], rhs=c.hT[:, kc, :],
                                                                       start=(kc == 0), stop=(kc == 7)),
                         [wk, "hT"], [pk])
                if kind == "u":
                    P.op("act", lambda e, ci=ci, pp=pp: e.activation(out=c.usb[:, ci, :], in_=pp[:, :], func=AF.Copy),
                         [pk], [c.k["usb"]])
                elif kind == "cg":
                    P.op("dve", lambda e, ci=ci, pp=pp: e.tensor_tensor(out=c.zt[:, ci, :], in0=pp[:, :],
                                                                        in1=c.usb[:, ci, :], op=ALU.mult),
                         [pk, c.k["usb"]], [c.k["zt"]])
                elif kind == "bg":
                    P.op("act", lambda e, ci=ci, pp=pp: e.activation(out=c.bgt[:, ci, :], in_=pp[:, :], func=AF.Copy),
                         [pk], [c.k["bgt"]])
                else:
                    is32 = kind in ("dq", "dk")
                    gi = {"dq": 0, "dk": 1, "lq": 2, "lk": 3}[kind]
                    b = n % 2
                    sq, rq, qk = c.sq[b], c.rq[b], c.qk[b]
                    pq = c.ps[2 + b]
                    pqk = "ps%d" % (2 + b)
                    P.op("act", lambda e, sq=sq, pp=pp: e.activation(out=sq[:, :], in_=pp[:, :], func=AF.Square),
                         [pk], ["sq%d" % b])
                    bd = c.bd32 if is32 else c.bd64
                    P.op("pe", lambda e, sq=sq, pq=pq, bd=bd: e.matmul(pq[:, :], lhsT=bd[:, :], rhs=sq[:, :], start=True,
                                                                       stop=True), ["sq%d" % b, "bd"], [pqk])
                    P.op("act", lambda e, rq=rq, pq=pq, is32=is32: e.activation(
                        out=rq[:, :], in_=pq[:, :], func=AF.Sqrt, bias=c.epsc[:, 0:1],
                        scale=1.0 / (32 if is32 else 64)), [pqk, "epsc"], ["rq%d" % b])
                    P.op("dve", lambda e, rq=rq: e.reciprocal(out=rq[:, :], in_=rq[:, :]), ["rq%d" % b], ["rq%d" % b])
                    P.op("dve", lambda e, rq=rq, qk=qk, pp=pp, gi=gi: e.scalar_tensor_tensor(
                        out=qk[:, :], in0=pp[:, :], scalar=c.gq[:, gi:gi + 1], in1=rq[:, :], op0=ALU.mult,
                        op1=ALU.mult), [pk, "gq", "rq%d" % b], ["qk%d" % b])
                    if is32:
                        sec = QD0 if kind == "dq" else KD0
                        for g in range(4):
                            u = s * 8 + ci * 4 + g
                            dest, sl = u // 3, u % 3
                            r0 = dest * RPD + sec + sl * 32
                            P.dma(send1[r0:r0 + 32, tq * 512:(tq + 1) * 512], qk[32 * g:32 * g + 32, :],
                                  ["qk%d" % b], ["send1"], "qko%d" % b)
                    else:
                        sec = QL0 if kind == "lq" else KL0
                        for g in range(2):
                            v = s * 6 + ci * 2 + g
                            dest, sl = v // 3, v % 3
                            r0 = dest * RPD + sec + sl * 64
                            P.dma(send1[r0:r0 + 64, tq * 512:(tq + 1) * 512], qk[64 * g:64 * g + 64, :],
                                  ["qk%d" % b], ["send1"], "qko%d" % b)
            P.dma(zs[:, g0:g0 + 512].rearrange("(c p) n -> p c n", p=128), c.zt[:, :, :], [c.k["zt"]], ["zs"], "zo")
            P.dma(bgs[:, g0:g0 + 512].rearrange("(c p) n -> p c n", p=128), c.bgt[:, :, :], [c.k["bgt"]], ["bgs"],
                  "bgo")
            if tq == 0 or tq == 3:
                row = s * 2 + (0 if tq == 0 else 1)
                col = 0 if tq == 0 else 511
                dst = bass.AP(tensor=zb.tensor, offset=row * 384, ap=[[1, 128], [128, 3], [1, 1]])
                P.op("sp", lambda e, dst=dst, col=col: e.dma_start(out=dst, in_=c.zt[:, :, col:col + 1],
                                                                   allow_slow_non_contiguous=True),
                     [c.k["zt"]], ["zb"], dkey="zbo")
            for j in range(4):
                pa, pb = c.ps[4], c.ps[5]
                for kc in range(8):
                    P.op("pe", lambda e, j=j, kc=kc: e.matmul(pa[:, 0:256], lhsT=c.hT[:, kc, j * 128:(j + 1) * 128],
                                                              rhs=c.wv[:, kc, 0:256], start=(kc == 0), stop=(kc == 7)),
                         ["hT", "wv"], ["ps4"])
                for kc in range(8):
                    P.op("pe", lambda e, j=j, kc=kc: e.matmul(pb[:, 0:384], lhsT=c.hT[:, kc, j * 128:(j + 1) * 128],
                                                              rhs=c.wv[:, kc, 256:640], start=(kc == 0),
                                                              stop=(kc == 7)), ["hT", "wv"], ["ps5"])
                P.op("act", lambda e, j=j: e.activation(out=c.vt[:, j, 0:256], in_=pa[:, 0:256], func=AF.Copy),
                     ["ps4"], ["vt"])
                P.op("dve", lambda e, j=j: e.tensor_copy(out=c.vt[:, j, 256:640], in_=pb[:, 0:384]), ["ps5"], ["vt"])
            for hc in range(8):
                u = s * 8 + hc
                dest, sl = u // 3, u % 3
                h = hc // 2
                r0 = dest * 2 * RPD + VD0 + sl * 128
                P.dma(send1v[r0:r0 + 128, tq * 256:(tq + 1) * 256].rearrange("p (j d) -> p j d", d=64),
                      c.vt[:, :, h * 64:(h + 1) * 64], ["vt"], ["send1"], "vo")
            for h6 in range(6):
                v = s * 6 + h6
                dest, sl = v // 3, v % 3
                r0 = dest * 2 * RPD + VL0 + sl * 128
                P.dma(send1v[r0:r0 + 128, tq * 256:(tq + 1) * 256].rearrange("p (j d) -> p j d", d=64),
                      c.vt[:, :, 256 + h6 * 64:256 + (h6 + 1) * 64], ["vt"], ["send1"], "vo")

        def phaseC_tile(l, ti, last):
            x1s, zs, bgs = x1s_l[l], zs_l[l], bgs_l[l]
            g0 = ti * 512
            s, tq = ti // 4, ti % 4
            P.dma(c.xt[:, :, :], x1s[g0:g0 + 512, :].rearrange("(j p) n -> p j n", p=128), ["x1s"], ["xt"], "xt")
            for hc in range(8):
                u = s * 8 + hc
                j, k = u // 3, u % 3
                eo = j * SRC2 + ((k * 4 + tq) * 128) * 256
                P.op("pool", lambda e, hc=hc, eo=eo: e.indirect_dma_start(
                    out=c.od[:, hc, :], out_offset=None, in_=G2[:, :],
                    in_offset=bass.IndirectOffsetOnAxis(ap=idx[:, 2:3], axis=0), element_offset=eo),
                     ["G2", "idx"], ["od"], dkey="od")
            for h6 in range(6):
                v = s * 6 + h6
                j, k = v // 3, v % 3
                eo = j * SRC2 + (((3 + k) * 4 + tq) * 128) * 256
                P.op("pool", lambda e, h6=h6, eo=eo: e.indirect_dma_start(
                    out=c.ol[:, h6, :], out_offset=None, in_=G2[:, :],
                    in_offset=bass.IndirectOffsetOnAxis(ap=idx[:, 2:3], axis=0), element_offset=eo),
                     ["G2", "idx"], ["ol"], dkey="ol")
            zsv = zs.rearrange("(c p) n -> p c n", p=128)
            if tq == 0:
                P.dma(c.zh[:, :, 1:514], zsv[:, :, g0:g0 + 513], ["zs"], ["zh"], "zh")
                P.op("act", lambda e, s=s: e.activation(out=c.zh[:, :, 0:1], in_=c.halT[:, :, s:s + 1], func=AF.Copy),
                     ["halT"], ["zh"])
            elif tq == 3:
                P.dma(c.zh[:, :, 0:513], zsv[:, :, g0 - 1:g0 + 512], ["zs"], ["zh"], "zh")
                P.op("act", lambda e, s=s: e.activation(out=c.zh[:, :, 513:514], in_=c.halT[:, :, 3 + s:4 + s],
                                                        func=AF.Copy), ["halT"], ["zh"])
            else:
                P.dma(c.zh[:, :, :], zsv[:, :, g0 - 1:g0 + 513], ["zs"], ["zh"], "zh")
            P.dma(c.bg[:, :, :], bgs[:, g0:g0 + 512].rearrange("(c p) n -> p c n", p=128), ["bgs"], ["bg"], "bg")
            for ch in range(3):
                P.op("dve", lambda e, ch=ch: e.tensor_scalar(out=c.ca[:, :], in0=c.zh[:, ch, 0:512],
                                                             scalar1=c.cw[:, ch, 0:1], scalar2=None, op0=ALU.mult),
                     ["zh", "cw"], ["ca"])
                P.op("dve", lambda e, ch=ch: e.scalar_tensor_tensor(out=c.ca[:, :], in0=c.zh[:, ch, 1:513],
                                                                    scalar=c.cw[:, ch, 1:2], in1=c.ca[:, :],
                                                                    op0=ALU.mult, op1=ALU.add),
                     ["zh", "cw", "ca"], ["ca"])
                P.op("dve", lambda e, ch=ch: e.scalar_tensor_tensor(out=c.ca[:, :], in0=c.zh[:, ch, 2:514],
                                                                    scalar=c.cw[:, ch, 2:3], in1=c.ca[:, :],
                                                                    op0=ALU.mult, op1=ALU.add),
                     ["zh", "cw", "ca"], ["ca"])
                P.op("dve", lambda e, ch=ch: e.tensor_tensor(out=c.yT[:, ch, :], in0=c.ca[:, :], in1=c.bg[:, ch, :],
                                                             op=ALU.mult), ["ca", "bg"], ["yT"])
            for j in range(4):
                for hh in range(4):
                    o1 = c.od[:, 2 * hh, j * 64:(j + 1) * 64]
                    o2 = c.od[:, 2 * hh + 1, j * 64:(j + 1) * 64]
                    obv = c.ob[:, hh * 64:(hh + 1) * 64]
                    P.op("dve", lambda e, o1=o1, o2=o2, obv=obv: e.scalar_tensor_tensor(
                        out=obv, in0=o2, scalar=c.nlam[:, 0:1], in1=o1, op0=ALU.mult, op1=ALU.add),
                         ["od", "nlam"], ["ob"])
                    P.op("act", lambda e, obv=obv, hh=hh: e.activation(out=c.junk[:, 0:64], in_=obv, func=AF.Square,
                                                                       accum_out=c.ss[:, 4 + hh:5 + hh]),
                         ["ob"], ["junk", "ss"])
                P.op("act", lambda e: e.activation(out=c.rs[:, 4:8], in_=c.ss[:, 4:8], func=AF.Sqrt,
                                                   bias=c.epsc[:, 0:1], scale=1.0 / 64), ["ss", "epsc"], ["rs"])
                P.op("dve", lambda e: e.reciprocal(out=c.rs[:, 4:8], in_=c.rs[:, 4:8]), ["rs"], ["rs"])
                for hh in range(4):
                    obv = c.ob[:, hh * 64:(hh + 1) * 64]
                    P.op("dve", lambda e, obv=obv, hh=hh, j=j: e.scalar_tensor_tensor(
                        out=c.ytm[:, j, hh * 64:(hh + 1) * 64], in0=obv, scalar=c.rs[:, 4 + hh:5 + hh],
                        in1=c.gsub[:, :], op0=ALU.mult, op1=ALU.mult), ["ob", "rs", "gsub"], ["ytm"])
                P.op("act", lambda e, j=j: e.activation(
                    out=c.ytm[:, j, 256:640].rearrange("p (h d) -> p h d", d=64), in_=c.ol[:, :, j * 64:(j + 1) * 64],
                    func=AF.Copy), ["ol"], ["ytm"])
            for kc in range(5):
                pst = c.pst[kc % 2]
                pk = "pst%d" % (kc % 2)
                for j in range(4):
                    P.op("pe", lambda e, j=j, kc=kc, pst=pst: e.transpose(
                        out=pst[:, j * 128:(j + 1) * 128], in_=c.ytm[:, j, kc * 128:(kc + 1) * 128],
                        identity=c.ident[:, :]), ["ytm", "ident"], [pk])
                if kc % 2 == 0:
                    P.op("act", lambda e, kc=kc, pst=pst: e.activation(out=c.yT[:, 3 + kc, :], in_=pst[:, 0:512],
                                                                       func=AF.Copy), [pk], ["yT"])
                else:
                    P.op("dve", lambda e, kc=kc, pst=pst: e.tensor_copy(out=c.yT[:, 3 + kc, :], in_=pst[:, 0:512]),
                         [pk], ["yT"])
            n = 0
            for j in range(4):
                for hf in range(2):
                    pd = c.ps[4 + n % 2]
                    pk = "ps%d" % (4 + n % 2)
                    n += 1
                    for kc in range(8):
                        P.op("pe", lambda e, j=j, hf=hf, kc=kc, pd=pd: e.matmul(
                            pd[:, :], lhsT=c.yT[:, kc, j * 128:(j + 1) * 128],
                            rhs=c.wout[:, kc, hf * 512:(hf + 1) * 512], start=(kc == 0), stop=(kc == 7)),
                             ["yT", "wout"], [pk])
                    P.op("dve", lambda e, j=j, hf=hf, pd=pd: e.tensor_tensor(
                        out=c.xt[:, j, hf * 512:(hf + 1) * 512], in0=pd[:, :],
                        in1=c.xt[:, j, hf * 512:(hf + 1) * 512], op=ALU.add), [pk, "xt"], ["xt"])
            norm_T(P, c, c.xt, c.gcols[:, 2, :], "n2")
            ffn(P, c, c.xt, wgu2[l], c.wdn, cnt)
            for j in range(4):
                P.op("act", lambda e, j=j: e.activation(out=c.junk[:, :], in_=c.xt[:, j, :], func=AF.Square,
                                                        accum_out=c.ss[:, j:j + 1]), ["xt"], ["junk", "ss"])
            P.op("act", lambda e: e.activation(out=c.rs[:, 0:4], in_=c.ss[:, 0:4], func=AF.Sqrt,
                                               bias=c.epsc[:, 0:1], scale=1.0 / D), ["ss", "epsc"], ["rs"])
            P.op("dve", lambda e: e.reciprocal(out=c.rs[:, 0:4], in_=c.rs[:, 0:4]), ["rs"], ["rs"])
            for j in range(4):
                P.op("dve", lambda e, j=j: e.scalar_tensor_tensor(out=c.xt[:, j, :], in0=c.xt[:, j, :],
                                                                  scalar=c.rs[:, j:j + 1], in1=c.nf[:, :],
                                                                  op0=ALU.mult, op1=ALU.mult),
                     ["xt", "rs", "nf"], ["xt"])
            if last:
                P.dma(yo[g0:g0 + 512, :].rearrange("(j p) n -> p j n", p=128), c.xt[:, :, :], ["xt"], ["yo"], "yo")

        def phaseB():
            A.reset()
            qT = A.take([128, 8, 2048], BF16); kT = A.take([128, 8, 2048], BF16)
            V = A.take([128, 128, 65], BF16); H = A.take([128, HW_], BF16)
            Vc = A.take([128, 128, 64], BF16)
            Vcf = Vc.rearrange("p k d -> p (k d)")
            G1h = G1.rearrange("r (h e) -> (r h) e", h=2)
            J = A.take([128, 128], BF16); identf = A.take([128, 128], F32)
            ms = [A.take([128, w], BF16) for (r, _, _, w) in DIL]
            pT = [A.take([128, 512], BF16) for _ in range(3)]
            osb = A.take([65, 512], F32); rc = A.take([128, 4], F32); on = A.take([128, 4, 64], F32)
            cbs = A.take([128, 12], F32)
            qTf = qT.rearrange("p r t -> p (r t)")
            kTf = kT.rearrange("p r t -> p (r t)")
            pss = [banks[0], banks[1]]
            pso, ptr = banks[2], banks[3]
            grp[0] = "cB"
            P.dma(J[:, :], Jd[:, :], [], ["J"], ukey("c"))
            P.dma(identf[:, :], idf[:, :], [], ["identfB"], ukey("c"))
            P.dma(cbs[:, :], cbd[:, :], ["cbd"], ["cbs"], ukey("c"))
            for i in range(3):
                P.dma(ms[i][:, :], mds[i][:, :], [], ["ms"], ukey("c"))
            P.op("dve", lambda e: e.memset(V[:, :, 64:65], 1.0), [], ["Vones"])
            bcnt = [0]

            def load_qk(sec_q, sec_k):
                for r in range(8):
                    P.op("pool", lambda e, r=r: e.indirect_dma_start(
                        out=qT[:, r, :], out_offset=None, in_=G1[:, :],
                        in_offset=bass.IndirectOffsetOnAxis(ap=idx[:, 0:1], axis=0),
                        element_offset=r * SRC1 + sec_q * 2048), ["G1", "idx"], ["qT"], dkey="qT")
                    P.op("pool", lambda e, r=r: e.indirect_dma_start(
                        out=kT[:, r, :], out_offset=None, in_=G1[:, :],
                        in_offset=bass.IndirectOffsetOnAxis(ap=idx[:, 0:1], axis=0),
                        element_offset=r * SRC1 + sec_k * 2048), ["G1", "idx"], ["kT"], dkey="kT")

            def unit(slot, is_dil, pb):
                K = 64 if is_dil else 32
                k = slot - 3 if is_dil else slot
                vsec = (VL0 if is_dil else VD0) + k * 128
                for r in range(8):
                    P.op("pool", lambda e, r=r: e.indirect_dma_start(
                        out=Vcf[:, r * 1024:(r + 1) * 1024], out_offset=None, in_=G1h[:, :],
                        in_offset=bass.IndirectOffsetOnAxis(ap=idx[:, 1:2], axis=0),
                        element_offset=r * SRC1 + vsec * 1024), ["G1", "idx"], ["Vc"], dkey="V")
                P.op("dve", lambda e: e.tensor_copy(out=V[:, :, 0:64], in_=Vc[:, :, :]), ["Vc"], ["V"])
                hsrc = bass.AP(tensor=T.tensor, offset=slot * 3328, ap=[[1, 128], [1, HW_]])
                P.dma(H[:, :], hsrc, ["T"], ["H"], "H", eng="pool")
                for qt in ([5] if dbg == 2 else [0, 5, 31] if dbg else range(32)):
                    qb = qt * 512
                    tiles = []
                    if is_dil:
                        for bi, (r, dmin, dmax, w) in enumerate(DIL):
                            for kt in range(max(0, (qb + dmin) // 128), min(127, (qb + dmax) // 128) + 1):
                                tiles.append((kt, bi))
                    else:
                        tiles = [(kt, None) for kt in range(128)]
                    for ti, (kt, bi) in enumerate(tiles):
                        d = kt * 128 - qb
                        near = is_dil or (-640 <= d <= 1024)
                        i = bcnt[0]
                        bcnt[0] += 1
                        ps = pss[i % 2]
                        pk = "pss%d" % (i % 2)
                        pt = pT[i % 3]
                        ptk = "pT%d" % (i % 3)
                        P.op("pe", lambda e, kt=kt, qb=qb, ps=ps, near=near: e.matmul(
                            ps[:, :], lhsT=kTf[pb:pb + K, kt * 128:(kt + 1) * 128], rhs=qTf[pb:pb + K, qb:qb + 512],
                            start=True, stop=(not near)), ["kT", "qT"], [pk])
                        if near:
                            off = 1536 - d
                            P.op("pe", lambda e, ps=ps, off=off: e.matmul(ps[:, :], lhsT=J[:, :],
                                                                          rhs=H[:, off:off + 512], start=False,
                                                                          stop=True), ["J", "H"], [pk])
                            P.op("act", lambda e, ps=ps, pt=pt: e.activation(out=pt[:, :], in_=ps[:, :], func=AF.Exp),
                                 [pk], [ptk])
                        else:
                            col = slot if d < 0 else 6 + slot
                            P.op("act", lambda e, ps=ps, pt=pt, col=col: e.activation(
                                out=pt[:, :], in_=ps[:, :], func=AF.Exp, bias=cbs[:, col:col + 1], scale=1.0),
                                 [pk, "cbs"], [ptk])
                        if is_dil:
                            r, dmin, dmax, w = DIL[bi]
                            o = dmax - d
                            P.op("dve", lambda e, pt=pt, bi=bi, o=o: e.tensor_tensor(out=pt[:, :], in0=pt[:, :],
                                                                                     in1=ms[bi][:, o:o + 512],
                                                                                     op=ALU.mult), [ptk, "ms"], [ptk])
                        P.op("pe", lambda e, kt=kt, pt=pt, ti=ti, nt=len(tiles): e.matmul(
                            pso[0:65, :], lhsT=V[:, kt, 0:65], rhs=pt[:, :], start=(ti == 0), stop=(ti == nt - 1)),
                             ["V", "Vones", ptk], ["pso"])
                    P.op("act", lambda e: e.activation(out=osb[:, :], in_=pso[0:65, :], func=AF.Copy), ["pso"],
                         ["osb"])
                    for j in range(4):
                        P.op("pe", lambda e, j=j: e.transpose(out=ptr[:, j * 65:(j + 1) * 65],
                                                              in_=osb[:, j * 128:(j + 1) * 128],
                                                              identity=identf[0:65, 0:65]), ["osb", "identfB"],
                             ["ptr"])
                    ptv = ptr[:, 0:260].rearrange("p (j d) -> p j d", d=65)
                    P.op("dve", lambda e, ptv=ptv: e.reciprocal(out=rc[:, :], in_=ptv[:, :, 64]), ["ptr"], ["rc"])
                    for j in range(4):
                        P.op("dve", lambda e, j=j, ptv=ptv: e.tensor_scalar(out=on[:, j, :], in0=ptv[:, j, 0:64],
                                                                            scalar1=rc[:, j:j + 1], scalar2=None,
                                                                            op0=ALU.mult), ["ptr", "rc"], ["on"])
                    dest, tq = qt // 4, qt % 4
                    r0 = dest * R2D + (slot * 4 + tq) * 128
                    P.dma(send2[r0:r0 + 128, :].rearrange("p (j d) -> p j d", d=64), on[:, :, :], ["on"], ["send2"],
                          "on")

            load_qk(QD0, KD0)
            if dbg == 2:
                unit(0, False, 0)
                P.dma(dqT[:, :], qTf[:, :], ["qT"], ["dqT"], "dbg5")
                P.dma(dkT[:, :], kTf[:, :], ["kT"], ["dkT"], "dbg6")
                P.dma(dV[:, :], V.rearrange("p k d -> p (k d)"), ["V", "Vones"], ["dV"], "dbg7")
                P.dma(dsend1[:, :], send1[:, :], ["send1"], ["dsend1"], "dbg8")
                return
            for slot in range(3):
                unit(slot, False, 32 * slot)
            load_qk(QL0, KL0)
            unit(3, True, 0)
            unit(4, True, 64)
            load_qk(QL0 + 128, KL0 + 128)
            unit(5, True, 0)

        def allgather(src, dst, rkey, wkey):
            P.op("pool", lambda e: e.collective_compute("AllGather", ALU.bypass, replica_groups=[list(range(8))],
                                                        ins=[src.tensor.ap().opt()], outs=[dst.tensor.ap().opt()]),
                 [rkey], [wkey], dkey="ag", dinc=1)

        P.barrier()
        carve_AC()
        consts_A(0)
        load_wdn(P, c, wdn1[0])
        for ti in range(12):
            g0 = ti * 512
            P.dma(c.xt[:, :, :], xin[g0:g0 + 512, :].rearrange("(j p) n -> p j n", p=128), [], ["xt"], "xt")
            phaseA_tile(0, ti, None, None)
        for l in ([0] if dbg else range(2)):
            P.barrier()
            allgather(send1, G1, "send1", "G1")
            allgather(zb, G3, "zb", "G3")
            P.barrier()
            phaseB()
            if dbg == 2:
                break
            P.barrier()
            allgather(send2, G2, "send2", "G2")
            P.barrier()
            carve_AC()
            consts_C(l)
            if l == 0:
                consts_A(1)
            load_wdn(P, c, wdn2[l])
            for ti in range(12):
                if dbg and (ti % 4) == 2:
                    continue
                phaseC_tile(l, ti, last=(l == 1 or dbg))
                if l == 0 and not dbg:
                    phaseA_tile(1, ti, wdn1[1], (wdn2[0] if ti < 11 else None))
            if dbg:
                P.dma(dsend2[:, :], send2[:, :], ["send2"], ["dsend2"], "dbg4")
        P.final_wait()
        P.emit()
    return nc


def _rel_bucket(rel):
    nb, me = 16, 8
    rel = np.asarray(rel, np.int64)
    ret = np.where(rel > 0, nb, 0)
    n = np.abs(rel)
    nf = np.maximum(n, 1).astype(np.float32)
    large = me + (np.log(nf / np.float32(me)) / np.float32(math.log(1024 / me)) * np.float32(nb - me)).astype(np.int32)
    large = np.minimum(large, nb - 1)
    return ret + np.where(n < me, n, large)


def _consts():
    cst = {}
    cst["ident"] = np.eye(128, dtype=np.float32).astype(NPBF)
    cst["identf"] = np.eye(128, dtype=np.float32)
    cst["J"] = np.eye(128, dtype=np.float32)[::-1].copy().astype(NPBF)
    cst["bd32"] = np.kron(np.eye(4, dtype=np.float32), np.ones((32, 32), np.float32)).astype(NPBF)
    cst["bd64"] = np.kron(np.eye(2, dtype=np.float32), np.ones((64, 64), np.float32)).astype(NPBF)
    e15 = np.zeros((32, 128), np.float32); e15[15] = 1
    e31 = np.zeros((32, 128), np.float32); e31[31] = 1
    cst["e15"], cst["e31"] = e15, e31
    m = np.arange(3328)
    bk = _rel_bucket(CC - m)
    oh = (bk[None, :] == np.arange(32)[:, None]).astype(np.float32)
    oh[:, 3327:] = 0
    cst["ohrev"] = oh
    for (r, dmin, dmax, w) in DIL:
        rel = np.arange(128)[:, None] - np.arange(w)[None, :] + dmax
        cst["m%d" % r] = ((rel % r == 0) & (np.abs(rel) <= 64 * r)).astype(np.float32).astype(NPBF)
    return cst


def _col(v):
    v = np.asarray(v, np.float32)
    return np.ascontiguousarray(v.reshape(v.shape[0], -1, 128).transpose(0, 2, 1))


_PROGS = {}


def kernel(x_prompt, x_sample, ffn1_norm, ffn1_w_gu, ffn1_w_down, mix_norm, w_in, conv_w, diff_q_norm, diff_k_norm,
           lambda_q1, lambda_k1, lambda_q2, lambda_k2, diff_sub_norm, dil_q_norm, dil_k_norm, w_out, ffn2_norm,
           ffn2_w_gu, ffn2_w_down, final_norm, rel_bias):
    f = lambda a: np.ascontiguousarray(np.asarray(a, np.float32))
    cst = _consts()
    xs = [f(x_prompt)[0], f(x_sample)[0], f(x_sample)[1]]
    cores = list(range(NCORE))
    rel_bias = f(rel_bias)
    gq = np.stack([np.tile(f(diff_q_norm), (1, 4)), np.tile(f(diff_k_norm), (1, 4)), np.tile(f(dil_q_norm), (1, 2)),
                   np.tile(f(dil_k_norm), (1, 2))], 2)
    cw = f(conv_w)
    cwl = np.ascontiguousarray(cw.reshape(2, 3, 3, 128).transpose(0, 3, 2, 1).reshape(2, 128, 9))
    lamv = np.concatenate([f(lambda_q1), f(lambda_k1), f(lambda_q2), f(lambda_k2)], 1)[:, None, :]
    shared = {
        "n1": _col(ffn1_norm), "nm": _col(mix_norm), "n2": _col(ffn2_norm), "nf": f(final_norm)[:, None, :],
        "gq": np.ascontiguousarray(gq.astype(np.float32)), "conv_w": cwl,
        "lamv": np.ascontiguousarray(lamv), "gsub": f(diff_sub_norm)[:, None, :],
    }
    shared.update(cst)
    maps = []
    p = np.arange(128)
    for c in cores:
        xc = np.concatenate([xs[s][c * SL:(c + 1) * SL] for s in range(3)], 0)
        idx = np.zeros((128, 4), np.int32)
        idx[:, 0] = c * RPD + p
        idx[:, 1] = c * 2 * RPD + p
        idx[:, 2] = c * R2D + p
        idx[:, 3] = c * 7 + 6
        for s in range(3):
            if c > 0:
                idx[s, 3] = (c - 1) * 7 + s * 2 + 1
            if c < NCORE - 1:
                idx[3 + s, 3] = (c + 1) * 7 + s * 2
        cols = []
        for k in range(3):
            u = 3 * c + k
            cols.append((u % 8) // 2)
        for k in range(3):
            v = 3 * c + k
            cols.append(4 + (v % 6))
        m = dict(shared)
        for nm_, arr in (("w_gu1", ffn1_w_gu), ("w_dn1", ffn1_w_down), ("w_gu2", ffn2_w_gu), ("w_dn2", ffn2_w_down),
                         ("w_in", w_in), ("w_out", w_out)):
            a = np.asarray(arr, np.float32)
            kk = a.shape[1] // 8
            m[nm_] = np.ascontiguousarray(a[:, c * kk:(c + 1) * kk, :])
        m.update({"x": xc, "idx": idx, "tabc": np.ascontiguousarray(rel_bias[:, cols])})
        maps.append(m)
    if "F" not in _PROGS:
        _PROGS["F"] = build_fused()
    res = run_bass_kernel_spmd(_PROGS["F"], maps, core_ids=cores).results
    y = [np.asarray(r["y"]) for r in res]
    ys = [np.concatenate([y[c][s * SL:(s + 1) * SL] for c in cores], 0) for s in range(3)]
    y_prompt = ys[0][None].astype(np.float32)
    y_sample = np.stack([ys[1], ys[2]], 0).astype(np.float32)
    return (y_prompt, y_sample)
```

```python
import math
import numpy as np
import ml_dtypes
import concourse.bass as bass
import concourse.mybir as mybir
from concourse.bass_utils import run_bass_kernel_spmd

F32 = mybir.dt.float32
I32 = mybir.dt.int32
BF16 = mybir.dt.bfloat16
AF = mybir.ActivationFunctionType
ALU = mybir.AluOpType
NPBF = ml_dtypes.bfloat16

NCORE = 8
S = 16384
SL = S // NCORE
NT = 3 * SL
D = 1024
DFF = 2816
EPS = 1e-6
TW = 3327
HW_ = 3200
CC = 1663


class Op:
    __slots__ = ("fn", "waits", "signal", "dkey", "dcnt", "dinc", "epoch")

    def __init__(self, fn):
        self.fn = fn
        self.waits = []
        self.signal = False
        self.dkey = None
        self.dcnt = 0
        self.dinc = 16
        self.epoch = 0


class Prog:
    ENG = ("pe", "act", "dve", "pool", "sp")

    def __init__(self, nc):
        self.nc = nc
        self.ops = {e: [] for e in self.ENG}
        self.res = {}
        self.dcount = {}
        self.waited = {e: {} for e in self.ENG}
        self.epoch = 0

    def _st(self, k):
        st = self.res.get(k)
        if st is None:
            st = {"w": {}, "r": {}, "wd": {}, "rd": {}}
            self.res[k] = st
        return st

    def op(self, eng, fn, reads=(), writes=(), dkey=None, dinc=16):
        lst = self.ops[eng]
        idx = len(lst)
        o = Op(fn)
        o.epoch = self.epoch
        edeps = {}
        ddeps = {}

        def add(dst, src):
            for k, v in src.items():
                if dst.get(k, -1) < v:
                    dst[k] = v

        for k in reads:
            st = self._st(k)
            add(edeps, st["w"])
            add(ddeps, st["wd"])
        for k in writes:
            st = self._st(k)
            add(edeps, st["w"])
            add(edeps, st["r"])
            add(ddeps, st["wd"])
            add(ddeps, st["rd"])
        wd = self.waited[eng]
        for e2, i2 in edeps.items():
            if e2 == eng and eng in ("pe", "sp"):
                continue
            if wd.get(("e", e2), -1) >= i2:
                continue
            wd[("e", e2)] = i2
            self.ops[e2][i2].signal = True
            o.waits.append(("e", e2, i2))
        for dk, c in ddeps.items():
            c = self.dcount[dk]
            if wd.get(("d", dk), -1) >= c:
                continue
            wd[("d", dk)] = c
            o.waits.append(("d", dk, c))
        if dkey is not None:
            c = self.dcount.get(dkey, 0) + dinc
            o.dinc = dinc
            self.dcount[dkey] = c
            o.dkey = dkey
            o.dcnt = c
            for k in reads:
                self._st(k)["rd"][dkey] = c
            for k in writes:
                self._st(k)["wd"][dkey] = c
        else:
            for k in reads:
                self._st(k)["r"][eng] = idx
            for k in writes:
                self._st(k)["w"][eng] = idx
        lst.append(o)
        return o

    def dma(self, out, in_, reads, writes, dkey, eng="sp"):
        return self.op(eng, lambda e: e.dma_start(out=out, in_=in_), reads, writes, dkey=dkey)

    def _last_real(self, e):
        lst = self.ops[e]
        for i in range(len(lst) - 1, -1, -1):
            if lst[i].fn is not None and lst[i].dkey is None:
                return i
        return None

    def barrier(self):
        lasts = {}
        for e in ("pe", "act", "dve"):
            i = self._last_real(e)
            if i is not None:
                lasts[e] = i
                self.ops[e][i].signal = True
        dk = dict(self.dcount)
        for e in self.ENG:
            o = Op(None)
            o.epoch = self.epoch
            for e2, i2 in lasts.items():
                if e2 != e:
                    o.waits.append(("e", e2, i2))
                    self.waited[e][("e", e2)] = i2
            for k, c in dk.items():
                o.waits.append(("d", k, c))
                self.waited[e][("d", k)] = c
            self.ops[e].append(o)
        self.epoch += 1

    def final_wait(self):
        o = Op(None)
        o.epoch = self.epoch
        for e in ("pe", "act", "dve"):
            i = self._last_real(e)
            if i is not None:
                self.ops[e][i].signal = True
                o.waits.append(("e", e, i))
        for dk, c in self.dcount.items():
            o.waits.append(("d", dk, c))
        self.ops["sp"].append(o)

    def emit(self):
        nc = self.nc
        import contextlib
        with contextlib.ExitStack() as es:
            esem = {}
            for e in self.ENG:
                for o in self.ops[e]:
                    if o.signal and o.dkey is None and (e, o.epoch) not in esem:
                        esem[(e, o.epoch)] = es.enter_context(nc.semaphore("s_%s_%d" % (e, o.epoch)))
            dsem = {dk: es.enter_context(nc.semaphore("d_%d" % i)) for i, dk in enumerate(self.dcount)}
            cnt = {}
            for e in self.ENG:
                c = 0
                ep = -1
                arr = []
                for o in self.ops[e]:
                    if o.epoch != ep:
                        ep = o.epoch
                        c = 0
                    if o.signal and o.dkey is None:
                        c += 1
                    arr.append(c)
                cnt[e] = arr
            block = es.enter_context(nc.Block())

            def run(eng_name, eng):
                for o in self.ops[eng_name]:
                    for w in o.waits:
                        if w[0] == "e":
                            eng.wait_ge(esem[(w[1], self.ops[w[1]][w[2]].epoch)], cnt[w[1]][w[2]])
                        else:
                            eng.wait_ge(dsem[w[1]], w[2])
                    if o.fn is None:
                        continue
                    ins = o.fn(eng)
                    if o.dkey is not None:
                        if o.dinc == 16:
                            ins.then_inc(dsem[o.dkey], 16)
                        else:
                            ins.then_inc(dsem[o.dkey])
                    elif o.signal:
                        ins.then_inc(esem[(eng_name, o.epoch)], 1)

            @block.tensor
            def _(e):
                run("pe", e)

            @block.scalar
            def _(e):
                run("act", e)

            @block.vector
            def _(e):
                run("dve", e)

            @block.gpsimd
            def _(e):
                run("pool", e)

            @block.sync
            def _(e):
                run("sp", e)


class Ctx:
    pass


def load_cols(P, c, name, dram_vec_ap, ncols_chunks):
    pass


def norm_T(P, c, xt, gcols, tag):
    nc = P.nc
    for j in range(4):
        P.op("act", lambda e, j=j: e.activation(out=c.junk[:, :], in_=xt[:, j, :], func=AF.Square,
                                                accum_out=c.ss[:, j:j + 1]),
             reads=["xt"], writes=["junk", "ss"])
    P.op("act", lambda e: e.activation(out=c.rs[:, 0:4], in_=c.ss[:, 0:4], func=AF.Sqrt,
                                       bias=c.epsc[:, 0:1], scale=1.0 / D),
         reads=["ss"], writes=["rs"])
    P.op("dve", lambda e: e.reciprocal(out=c.rs[:, 0:4], in_=c.rs[:, 0:4]), reads=["rs"], writes=["rs"])
    for j in range(4):
        P.op("dve", lambda e, j=j: e.tensor_scalar(out=c.h[:, j, :], in0=xt[:, j, :], scalar1=c.rs[:, j:j + 1],
                                                   scalar2=None, op0=ALU.mult),
             reads=["xt", "rs"], writes=["h"])
    for kc in range(8):
        pst = c.pst[kc % 2]
        pk = "pst%d" % (kc % 2)
        for j in range(4):
            P.op("pe", lambda e, j=j, kc=kc, pst=pst: e.transpose(out=pst[:, j * 128:(j + 1) * 128],
                                                                  in_=c.h[:, j, kc * 128:(kc + 1) * 128],
                                                                  identity=c.ident[:, :]),
                 reads=["h"], writes=[pk])
        if kc % 2 == 0:
            P.op("act", lambda e, kc=kc, pst=pst: e.activation(out=c.hT[:, kc, :], in_=pst[:, 0:512], func=AF.Copy,
                                                               scale=gcols[:, kc:kc + 1]),
                 reads=[pk], writes=["hT"])
        else:
            P.op("dve", lambda e, kc=kc, pst=pst: e.tensor_scalar(out=c.hT[:, kc, :], in0=pst[:, 0:512],
                                                                  scalar1=gcols[:, kc:kc + 1], scalar2=None,
                                                                  op0=ALU.mult),
                 reads=[pk], writes=["hT"])


def ffn(P, c, xt, wgu, wdn_sb, cnt):
    wgu_v = wgu.rearrange("(kc p) n -> p kc n", p=128)
    for j in range(22):
        slot = cnt[0] % 2
        cnt[0] += 1
        wk = "wup%d" % slot
        wt = c.wup[slot]
        P.dma(wt[:, :, 0:128], wgu_v[:, :, j * 128:(j + 1) * 128], reads=[], writes=[wk], dkey=wk, eng="pool")
        P.dma(wt[:, :, 128:256], wgu_v[:, :, DFF + j * 128:DFF + (j + 1) * 128], reads=[], writes=[wk], dkey=wk,
              eng="pool")
        pg, pu = c.ps[0 + 2 * (j % 2)], c.ps[1 + 2 * (j % 2)]
        kg, ku = "ps%d" % (2 * (j % 2)), "ps%d" % (1 + 2 * (j % 2))
        for kc in range(8):
            P.op("pe", lambda e, kc=kc, wt=wt, pg=pg: e.matmul(pg[:, :], lhsT=wt[:, kc, 0:128], rhs=c.hT[:, kc, :],
                                                               start=(kc == 0), stop=(kc == 7)),
                 reads=[wk, "hT"], writes=[kg])
        for kc in range(8):
            P.op("pe", lambda e, kc=kc, wt=wt, pu=pu: e.matmul(pu[:, :], lhsT=wt[:, kc, 128:256], rhs=c.hT[:, kc, :],
                                                               start=(kc == 0), stop=(kc == 7)),
                 reads=[wk, "hT"], writes=[ku])
        sg = c.sg[j % 2]
        sk = "sg%d" % (j % 2)
        P.op("act", lambda e, pg=pg, sg=sg: e.activation(out=sg[:, :], in_=pg[:, :], func=AF.Silu),
             reads=[kg], writes=[sk])
        P.op("dve", lambda e, j=j, pu=pu, sg=sg: e.tensor_tensor(out=c.actT[:, j, :], in0=sg[:, :], in1=pu[:, :],
                                                                 op=ALU.mult),
             reads=[sk, ku], writes=["actT"])
    n = 0
    for j in range(4):
        for hf in range(2):
            pd = c.ps[4 + n % 2]
            pk = "ps%d" % (4 + n % 2)
            n += 1
            for jj in range(22):
                P.op("pe", lambda e, j=j, hf=hf, jj=jj, pd=pd: e.matmul(
                    pd[:, :], lhsT=c.actT[:, jj, j * 128:(j + 1) * 128], rhs=wdn_sb[:, jj, hf * 512:(hf + 1) * 512],
                    start=(jj == 0), stop=(jj == 21)), reads=["actT", "wdn"], writes=[pk])
            P.op("dve", lambda e, j=j, hf=hf, pd=pd: e.scalar_tensor_tensor(
                out=xt[:, j, hf * 512:(hf + 1) * 512], in0=pd[:, :], scalar=0.5, in1=xt[:, j, hf * 512:(hf + 1) * 512],
                op0=ALU.mult, op1=ALU.add), reads=[pk, "xt"], writes=["xt"])


def load_wdn(P, c, wdn):
    v = wdn.rearrange("(kc p) n -> p kc n", p=128)
    for q in range(2):
        P.dma(c.wdn[:, q * 11:(q + 1) * 11, :], v[:, q * 11:(q + 1) * 11, :], reads=[], writes=["wdn"], dkey="wdn",
              eng="pool")


def gcol_load(P, dst, vec_ap, nch, key):
    P.dma(dst[:, 0:nch], vec_ap[:, :], reads=[], writes=[key], dkey=key)


DIL = ((1, -128, 512, 1152), (4, -256, 640, 1408), (16, -1024, 1408, 2944))
RPD = 1024
SRC1 = 8 * RPD * 2048
R2D = 3072
SRC2 = 8 * R2D * 256
QD0, KD0, QL0, KL0 = 0, 128, 256, 448
VD0, VL0 = 1280, 1664


class Arena:
    def __init__(self, ap):
        self.a = ap
        self.off = 0

    def reset(self):
        self.off = 0

    def take(self, shape, dt):
        n = 1
        for v in shape[1:]:
            n *= v
        ne = n * 2 if dt in (F32, I32) else n
        ne = (ne + 1) // 2 * 2
        v = self.a[0:shape[0], self.off:self.off + ne]
        self.off += ne
        assert self.off <= self.a.shape[1], ("arena overflow", self.off)
        if dt in (F32, I32):
            v = v.bitcast(dt)
        if len(shape) == 3:
            v = v.rearrange("p (a b) -> p a b", b=shape[2])
        return v


def build_fused(dbg=False):
    import contextlib
    nc = bass.Bass("TRN2", target_bir_lowering=False)
    P = Prog(nc)
    c = Ctx()

    def din(name, shape, dt=F32):
        return nc.dram_tensor(name, shape, dt, kind="ExternalInput").ap()

    def dscr(name, shape, dt=F32):
        return nc.dram_tensor(name, shape, dt).ap()

    xin = din("x", [NT, D])
    yo = nc.dram_tensor("y", [NT, D], F32, kind="ExternalOutput").ap()
    idxd = din("idx", [128, 4], I32)
    tabc = din("tabc", [32, 6])
    n1 = din("n1", [2, 128, 8]); nm = din("nm", [2, 128, 8]); n2 = din("n2", [2, 128, 8])
    nf = din("nf", [2, 1, D])
    wshapes = {"w_gu1": (D, 2 * DFF), "w_dn1": (DFF, D), "w_gu2": (D, 2 * DFF), "w_dn2": (DFF, D),
               "w_in": (D, 3072), "w_out": (D, D)}
    wsl = {k: din(k, [2, kk // 8, nn]) for k, (kk, nn) in wshapes.items()}
    wloc = {k: [dscr("%s_loc%d" % (k, l), [kk // 8, nn], BF16) for l in range(2)] for k, (kk, nn) in wshapes.items()}
    wful = {k: [dscr("%s_bf%d" % (k, l), [kk, nn], BF16) for l in range(2)] for k, (kk, nn) in wshapes.items()}
    wgu1, wdn1, wgu2, wdn2, win, wout = (wful[k] for k in ("w_gu1", "w_dn1", "w_gu2", "w_dn2", "w_in", "w_out"))
    gqd = din("gq", [2, 128, 4]); convw = din("conv_w", [2, 128, 9])
    lamv = din("lamv", [2, 1, 128]); gsub = din("gsub", [2, 1, 64])
    ident_d = din("ident", [128, 128], BF16); idf = din("identf", [128, 128])
    Jd = din("J", [128, 128], BF16)
    bd32 = din("bd32", [128, 128], BF16); bd64 = din("bd64", [128, 128], BF16)
    e15 = din("e15", [32, 128]); e31 = din("e31", [32, 128]); ohrev = din("ohrev", [32, 3328])
    mds = [din("m%d" % r, [128, w], BF16) for (r, _, _, w) in DIL]

    x1s_l = [dscr("x1s%d" % l, [NT, D]) for l in range(2)]
    zs_l = [dscr("zs%d" % l, [384, NT]) for l in range(2)]
    bgs_l = [dscr("bgs%d" % l, [384, NT]) for l in range(2)]
    send1 = dscr("send1", [8 * RPD, 2048], BF16); G1 = dscr("G1", [64 * RPD, 2048], BF16)
    zb = dscr("zb", [7, 384]); G3 = dscr("G3", [56, 384])
    send2 = dscr("send2", [8 * R2D, 256]); G2 = dscr("G2", [64 * R2D, 256])
    T = dscr("Ttab", [6, 3328]); cbd = dscr("cbd", [128, 12])
    send1v = send1.rearrange("r (h e) -> (r h) e", h=2)
    G1v = G1.rearrange("r (h e) -> (r h) e", h=2).rearrange("r (k d) -> r k d", d=64)
    G2v = G2.rearrange("r (j d) -> r j d", d=64)

    if dbg == 2:
        dsend1 = nc.dram_tensor("dsend1", [8 * RPD, 2048], BF16, kind="ExternalOutput").ap()
        dqT = nc.dram_tensor("dqT", [128, 16384], BF16, kind="ExternalOutput").ap()
        dkT = nc.dram_tensor("dkT", [128, 16384], BF16, kind="ExternalOutput").ap()
        dV = nc.dram_tensor("dV", [128, 128 * 65], BF16, kind="ExternalOutput").ap()
    if dbg:
        dsend2 = nc.dram_tensor("dsend2", [8 * R2D, 256], F32, kind="ExternalOutput").ap()
    with contextlib.ExitStack() as es:
        arena_t = es.enter_context(nc.sbuf_tensor("arena", [128, 106200], BF16))
        idx = es.enter_context(nc.sbuf_tensor("idxs", [128, 4], I32))
        banks = [es.enter_context(nc.psum_tensor("bk%d" % i, [128, 512], F32)) for i in range(6)]
        pstb = [es.enter_context(nc.psum_tensor("pstb%d" % i, [128, 1024], BF16)) for i in range(2)]
        A = Arena(arena_t[:, :])
        P.dma(idx[:, :], idxd[:, :], [], ["idx"], "idx")
        c.ps = banks
        c.pst = pstb
        cnt = [0]
        wcnt = [0]
        uq = [0]

        grp = ["cP"]

        def ukey(prefix):
            if prefix == "c":
                return grp[0]
            return prefix

        for k in ("w_gu1", "w_dn1", "w_in", "w_out", "w_gu2", "w_dn2"):
            for l in range(2):
                kk = wshapes[k][0] // 8
                hk = kk // 2
                for q in range(2):
                    P.dma(wloc[k][l][q * hk:(q + 1) * hk, :], wsl[k][l, q * hk:(q + 1) * hk, :], [],
                          ["wloc"], "wc", eng="pool")
        for k in ("w_gu1", "w_dn1", "w_in", "w_out", "w_gu2", "w_dn2"):
            for l in range(2):
                P.op("pool", lambda e, k=k, l=l: e.collective_compute(
                    "AllGather", ALU.bypass, replica_groups=[list(range(8))],
                    ins=[wloc[k][l].tensor.ap().opt()], outs=[wful[k][l].tensor.ap().opt()]),
                     ["wloc"], ["wful"], dkey="agw", dinc=1)
        tab = A.take([32, 6], F32); oh = A.take([32, 3328], F32)
        e15s = A.take([32, 128], F32); e31s = A.take([32, 128], F32)
        tsb = A.take([6, 3328], F32); cb = A.take([128, 12], F32); zrow = A.take([1, 384], F32)
        P.dma(tab[:, :], tabc[:, :], [], ["tab"], ukey("c"))
        P.dma(oh[:, :], ohrev[:, :], [], ["oh"], ukey("c"))
        P.dma(e15s[:, :], e15[:, :], [], ["e15"], ukey("c"))
        P.dma(e31s[:, :], e31[:, :], [], ["e31"], ukey("c"))
        ptb = banks[0]
        for i in range(7):
            n = 512 if i < 6 else 256
            P.op("pe", lambda e, i=i, n=n: e.matmul(ptb[0:6, 0:n], lhsT=tab[:, 0:6], rhs=oh[:, i * 512:i * 512 + n],
                                                   start=True, stop=True), ["tab", "oh"], ["ptb"])
            P.op("act", lambda e, i=i, n=n: e.activation(out=tsb[0:6, i * 512:i * 512 + n], in_=ptb[0:6, 0:n],
                                                         func=AF.Copy), ["ptb"], ["tsb"])
        P.dma(T[:, :], tsb[:, :], ["tsb"], ["T"], "T")
        P.op("pe", lambda e: e.matmul(ptb[:, 0:6], lhsT=e15s[:, :], rhs=tab[:, 0:6], start=True, stop=True),
             ["e15", "tab"], ["ptb"])
        P.op("act", lambda e: e.activation(out=cb[:, 0:6], in_=ptb[:, 0:6], func=AF.Copy), ["ptb"], ["cb"])
        P.op("pe", lambda e: e.matmul(ptb[:, 0:6], lhsT=e31s[:, :], rhs=tab[:, 0:6], start=True, stop=True),
             ["e31", "tab"], ["ptb"])
        P.op("act", lambda e: e.activation(out=cb[:, 6:12], in_=ptb[:, 0:6], func=AF.Copy), ["ptb"], ["cb"])
        P.dma(cbd[:, :], cb[:, :], ["cb"], ["cbd"], "cbd")
        P.op("dve", lambda e: e.memset(zrow[:, :], 0.0), [], ["zrow"])
        P.dma(zb[6:7, :], zrow[:, :], ["zrow"], ["zb"], "zb6")

        def carve_AC():
            A.reset()
            c.xt = A.take([128, 4, 1024], F32)
            c.h = A.take([128, 4, 1024], BF16)
            c.hT = A.take([128, 8, 512], BF16)
            c.actT = A.take([128, 22, 512], BF16)
            c.junk = A.take([128, 1024], BF16)
            c.ss = A.take([128, 8], F32); c.rs = A.take([128, 8], F32); c.epsc = A.take([128, 2], F32)
            c.ident = A.take([128, 128], BF16)
            c.wup = [A.take([128, 8, 256], BF16) for _ in range(2)]
            c.wdn = A.take([128, 22, 1024], BF16)
            c.sg = [A.take([128, 512], F32) for _ in range(2)]
            c.gcols = A.take([128, 3, 8], F32)
            c.yT = A.take([128, 8, 512], BF16); c.ytm = A.take([128, 4, 640], BF16)
            odf = A.take([128, 2048], F32)
            c.od = odf.rearrange("p (u n) -> p u n", n=256)
            c.usb_view = odf[:, 0:1536].rearrange("p (a b) -> p a b", b=512)
            olf = A.take([128, 1536], F32)
            c.ol = olf.rearrange("p (u n) -> p u n", n=256)
            c.ob = A.take([128, 256], F32); c.wout = A.take([128, 8, 1024], BF16)
            c.zh = A.take([128, 3, 514], F32); c.bg = A.take([128, 3, 512], F32)
            c.cw = A.take([128, 3, 3], F32); c.ca = A.take([128, 512], F32)
            c.nlam = A.take([128, 2], F32); c.gsub = A.take([128, 64], F32); c.nf = A.take([128, 1024], F32)
            c.lamv = A.take([128, 128], F32); c.lt = A.take([128, 4], F32)
            c.hal = A.take([128, 384], F32); c.halT = A.take([128, 3, 8], F32); c.identf = A.take([128, 128], F32)
            c.usb = c.usb_view; c.zt = c.zh[:, :, 0:512]; c.bgt = c.bg
            c.k = {"usb": "od", "zt": "zh", "bgt": "bg"}
            c.wch = [A.take([128, 8, 128], BF16) for _ in range(2)]
            c.wv = A.take([128, 8, 640], BF16)
            c.sq = [A.take([128, 512], BF16) for _ in range(2)]
            c.rq = [A.take([128, 512], F32) for _ in range(2)]
            c.qk = [A.take([128, 512], BF16) for _ in range(2)]
            c.vt = A.take([128, 4, 640], BF16)
            c.gq = A.take([128, 4], F32); c.bd32 = A.take([128, 128], BF16); c.bd64 = A.take([128, 128], BF16)

        def consts_A(l):
            grp[0] = "cA"
            P.dma(c.ident[:, :], ident_d[:, :], [], ["ident"], ukey("c"))
            P.dma(c.gcols[:, 0, :], n1[l], [], ["gcols"], ukey("c"))
            P.dma(c.gcols[:, 1, :], nm[l], [], ["gcols"], ukey("c"))
            P.dma(c.gq[:, :], gqd[l], [], ["gq"], ukey("c"))
            P.dma(c.bd32[:, :], bd32[:, :], [], ["bd"], ukey("c"))
            P.dma(c.bd64[:, :], bd64[:, :], [], ["bd"], ukey("c"))
            P.op("dve", lambda e: e.memset(c.epsc[:, :], EPS), [], ["epsc"])
            P.op("dve", lambda e: e.tensor_scalar(out=c.gq[:, 0:1], in0=c.gq[:, 0:1], scalar1=32 ** -0.5, scalar2=None,
                                                  op0=ALU.mult), ["gq"], ["gq"])
            P.op("dve", lambda e: e.tensor_scalar(out=c.gq[:, 2:3], in0=c.gq[:, 2:3], scalar1=64 ** -0.5, scalar2=None,
                                                  op0=ALU.mult), ["gq"], ["gq"])
            wv_v = win[l].rearrange("(kc p) n -> p kc n", p=128)
            P.dma(c.wv[:, :, 0:256], wv_v[:, :, 1664:1920], [], ["wv"], "wv", eng="pool")
            P.dma(c.wv[:, :, 256:640], wv_v[:, :, 2688:3072], [], ["wv"], "wv", eng="pool")

        def consts_C(l):
            lam_init = 0.8 - 0.6 * math.exp(-0.3 * l)
            grp[0] = "cC"
            P.dma(c.ident[:, :], ident_d[:, :], [], ["ident"], ukey("c"))
            P.dma(c.identf[:, :], idf[:, :], [], ["identf"], ukey("c"))
            P.dma(c.gcols[:, 2, :], n2[l], [], ["gcols"], ukey("c"))
            P.dma(c.cw[:, :, :], convw[l].rearrange("p (c k) -> p c k", k=3), [], ["cw"], ukey("c"))
            P.dma(c.lamv[:, :], lamv[l].partition_broadcast(128), [], ["lamv"], ukey("c"))
            P.dma(c.gsub[:, :], gsub[l].partition_broadcast(128), [], ["gsub"], ukey("c"))
            P.dma(c.nf[:, :], nf[l].partition_broadcast(128), [], ["nf"], ukey("c"))
            P.op("dve", lambda e: e.memset(c.epsc[:, :], EPS), [], ["epsc"])
            for q in range(2):
                P.op("dve", lambda e, q=q: e.tensor_tensor(out=c.lamv[:, q * 64:q * 64 + 32],
                                                           in0=c.lamv[:, q * 64:q * 64 + 32],
                                                           in1=c.lamv[:, q * 64 + 32:q * 64 + 64], op=ALU.mult),
                     ["lamv"], ["lamv"])
                P.op("dve", lambda e, q=q: e.tensor_reduce(out=c.lt[:, q:q + 1], in_=c.lamv[:, q * 64:q * 64 + 32],
                                                           axis=mybir.AxisListType.X, op=ALU.add), ["lamv"], ["lt"])
            P.op("act", lambda e: e.activation(out=c.lt[:, 2:4], in_=c.lt[:, 0:2], func=AF.Exp), ["lt"], ["lt"])
            P.op("dve", lambda e: e.tensor_tensor(out=c.nlam[:, 0:1], in0=c.lt[:, 3:4], in1=c.lt[:, 2:3],
                                                  op=ALU.subtract), ["lt"], ["nlam"])
            P.op("dve", lambda e: e.tensor_scalar(out=c.nlam[:, 0:1], in0=c.nlam[:, 0:1], scalar1=-lam_init,
                                                  scalar2=None, op0=ALU.add), ["nlam"], ["nlam"])
            P.op("dve", lambda e: e.tensor_scalar(out=c.gsub[:, :], in0=c.gsub[:, :], scalar1=1.0 - lam_init,
                                                  scalar2=None, op0=ALU.mult), ["gsub"], ["gsub"])
            wo_v = wout[l].rearrange("(kc p) n -> p kc n", p=128)
            for q in range(2):
                P.dma(c.wout[:, q * 4:(q + 1) * 4, :], wo_v[:, q * 4:(q + 1) * 4, :], [], ["wout"], "wout", eng="pool")
            P.op("pool", lambda e: e.indirect_dma_start(
                out=c.hal[:, :], out_offset=None, in_=G3[:, :],
                in_offset=bass.IndirectOffsetOnAxis(ap=idx[:, 3:4], axis=0)), ["G3", "idx"], ["hal"], dkey="hal")
            for ch in range(3):
                P.op("pe", lambda e, ch=ch: e.transpose(out=banks[0][:, ch * 8:(ch + 1) * 8],
                                                        in_=c.hal[0:8, ch * 128:(ch + 1) * 128],
                                                        identity=c.identf[0:8, 0:8]), ["hal", "identf"], ["ps0"])
            P.op("act", lambda e: e.activation(out=c.halT[:, :, :],
                                               in_=banks[0][:, 0:24].rearrange("p (c k) -> p c k", k=8),
                                               func=AF.Copy), ["ps0"], ["halT"])

        def phaseA_tile(l, ti, wdn_reload_first, wdn_reload_after):
            x1s, zs, bgs = x1s_l[l], zs_l[l], bgs_l[l]
            g0 = ti * 512
            s, tq = ti // 4, ti % 4
            if wdn_reload_first is not None:
                load_wdn(P, c, wdn_reload_first)
            norm_T(P, c, c.xt, c.gcols[:, 0, :], "n1")
            ffn(P, c, c.xt, wgu1[l], c.wdn, cnt)
            if wdn_reload_after is not None:
                load_wdn(P, c, wdn_reload_after)
            P.dma(x1s[g0:g0 + 512, :].rearrange("(j p) n -> p j n", p=128), c.xt[:, :, :], ["xt"], ["x1s"], "xo")
            norm_T(P, c, c.xt, c.gcols[:, 1, :], "nm")
            win_v = win[l].rearrange("(kc p) n -> p kc n", p=128)
            chunks = [(0, "u", 0), (128, "u", 1), (256, "u", 2), (768, "cg", 0), (896, "cg", 1), (1024, "cg", 2),
                      (384, "bg", 0), (512, "bg", 1), (640, "bg", 2),
                      (1152, "dq", 0), (1280, "dq", 1), (1408, "dk", 0), (1536, "dk", 1),
                      (1920, "lq", 0), (2048, "lq", 1), (2176, "lq", 2), (2304, "lk", 0), (2432, "lk", 1),
                      (2560, "lk", 2)]
            n = 0
            for (col, kind, ci) in chunks:
                slot = wcnt[0] % 2
                wcnt[0] += 1
                wk = "wch%d" % slot
                wt = c.wch[slot]
                P.dma(wt[:, :, :], win_v[:, :, col:col + 128], [], [wk], wk, eng="pool")
                pp = c.ps[n % 2]
                pk = "ps%d" % (n % 2)
                n += 1
                for kc in range(8):
                    P.op("pe", lambda e, kc=kc, wt=wt, pp=pp: e.matmul(pp[:, :], lhsT=wt[:, kc, :], rhs=c.hT[:, kc, :],
                                                                       start=(kc == 0), stop=(kc == 7)),
                         [wk, "hT"], [pk])
                if kind == "u":
                    P.op("act", lambda e, ci=ci, pp=pp: e.activation(out=c.usb[:, ci, :], in_=pp[:, :], func=AF.Copy),
                         [pk], [c.k["usb"]])
                elif kind == "cg":
                    P.op("dve", lambda e, ci=ci, pp=pp: e.tensor_tensor(out=c.zt[:, ci, :], in0=pp[:, :],
                                                                        in1=c.usb[:, ci, :], op=ALU.mult),
                         [pk, c.k["usb"]], [c.k["zt"]])
                elif kind == "bg":
                    P.op("act", lambda e, ci=ci, pp=pp: e.activation(out=c.bgt[:, ci, :], in_=pp[:, :], func=AF.Copy),
                         [pk], [c.k["bgt"]])
                else:
                    is32 = kind in ("dq", "dk")
                    gi = {"dq": 0, "dk": 1, "lq": 2, "lk": 3}[kind]
                    b = n % 2
                    sq, rq, qk = c.sq[b], c.rq[b], c.qk[b]
                    pq = c.ps[2 + b]
                    pqk = "ps%d" % (2 + b)
                    P.op("act", lambda e, sq=sq, pp=pp: e.activation(out=sq[:, :], in_=pp[:, :], func=AF.Square),
                         [pk], ["sq%d" % b])
                    bd = c.bd32 if is32 else c.bd64
                    P.op("pe", lambda e, sq=sq, pq=pq, bd=bd: e.matmul(pq[:, :], lhsT=bd[:, :], rhs=sq[:, :], start=True,
                                                                       stop=True), ["sq%d" % b, "bd"], [pqk])
                    P.op("act", lambda e, rq=rq, pq=pq, is32=is32: e.activation(
                        out=rq[:, :], in_=pq[:, :], func=AF.Sqrt, bias=c.epsc[:, 0:1],
                        scale=1.0 / (32 if is32 else 64)), [pqk, "epsc"], ["rq%d" % b])
                    P.op("dve", lambda e, rq=rq: e.reciprocal(out=rq[:, :], in_=rq[:, :]), ["rq%d" % b], ["rq%d" % b])
                    P.op("dve", lambda e, rq=rq, qk=qk, pp=pp, gi=gi: e.scalar_tensor_tensor(
                        out=qk[:, :], in0=pp[:, :], scalar=c.gq[:, gi:gi + 1], in1=rq[:, :], op0=ALU.mult,
                        op1=ALU.mult), [pk, "gq", "rq%d" % b], ["qk%d" % b])
                    if is32:
                        sec = QD0 if kind == "dq" else KD0
                        for g in range(4):
                            u = s * 8 + ci * 4 + g
                            dest, sl = u // 3, u % 3
                            r0 = dest * RPD + sec + sl * 32
                            P.dma(send1[r0:r0 + 32, tq * 512:(tq + 1) * 512], qk[32 * g:32 * g + 32, :],
                                  ["qk%d" % b], ["send1"], "qko%d" % b)
                    else:
                        sec = QL0 if kind == "lq" else KL0
                        for g in range(2):
                            v = s * 6 + ci * 2 + g
                            dest, sl = v // 3, v % 3
                            r0 = dest * RPD + sec + sl * 64
                            P.dma(send1[r0:r0 + 64, tq * 512:(tq + 1) * 512], qk[64 * g:64 * g + 64, :],
                                  ["qk%d" % b], ["send1"], "qko%d" % b)
            P.dma(zs[:, g0:g0 + 512].rearrange("(c p) n -> p c n", p=128), c.zt[:, :, :], [c.k["zt"]], ["zs"], "zo")
            P.dma(bgs[:, g0:g0 + 512].rearrange("(c p) n -> p c n", p=128), c.bgt[:, :, :], [c.k["bgt"]], ["bgs"],
                  "bgo")
            if tq == 0 or tq == 3:
                row = s * 2 + (0 if tq == 0 else 1)
                col = 0 if tq == 0 else 511
                dst = bass.AP(tensor=zb.tensor, offset=row * 384, ap=[[1, 128], [128, 3], [1, 1]])
                P.op("sp", lambda e, dst=dst, col=col: e.dma_start(out=dst, in_=c.zt[:, :, col:col + 1],
                                                                   allow_slow_non_contiguous=True),
                     [c.k["zt"]], ["zb"], dkey="zbo")
            for j in range(4):
                pa, pb = c.ps[4], c.ps[5]
                for kc in range(8):
                    P.op("pe", lambda e, j=j, kc=kc: e.matmul(pa[:, 0:256], lhsT=c.hT[:, kc, j * 128:(j + 1) * 128],
                                                              rhs=c.wv[:, kc, 0:256], start=(kc == 0), stop=(kc == 7)),
                         ["hT", "wv"], ["ps4"])
                for kc in range(8):
                    P.op("pe", lambda e, j=j, kc=kc: e.matmul(pb[:, 0:384], lhsT=c.hT[:, kc, j * 128:(j + 1) * 128],
                                                              rhs=c.wv[:, kc, 256:640], start=(kc == 0),
                                                              stop=(kc == 7)), ["hT", "wv"], ["ps5"])
                P.op("act", lambda e, j=j: e.activation(out=c.vt[:, j, 0:256], in_=pa[:, 0:256], func=AF.Copy),
                     ["ps4"], ["vt"])
                P.op("dve", lambda e, j=j: e.tensor_copy(out=c.vt[:, j, 256:640], in_=pb[:, 0:384]), ["ps5"], ["vt"])
            for hc in range(8):
                u = s * 8 + hc
                dest, sl = u // 3, u % 3
                h = hc // 2
                r0 = dest * 2 * RPD + VD0 + sl * 128
                P.dma(send1v[r0:r0 + 128, tq * 256:(tq + 1) * 256].rearrange("p (j d) -> p j d", d=64),
                      c.vt[:, :, h * 64:(h + 1) * 64], ["vt"], ["send1"], "vo")
            for h6 in range(6):
                v = s * 6 + h6
                dest, sl = v // 3, v % 3
                r0 = dest * 2 * RPD + VL0 + sl * 128
                P.dma(send1v[r0:r0 + 128, tq * 256:(tq + 1) * 256].rearrange("p (j d) -> p j d", d=64),
                      c.vt[:, :, 256 + h6 * 64:256 + (h6 + 1) * 64], ["vt"], ["send1"], "vo")

        def phaseC_tile(l, ti, last):
            x1s, zs, bgs = x1s_l[l], zs_l[l], bgs_l[l]
            g0 = ti * 512
            s, tq = ti // 4, ti % 4
            P.dma(c.xt[:, :, :], x1s[g0:g0 + 512, :].rearrange("(j p) n -> p j n", p=128), ["x1s"], ["xt"], "xt")
            for hc in range(8):
                u = s * 8 + hc
                j, k = u // 3, u % 3
                eo = j * SRC2 + ((k * 4 + tq) * 128) * 256
                P.op("pool", lambda e, hc=hc, eo=eo: e.indirect_dma_start(
                    out=c.od[:, hc, :], out_offset=None, in_=G2[:, :],
                    in_offset=bass.IndirectOffsetOnAxis(ap=idx[:, 2:3], axis=0), element_offset=eo),
                     ["G2", "idx"], ["od"], dkey="od")
            for h6 in range(6):
                v = s * 6 + h6
                j, k = v // 3, v % 3
                eo = j * SRC2 + (((3 + k) * 4 + tq) * 128) * 256
                P.op("pool", lambda e, h6=h6, eo=eo: e.indirect_dma_start(
                    out=c.ol[:, h6, :], out_offset=None, in_=G2[:, :],
                    in_offset=bass.IndirectOffsetOnAxis(ap=idx[:, 2:3], axis=0), element_offset=eo),
                     ["G2", "idx"], ["ol"], dkey="ol")
            zsv = zs.rearrange("(c p) n -> p c n", p=128)
            if tq == 0:
                P.dma(c.zh[:, :, 1:514], zsv[:, :, g0:g0 + 513], ["zs"], ["zh"], "zh")
                P.op("act", lambda e, s=s: e.activation(out=c.zh[:, :, 0:1], in_=c.halT[:, :, s:s + 1], func=AF.Copy),
                     ["halT"], ["zh"])
            elif tq == 3:
                P.dma(c.zh[:, :, 0:513], zsv[:, :, g0 - 1:g0 + 512], ["zs"], ["zh"], "zh")
                P.op("act", lambda e, s=s: e.activation(out=c.zh[:, :, 513:514], in_=c.halT[:, :, 3 + s:4 + s],
                                                        func=AF.Copy), ["halT"], ["zh"])
            else:
                P.dma(c.zh[:, :, :], zsv[:, :, g0 - 1:g0 + 513], ["zs"], ["zh"], "zh")
            P.dma(c.bg[:, :, :], bgs[:, g0:g0 + 512].rearrange("(c p) n -> p c n", p=128), ["bgs"], ["bg"], "bg")
            for ch in range(3):
                P.op("dve", lambda e, ch=ch: e.tensor_scalar(out=c.ca[:, :], in0=c.zh[:, ch, 0:512],
                                                             scalar1=c.cw[:, ch, 0:1], scalar2=None, op0=ALU.mult),
                     ["zh", "cw"], ["ca"])
                P.op("dve", lambda e, ch=ch: e.scalar_tensor_tensor(out=c.ca[:, :], in0=c.zh[:, ch, 1:513],
                                                                    scalar=c.cw[:, ch, 1:2], in1=c.ca[:, :],
                                                                    op0=ALU.mult, op1=ALU.add),
                     ["zh", "cw", "ca"], ["ca"])
                P.op("dve", lambda e, ch=ch: e.scalar_tensor_tensor(out=c.ca[:, :], in0=c.zh[:, ch, 2:514],
                                                                    scalar=c.cw[:, ch, 2:3], in1=c.ca[:, :],
                                                                    op0=ALU.mult, op1=ALU.add),
                     ["zh", "cw", "ca"], ["ca"])
                P.op("dve", lambda e, ch=ch: e.tensor_tensor(out=c.yT[:, ch, :], in0=c.ca[:, :], in1=c.bg[:, ch, :],
                                                             op=ALU.mult), ["ca", "bg"], ["yT"])
            for j in range(4):
                for hh in range(4):
                    o1 = c.od[:, 2 * hh, j * 64:(j + 1) * 64]
                    o2 = c.od[:, 2 * hh + 1, j * 64:(j + 1) * 64]
                    obv = c.ob[:, hh * 64:(hh + 1) * 64]
                    P.op("dve", lambda e, o1=o1, o2=o2, obv=obv: e.scalar_tensor_tensor(
                        out=obv, in0=o2, scalar=c.nlam[:, 0:1], in1=o1, op0=ALU.mult, op1=ALU.add),
                         ["od", "nlam"], ["ob"])
                    P.op("act", lambda e, obv=obv, hh=hh: e.activation(out=c.junk[:, 0:64], in_=obv, func=AF.Square,
                                                                       accum_out=c.ss[:, 4 + hh:5 + hh]),
                         ["ob"], ["junk", "ss"])
                P.op("act", lambda e: e.activation(out=c.rs[:, 4:8], in_=c.ss[:, 4:8], func=AF.Sqrt,
                                                   bias=c.epsc[:, 0:1], scale=1.0 / 64), ["ss", "epsc"], ["rs"])
                P.op("dve", lambda e: e.reciprocal(out=c.rs[:, 4:8], in_=c.rs[:, 4:8]), ["rs"], ["rs"])
                for hh in range(4):
                    obv = c.ob[:, hh * 64:(hh + 1) * 64]
                    P.op("dve", lambda e, obv=obv, hh=hh, j=j: e.scalar_tensor_tensor(
                        out=c.ytm[:, j, hh * 64:(hh + 1) * 64], in0=obv, scalar=c.rs[:, 4 + hh:5 + hh],
                        in1=c.gsub[:, :], op0=ALU.mult, op1=ALU.mult), ["ob", "rs", "gsub"], ["ytm"])
                P.op("act", lambda e, j=j: e.activation(
                    out=c.ytm[:, j, 256:640].rearrange("p (h d) -> p h d", d=64), in_=c.ol[:, :, j * 64:(j + 1) * 64],
                    func=AF.Copy), ["ol"], ["ytm"])
            for kc in range(5):
                pst = c.pst[kc % 2]
                pk = "pst%d" % (kc % 2)
                for j in range(4):
                    P.op("pe", lambda e, j=j, kc=kc, pst=pst: e.transpose(
                        out=pst[:, j * 128:(j + 1) * 128], in_=c.ytm[:, j, kc * 128:(kc + 1) * 128],
                        identity=c.ident[:, :]), ["ytm", "ident"], [pk])
                if kc % 2 == 0:
                    P.op("act", lambda e, kc=kc, pst=pst: e.activation(out=c.yT[:, 3 + kc, :], in_=pst[:, 0:512],
                                                                       func=AF.Copy), [pk], ["yT"])
                else:
                    P.op("dve", lambda e, kc=kc, pst=pst: e.tensor_copy(out=c.yT[:, 3 + kc, :], in_=pst[:, 0:512]),
                         [pk], ["yT"])
            n = 0
            for j in range(4):
                for hf in range(2):
                    pd = c.ps[4 + n % 2]
                    pk = "ps%d" % (4 + n % 2)
                    n += 1
                    for kc in range(8):
                        P.op("pe", lambda e, j=j, hf=hf, kc=kc, pd=pd: e.matmul(
                            pd[:, :], lhsT=c.yT[:, kc, j * 128:(j + 1) * 128],
                            rhs=c.wout[:, kc, hf * 512:(hf + 1) * 512], start=(kc == 0), stop=(kc == 7)),
                             ["yT", "wout"], [pk])
                    P.op("dve", lambda e, j=j, hf=hf, pd=pd: e.tensor_tensor(
                        out=c.xt[:, j, hf * 512:(hf + 1) * 512], in0=pd[:, :],
                        in1=c.xt[:, j, hf * 512:(hf + 1) * 512], op=ALU.add), [pk, "xt"], ["xt"])
            norm_T(P, c, c.xt, c.gcols[:, 2, :], "n2")
            ffn(P, c, c.xt, wgu2[l], c.wdn, cnt)
            for j in range(4):
                P.op("act", lambda e, j=j: e.activation(out=c.junk[:, :], in_=c.xt[:, j, :], func=AF.Square,
                                                        accum_out=c.ss[:, j:j + 1]), ["xt"], ["junk", "ss"])
            P.op("act", lambda e: e.activation(out=c.rs[:, 0:4], in_=c.ss[:, 0:4], func=AF.Sqrt,
                                               bias=c.epsc[:, 0:1], scale=1.0 / D), ["ss", "epsc"], ["rs"])
            P.op("dve", lambda e: e.reciprocal(out=c.rs[:, 0:4], in_=c.rs[:, 0:4]), ["rs"], ["rs"])
            for j in range(4):
                P.op("dve", lambda e, j=j: e.scalar_tensor_tensor(out=c.xt[:, j, :], in0=c.xt[:, j, :],
                                                                  scalar=c.rs[:, j:j + 1], in1=c.nf[:, :],
                                                                  op0=ALU.mult, op1=ALU.mult),
                     ["xt", "rs", "nf"], ["xt"])
            if last:
                P.dma(yo[g0:g0 + 512, :].rearrange("(j p) n -> p j n", p=128), c.xt[:, :, :], ["xt"], ["yo"], "yo")

        def phaseB():
            A.reset()
            qT = A.take([128, 8, 2048], BF16); kT = A.take([128, 8, 2048], BF16)
            V = A.take([128, 128, 65], BF16); H = A.take([128, HW_], BF16)
            Vc = A.take([128, 128, 64], BF16)
            Vcf = Vc.rearrange("p k d -> p (k d)")
            G1h = G1.rearrange("r (h e) -> (r h) e", h=2)
            J = A.take([128, 128], BF16); identf = A.take([128, 128], F32)
            ms = [A.take([128, w], BF16) for (r, _, _, w) in DIL]
            pT = [A.take([128, 512], BF16) for _ in range(3)]
            osb = A.take([65, 512], F32); rc = A.take([128, 4], F32); on = A.take([128, 4, 64], F32)
            cbs = A.take([128, 12], F32)
            qTf = qT.rearrange("p r t -> p (r t)")
            kTf = kT.rearrange("p r t -> p (r t)")
            pss = [banks[0], banks[1]]
            pso, ptr = banks[2], banks[3]
            grp[0] = "cB"
            P.dma(J[:, :], Jd[:, :], [], ["J"], ukey("c"))
            P.dma(identf[:, :], idf[:, :], [], ["identfB"], ukey("c"))
            P.dma(cbs[:, :], cbd[:, :], ["cbd"], ["cbs"], ukey("c"))
            for i in range(3):
                P.dma(ms[i][:, :], mds[i][:, :], [], ["ms"], ukey("c"))
            P.op("dve", lambda e: e.memset(V[:, :, 64:65], 1.0), [], ["Vones"])
            bcnt = [0]

            def load_qk(sec_q, sec_k):
                for r in range(8):
                    P.op("pool", lambda e, r=r: e.indirect_dma_start(
                        out=qT[:, r, :], out_offset=None, in_=G1[:, :],
                        in_offset=bass.IndirectOffsetOnAxis(ap=idx[:, 0:1], axis=0),
                        element_offset=r * SRC1 + sec_q * 2048), ["G1", "idx"], ["qT"], dkey="qT")
                    P.op("pool", lambda e, r=r: e.indirect_dma_start(
                        out=kT[:, r, :], out_offset=None, in_=G1[:, :],
                        in_offset=bass.IndirectOffsetOnAxis(ap=idx[:, 0:1], axis=0),
                        element_offset=r * SRC1 + sec_k * 2048), ["G1", "idx"], ["kT"], dkey="kT")

            def unit(slot, is_dil, pb):
                K = 64 if is_dil else 32
                k = slot - 3 if is_dil else slot
                vsec = (VL0 if is_dil else VD0) + k * 128
                for r in range(8):
                    P.op("pool", lambda e, r=r: e.indirect_dma_start(
                        out=Vcf[:, r * 1024:(r + 1) * 1024], out_offset=None, in_=G1h[:, :],
                        in_offset=bass.IndirectOffsetOnAxis(ap=idx[:, 1:2], axis=0),
                        element_offset=r * SRC1 + vsec * 1024), ["G1", "idx"], ["Vc"], dkey="V")
                P.op("dve", lambda e: e.tensor_copy(out=V[:, :, 0:64], in_=Vc[:, :, :]), ["Vc"], ["V"])
                hsrc = bass.AP(tensor=T.tensor, offset=slot * 3328, ap=[[1, 128], [1, HW_]])
                P.dma(H[:, :], hsrc, ["T"], ["H"], "H", eng="pool")
                for qt in ([5] if dbg == 2 else [0, 5, 31] if dbg else range(32)):
                    qb = qt * 512
                    tiles = []
                    if is_dil:
                        for bi, (r, dmin, dmax, w) in enumerate(DIL):
                            for kt in range(max(0, (qb + dmin) // 128), min(127, (qb + dmax) // 128) + 1):
                                tiles.append((kt, bi))
                    else:
                        tiles = [(kt, None) for kt in range(128)]
                    for ti, (kt, bi) in enumerate(tiles):
                        d = kt * 128 - qb
                        near = is_dil or (-640 <= d <= 1024)
                        i = bcnt[0]
                        bcnt[0] += 1
                        ps = pss[i % 2]
                        pk = "pss%d" % (i % 2)
                        pt = pT[i % 3]
                        ptk = "pT%d" % (i % 3)
                        P.op("pe", lambda e, kt=kt, qb=qb, ps=ps, near=near: e.matmul(
                            ps[:, :], lhsT=kTf[pb:pb + K, kt * 128:(kt + 1) * 128], rhs=qTf[pb:pb + K, qb:qb + 512],
                            start=True, stop=(not near)), ["kT", "qT"], [pk])
                        if near:
                            off = 1536 - d
                            P.op("pe", lambda e, ps=ps, off=off: e.matmul(ps[:, :], lhsT=J[:, :],
                                                                          rhs=H[:, off:off + 512], start=False,
                                                                          stop=True), ["J", "H"], [pk])
                            P.op("act", lambda e, ps=ps, pt=pt: e.activation(out=pt[:, :], in_=ps[:, :], func=AF.Exp),
                                 [pk], [ptk])
                        else:
                            col = slot if d < 0 else 6 + slot
                            P.op("act", lambda e, ps=ps, pt=pt, col=col: e.activation(
                                out=pt[:, :], in_=ps[:, :], func=AF.Exp, bias=cbs[:, col:col + 1], scale=1.0),
                                 [pk, "cbs"], [ptk])
                        if is_dil:
                            r, dmin, dmax, w = DIL[bi]
                            o = dmax - d
                            P.op("dve", lambda e, pt=pt, bi=bi, o=o: e.tensor_tensor(out=pt[:, :], in0=pt[:, :],
                                                                                     in1=ms[bi][:, o:o + 512],
                                                                                     op=ALU.mult), [ptk, "ms"], [ptk])
                        P.op("pe", lambda e, kt=kt, pt=pt, ti=ti, nt=len(tiles): e.matmul(
                            pso[0:65, :], lhsT=V[:, kt, 0:65], rhs=pt[:, :], start=(ti == 0), stop=(ti == nt - 1)),
                             ["V", "Vones", ptk], ["pso"])
                    P.op("act", lambda e: e.activation(out=osb[:, :], in_=pso[0:65, :], func=AF.Copy), ["pso"],
                         ["osb"])
                    for j in range(4):
                        P.op("pe", lambda e, j=j: e.transpose(out=ptr[:, j * 65:(j + 1) * 65],
                                                              in_=osb[:, j * 128:(j + 1) * 128],
                                                              identity=identf[0:65, 0:65]), ["osb", "identfB"],
                             ["ptr"])
                    ptv = ptr[:, 0:260].rearrange("p (j d) -> p j d", d=65)
                    P.op("dve", lambda e, ptv=ptv: e.reciprocal(out=rc[:, :], in_=ptv[:, :, 64]), ["ptr"], ["rc"])
                    for j in range(4):
                        P.op("dve", lambda e, j=j, ptv=ptv: e.tensor_scalar(out=on[:, j, :], in0=ptv[:, j, 0:64],
                                                                            scalar1=rc[:, j:j + 1], scalar2=None,
                                                                            op0=ALU.mult), ["ptr", "rc"], ["on"])
                    dest, tq = qt // 4, qt % 4
                    r0 = dest * R2D + (slot * 4 + tq) * 128
                    P.dma(send2[r0:r0 + 128, :].rearrange("p (j d) -> p j d", d=64), on[:, :, :], ["on"], ["send2"],
                          "on")

            load_qk(QD0, KD0)
            if dbg == 2:
                unit(0, False, 0)
                P.dma(dqT[:, :], qTf[:, :], ["qT"], ["dqT"], "dbg5")
                P.dma(dkT[:, :], kTf[:, :], ["kT"], ["dkT"], "dbg6")
                P.dma(dV[:, :], V.rearrange("p k d -> p (k d)"), ["V", "Vones"], ["dV"], "dbg7")
                P.dma(dsend1[:, :], send1[:, :], ["send1"], ["dsend1"], "dbg8")
                return
            for slot in range(3):
                unit(slot, False, 32 * slot)
            load_qk(QL0, KL0)
            unit(3, True, 0)
            unit(4, True, 64)
            load_qk(QL0 + 128, KL0 + 128)
            unit(5, True, 0)

        def allgather(src, dst, rkey, wkey):
            P.op("pool", lambda e: e.collective_compute("AllGather", ALU.bypass, replica_groups=[list(range(8))],
                                                        ins=[src.tensor.ap().opt()], outs=[dst.tensor.ap().opt()]),
                 [rkey], [wkey], dkey="ag", dinc=1)

        P.barrier()
        carve_AC()
        consts_A(0)
        load_wdn(P, c, wdn1[0])
        for ti in range(12):
            g0 = ti * 512
            P.dma(c.xt[:, :, :], xin[g0:g0 + 512, :].rearrange("(j p) n -> p j n", p=128), [], ["xt"], "xt")
            phaseA_tile(0, ti, None, None)
        for l in ([0] if dbg else range(2)):
            P.barrier()
            allgather(send1, G1, "send1", "G1")
            allgather(zb, G3, "zb", "G3")
            P.barrier()
            phaseB()
            if dbg == 2:
                break
            P.barrier()
            allgather(send2, G2, "send2", "G2")
            P.barrier()
            carve_AC()
            consts_C(l)
            if l == 0:
                consts_A(1)
            load_wdn(P, c, wdn2[l])
            for ti in range(12):
                if dbg and (ti % 4) == 2:
                    continue
                phaseC_tile(l, ti, last=(l == 1 or dbg))
                if l == 0 and not dbg:
                    phaseA_tile(1, ti, wdn1[1], (wdn2[0] if ti < 11 else None))
            if dbg:
                P.dma(dsend2[:, :], send2[:, :], ["send2"], ["dsend2"], "dbg4")
        P.final_wait()
        P.emit()
    return nc


def _rel_bucket(rel):
    nb, me = 16, 8
    rel = np.asarray(rel, np.int64)
    ret = np.where(rel > 0, nb, 0)
    n = np.abs(rel)
    nf = np.maximum(n, 1).astype(np.float32)
    large = me + (np.log(nf / np.float32(me)) / np.float32(math.log(1024 / me)) * np.float32(nb - me)).astype(np.int32)
    large = np.minimum(large, nb - 1)
    return ret + np.where(n < me, n, large)


def _consts():
    cst = {}
    cst["ident"] = np.eye(128, dtype=np.float32).astype(NPBF)
    cst["identf"] = np.eye(128, dtype=np.float32)
    cst["J"] = np.eye(128, dtype=np.float32)[::-1].copy().astype(NPBF)
    cst["bd32"] = np.kron(np.eye(4, dtype=np.float32), np.ones((32, 32), np.float32)).astype(NPBF)
    cst["bd64"] = np.kron(np.eye(2, dtype=np.float32), np.ones((64, 64), np.float32)).astype(NPBF)
    e15 = np.zeros((32, 128), np.float32); e15[15] = 1
    e31 = np.zeros((32, 128), np.float32); e31[31] = 1
    cst["e15"], cst["e31"] = e15, e31
    m = np.arange(3328)
    bk = _rel_bucket(CC - m)
    oh = (bk[None, :] == np.arange(32)[:, None]).astype(np.float32)
    oh[:, 3327:] = 0
    cst["ohrev"] = oh
    for (r, dmin, dmax, w) in DIL:
        rel = np.arange(128)[:, None] - np.arange(w)[None, :] + dmax
        cst["m%d" % r] = ((rel % r == 0) & (np.abs(rel) <= 64 * r)).astype(np.float32).astype(NPBF)
    return cst


def _col(v):
    v = np.asarray(v, np.float32)
    return np.ascontiguousarray(v.reshape(v.shape[0], -1, 128).transpose(0, 2, 1))


_PROGS = {}


def kernel(x_prompt, x_sample, ffn1_norm, ffn1_w_gu, ffn1_w_down, mix_norm, w_in, conv_w, diff_q_norm, diff_k_norm,
           lambda_q1, lambda_k1, lambda_q2, lambda_k2, diff_sub_norm, dil_q_norm, dil_k_norm, w_out, ffn2_norm,
           ffn2_w_gu, ffn2_w_down, final_norm, rel_bias):
    f = lambda a: np.ascontiguousarray(np.asarray(a, np.float32))
    cst = _consts()
    xs = [f(x_prompt)[0], f(x_sample)[0], f(x_sample)[1]]
    cores = list(range(NCORE))
    rel_bias = f(rel_bias)
    gq = np.stack([np.tile(f(diff_q_norm), (1, 4)), np.tile(f(diff_k_norm), (1, 4)), np.tile(f(dil_q_norm), (1, 2)),
                   np.tile(f(dil_k_norm), (1, 2))], 2)
    cw = f(conv_w)
    cwl = np.ascontiguousarray(cw.reshape(2, 3, 3, 128).transpose(0, 3, 2, 1).reshape(2, 128, 9))
    lamv = np.concatenate([f(lambda_q1), f(lambda_k1), f(lambda_q2), f(lambda_k2)], 1)[:, None, :]
    shared = {
        "n1": _col(ffn1_norm), "nm": _col(mix_norm), "n2": _col(ffn2_norm), "nf": f(final_norm)[:, None, :],
        "gq": np.ascontiguousarray(gq.astype(np.float32)), "conv_w": cwl,
        "lamv": np.ascontiguousarray(lamv), "gsub": f(diff_sub_norm)[:, None, :],
    }
    shared.update(cst)
    maps = []
    p = np.arange(128)
    for c in cores:
        xc = np.concatenate([xs[s][c * SL:(c + 1) * SL] for s in range(3)], 0)
        idx = np.zeros((128, 4), np.int32)
        idx[:, 0] = c * RPD + p
        idx[:, 1] = c * 2 * RPD + p
        idx[:, 2] = c * R2D + p
        idx[:, 3] = c * 7 + 6
        for s in range(3):
            if c > 0:
                idx[s, 3] = (c - 1) * 7 + s * 2 + 1
            if c < NCORE - 1:
                idx[3 + s, 3] = (c + 1) * 7 + s * 2
        cols = []
        for k in range(3):
            u = 3 * c + k
            cols.append((u % 8) // 2)
        for k in range(3):
            v = 3 * c + k
            cols.append(4 + (v % 6))
        m = dict(shared)
        for nm_, arr in (("w_gu1", ffn1_w_gu), ("w_dn1", ffn1_w_down), ("w_gu2", ffn2_w_gu), ("w_dn2", ffn2_w_down),
                         ("w_in", w_in), ("w_out", w_out)):
            a = np.asarray(arr, np.float32)
            kk = a.shape[1] // 8
            m[nm_] = np.ascontiguousarray(a[:, c * kk:(c + 1) * kk, :])
        m.update({"x": xc, "idx": idx, "tabc": np.ascontiguousarray(rel_bias[:, cols])})
        maps.append(m)
    if "F" not in _PROGS:
        _PROGS["F"] = build_fused()
    res = run_bass_kernel_spmd(_PROGS["F"], maps, core_ids=cores).results
    y = [np.asarray(r["y"]) for r in res]
    ys = [np.concatenate([y[c][s * SL:(s + 1) * SL] for c in cores], 0) for s in range(3)]
    y_prompt = ys[0][None].astype(np.float32)
    y_sample = np.stack([ys[1], ys[2]], 0).astype(np.float32)
    return (y_prompt, y_sample)
```

```python
import math
import numpy as np
import ml_dtypes
import concourse.bass as bass
import concourse.mybir as mybir
from concourse.bass_utils import run_bass_kernel_spmd

F32 = mybir.dt.float32
I32 = mybir.dt.int32
BF16 = mybir.dt.bfloat16
AF = mybir.ActivationFunctionType
ALU = mybir.AluOpType
NPBF = ml_dtypes.bfloat16

NCORE = 8
S = 16384
SL = S // NCORE
NT = 3 * SL
D = 1024
DFF = 2816
EPS = 1e-6
TW = 3327
HW_ = 3200
CC = 1663


class Op:
    __slots__ = ("fn", "waits", "signal", "dkey", "dcnt", "dinc", "epoch")

    def __init__(self, fn):
        self.fn = fn
        self.waits = []
        self.signal = False
        self.dkey = None
        self.dcnt = 0
        self.dinc = 16
        self.epoch = 0


class Prog:
    ENG = ("pe", "act", "dve", "pool", "sp")

    def __init__(self, nc):
        self.nc = nc
        self.ops = {e: [] for e in self.ENG}
        self.res = {}
        self.dcount = {}
        self.waited = {e: {} for e in self.ENG}
        self.epoch = 0

    def _st(self, k):
        st = self.res.get(k)
        if st is None:
            st = {"w": {}, "r": {}, "wd": {}, "rd": {}}
            self.res[k] = st
        return st

    def op(self, eng, fn, reads=(), writes=(), dkey=None, dinc=16):
        lst = self.ops[eng]
        idx = len(lst)
        o = Op(fn)
        o.epoch = self.epoch
        edeps = {}
        ddeps = {}

        def add(dst, src):
            for k, v in src.items():
                if dst.get(k, -1) < v:
                    dst[k] = v

        for k in reads:
            st = self._st(k)
            add(edeps, st["w"])
            add(ddeps, st["wd"])
        for k in writes:
            st = self._st(k)
            add(edeps, st["w"])
            add(edeps, st["r"])
            add(ddeps, st["wd"])
            add(ddeps, st["rd"])
        wd = self.waited[eng]
        for e2, i2 in edeps.items():
            if e2 == eng and eng in ("pe", "sp"):
                continue
            if wd.get(("e", e2), -1) >= i2:
                continue
            wd[("e", e2)] = i2
            self.ops[e2][i2].signal = True
            o.waits.append(("e", e2, i2))
        for dk, c in ddeps.items():
            c = self.dcount[dk]
            if wd.get(("d", dk), -1) >= c:
                continue
            wd[("d", dk)] = c
            o.waits.append(("d", dk, c))
        if dkey is not None:
            c = self.dcount.get(dkey, 0) + dinc
            o.dinc = dinc
            self.dcount[dkey] = c
            o.dkey = dkey
            o.dcnt = c
            for k in reads:
                self._st(k)["rd"][dkey] = c
            for k in writes:
                self._st(k)["wd"][dkey] = c
        else:
            for k in reads:
                self._st(k)["r"][eng] = idx
            for k in writes:
                self._st(k)["w"][eng] = idx
        lst.append(o)
        return o

    def dma(self, out, in_, reads, writes, dkey, eng="sp"):
        return self.op(eng, lambda e: e.dma_start(out=out, in_=in_), reads, writes, dkey=dkey)

    def _last_real(self, e):
        lst = self.ops[e]
        for i in range(len(lst) - 1, -1, -1):
            if lst[i].fn is not None and lst[i].dkey is None:
                return i
        return None

    def barrier(self):
        lasts = {}
        for e in ("pe", "act", "dve"):
            i = self._last_real(e)
            if i is not None:
                lasts[e] = i
                self.ops[e][i].signal = True
        dk = dict(self.dcount)
        for e in self.ENG:
            o = Op(None)
            o.epoch = self.epoch
            for e2, i2 in lasts.items():
                if e2 != e:
                    o.waits.append(("e", e2, i2))
                    self.waited[e][("e", e2)] = i2
            for k, c in dk.items():
                o.waits.append(("d", k, c))
                self.waited[e][("d", k)] = c
            self.ops[e].append(o)
        self.epoch += 1

    def final_wait(self):
        o = Op(None)
        o.epoch = self.epoch
        for e in ("pe", "act", "dve"):
            i = self._last_real(e)
            if i is not None:
                self.ops[e][i].signal = True
                o.waits.append(("e", e, i))
        for dk, c in self.dcount.items():
            o.waits.append(("d", dk, c))
        self.ops["sp"].append(o)

    def emit(self):
        nc = self.nc
        import contextlib
        with contextlib.ExitStack() as es:
            esem = {}
            for e in self.ENG:
                for o in self.ops[e]:
                    if o.signal and o.dkey is None and (e, o.epoch) not in esem:
                        esem[(e, o.epoch)] = es.enter_context(nc.semaphore("s_%s_%d" % (e, o.epoch)))
            dsem = {dk: es.enter_context(nc.semaphore("d_%d" % i)) for i, dk in enumerate(self.dcount)}
            cnt = {}
            for e in self.ENG:
                c = 0
                ep = -1
                arr = []
                for o in self.ops[e]:
                    if o.epoch != ep:
                        ep = o.epoch
                        c = 0
                    if o.signal and o.dkey is None:
                        c += 1
                    arr.append(c)
                cnt[e] = arr
            block = es.enter_context(nc.Block())

            def run(eng_name, eng):
                for o in self.ops[eng_name]:
                    for w in o.waits:
                        if w[0] == "e":
                            eng.wait_ge(esem[(w[1], self.ops[w[1]][w[2]].epoch)], cnt[w[1]][w[2]])
                        else:
                            eng.wait_ge(dsem[w[1]], w[2])
                    if o.fn is None:
                        continue
                    ins = o.fn(eng)
                    if o.dkey is not None:
                        if o.dinc == 16:
                            ins.then_inc(dsem[o.dkey], 16)
                        else:
                            ins.then_inc(dsem[o.dkey])
                    elif o.signal:
                        ins.then_inc(esem[(eng_name, o.epoch)], 1)

            @block.tensor
            def _(e):
                run("pe", e)

            @block.scalar
            def _(e):
                run("act", e)

            @block.vector
            def _(e):
                run("dve", e)

            @block.gpsimd
            def _(e):
                run("pool", e)

            @block.sync
            def _(e):
                run("sp", e)


class Ctx:
    pass


def load_cols(P, c, name, dram_vec_ap, ncols_chunks):
    pass


def norm_T(P, c, xt, gcols, tag):
    nc = P.nc
    for j in range(4):
        P.op("act", lambda e, j=j: e.activation(out=c.junk[:, :], in_=xt[:, j, :], func=AF.Square,
                                                accum_out=c.ss[:, j:j + 1]),
             reads=["xt"], writes=["junk", "ss"])
    P.op("act", lambda e: e.activation(out=c.rs[:, 0:4], in_=c.ss[:, 0:4], func=AF.Sqrt,
                                       bias=c.epsc[:, 0:1], scale=1.0 / D),
         reads=["ss"], writes=["rs"])
    P.op("dve", lambda e: e.reciprocal(out=c.rs[:, 0:4], in_=c.rs[:, 0:4]), reads=["rs"], writes=["rs"])
    for j in range(4):
        P.op("dve", lambda e, j=j: e.tensor_scalar(out=c.h[:, j, :], in0=xt[:, j, :], scalar1=c.rs[:, j:j + 1],
                                                   scalar2=None, op0=ALU.mult),
             reads=["xt", "rs"], writes=["h"])
    for kc in range(8):
        pst = c.pst[kc % 2]
        pk = "pst%d" % (kc % 2)
        for j in range(4):
            P.op("pe", lambda e, j=j, kc=kc, pst=pst: e.transpose(out=pst[:, j * 128:(j + 1) * 128],
                                                                  in_=c.h[:, j, kc * 128:(kc + 1) * 128],
                                                                  identity=c.ident[:, :]),
                 reads=["h"], writes=[pk])
        if kc % 2 == 0:
            P.op("act", lambda e, kc=kc, pst=pst: e.activation(out=c.hT[:, kc, :], in_=pst[:, 0:512], func=AF.Copy,
                                                               scale=gcols[:, kc:kc + 1]),
                 reads=[pk], writes=["hT"])
        else:
            P.op("dve", lambda e, kc=kc, pst=pst: e.tensor_scalar(out=c.hT[:, kc, :], in0=pst[:, 0:512],
                                                                  scalar1=gcols[:, kc:kc + 1], scalar2=None,
                                                                  op0=ALU.mult),
                 reads=[pk], writes=["hT"])


def ffn(P, c, xt, wgu, wdn_sb, cnt):
    wgu_v = wgu.rearrange("(kc p) n -> p kc n", p=128)
    for j in range(22):
        slot = cnt[0] % 2
        cnt[0] += 1
        wk = "wup%d" % slot
        wt = c.wup[slot]
        P.dma(wt[:, :, 0:128], wgu_v[:, :, j * 128:(j + 1) * 128], reads=[], writes=[wk], dkey=wk, eng="pool")
        P.dma(wt[:, :, 128:256], wgu_v[:, :, DFF + j * 128:DFF + (j + 1) * 128], reads=[], writes=[wk], dkey=wk,
              eng="pool")
        pg, pu = c.ps[0 + 2 * (j % 2)], c.ps[1 + 2 * (j % 2)]
        kg, ku = "ps%d" % (2 * (j % 2)), "ps%d" % (1 + 2 * (j % 2))
        for kc in range(8):
            P.op("pe", lambda e, kc=kc, wt=wt, pg=pg: e.matmul(pg[:, :], lhsT=wt[:, kc, 0:128], rhs=c.hT[:, kc, :],
                                                               start=(kc == 0), stop=(kc == 7)),
                 reads=[wk, "hT"], writes=[kg])
        for kc in range(8):
            P.op("pe", lambda e, kc=kc, wt=wt, pu=pu: e.matmul(pu[:, :], lhsT=wt[:, kc, 128:256], rhs=c.hT[:, kc, :],
                                                               start=(kc == 0), stop=(kc == 7)),
                 reads=[wk, "hT"], writes=[ku])
        sg = c.sg[j % 2]
        sk = "sg%d" % (j % 2)
        P.op("act", lambda e, pg=pg, sg=sg: e.activation(out=sg[:, :], in_=pg[:, :], func=AF.Silu),
             reads=[kg], writes=[sk])
        P.op("dve", lambda e, j=j, pu=pu, sg=sg: e.tensor_tensor(out=c.actT[:, j, :], in0=sg[:, :], in1=pu[:, :],
                                                                 op=ALU.mult),
             reads=[sk, ku], writes=["actT"])
    n = 0
    for j in range(4):
        for hf in range(2):
            pd = c.ps[4 + n % 2]
            pk = "ps%d" % (4 + n % 2)
            n += 1
            for jj in range(22):
                P.op("pe", lambda e, j=j, hf=hf, jj=jj, pd=pd: e.matmul(
                    pd[:, :], lhsT=c.actT[:, jj, j * 128:(j + 1) * 128], rhs=wdn_sb[:, jj, hf * 512:(hf + 1) * 512],
                    start=(jj == 0), stop=(jj == 21)), reads=["actT", "wdn"], writes=[pk])
            P.op("dve", lambda e, j=j, hf=hf, pd=pd: e.scalar_tensor_tensor(
                out=xt[:, j, hf * 512:(hf + 1) * 512], in0=pd[:, :], scalar=0.5, in1=xt[:, j, hf * 512:(hf + 1) * 512],
                op0=ALU.mult, op1=ALU.add), reads=[pk, "xt"], writes=["xt"])


def load_wdn(P, c, wdn):
    v = wdn.rearrange("(kc p) n -> p kc n", p=128)
    for q in range(2):
        P.dma(c.wdn[:, q * 11:(q + 1) * 11, :], v[:, q * 11:(q + 1) * 11, :], reads=[], writes=["wdn"], dkey="wdn",
              eng="pool")


def gcol_load(P, dst, vec_ap, nch, key):
    P.dma(dst[:, 0:nch], vec_ap[:, :], reads=[], writes=[key], dkey=key)


DIL = ((1, -128, 512, 1152), (4, -256, 640, 1408), (16, -1024, 1408, 2944))
RPD = 1024
SRC1 = 8 * RPD * 2048
R2D = 3072
SRC2 = 8 * R2D * 256
QD0, KD0, QL0, KL0 = 0, 128, 256, 448
VD0, VL0 = 1280, 1664


class Arena:
    def __init__(self, ap):
        self.a = ap
        self.off = 0

    def reset(self):
        self.off = 0

    def take(self, shape, dt):
        n = 1
        for v in shape[1:]:
            n *= v
        ne = n * 2 if dt in (F32, I32) else n
        ne = (ne + 1) // 2 * 2
        v = self.a[0:shape[0], self.off:self.off + ne]
        self.off += ne
        assert self.off <= self.a.shape[1], ("arena overflow", self.off)
        if dt in (F32, I32):
            v = v.bitcast(dt)
        if len(shape) == 3:
            v = v.rearrange("p (a b) -> p a b", b=shape[2])
        return v


def build_fused(dbg=False):
    import contextlib
    nc = bass.Bass("TRN2", target_bir_lowering=False)
    P = Prog(nc)
    c = Ctx()

    def din(name, shape, dt=F32):
        return nc.dram_tensor(name, shape, dt, kind="ExternalInput").ap()

    def dscr(name, shape, dt=F32):
        return nc.dram_tensor(name, shape, dt).ap()

    xin = din("x", [NT, D])
    yo = nc.dram_tensor("y", [NT, D], F32, kind="ExternalOutput").ap()
    idxd = din("idx", [128, 4], I32)
    tabc = din("tabc", [32, 6])
    n1 = din("n1", [2, 128, 8]); nm = din("nm", [2, 128, 8]); n2 = din("n2", [2, 128, 8])
    nf = din("nf", [2, 1, D])
    wshapes = {"w_gu1": (D, 2 * DFF), "w_dn1": (DFF, D), "w_gu2": (D, 2 * DFF), "w_dn2": (DFF, D),
               "w_in": (D, 3072), "w_out": (D, D)}
    wsl = {k: din(k, [2, kk // 8, nn]) for k, (kk, nn) in wshapes.items()}
    wloc = {k: [dscr("%s_loc%d" % (k, l), [kk // 8, nn], BF16) for l in range(2)] for k, (kk, nn) in wshapes.items()}
    wful = {k: [dscr("%s_bf%d" % (k, l), [kk, nn], BF16) for l in range(2)] for k, (kk, nn) in wshapes.items()}
    wgu1, wdn1, wgu2, wdn2, win, wout = (wful[k] for k in ("w_gu1", "w_dn1", "w_gu2", "w_dn2", "w_in", "w_out"))
    gqd = din("gq", [2, 128, 4]); convw = din("conv_w", [2, 128, 9])
    lamv = din("lamv", [2, 1, 128]); gsub = din("gsub", [2, 1, 64])
    ident_d = din("ident", [128, 128], BF16); idf = din("identf", [128, 128])
    Jd = din("J", [128, 128], BF16)
    bd32 = din("bd32", [128, 128], BF16); bd64 = din("bd64", [128, 128], BF16)
    e15 = din("e15", [32, 128]); e31 = din("e31", [32, 128]); ohrev = din("ohrev", [32, 3328])
    mds = [din("m%d" % r, [128, w], BF16) for (r, _, _, w) in DIL]

    x1s_l = [dscr("x1s%d" % l, [NT, D]) for l in range(2)]
    zs_l = [dscr("zs%d" % l, [384, NT]) for l in range(2)]
    bgs_l = [dscr("bgs%d" % l, [384, NT]) for l in range(2)]
    send1 = dscr("send1", [8 * RPD, 2048], BF16); G1 = dscr("G1", [64 * RPD, 2048], BF16)
    zb = dscr("zb", [7, 384]); G3 = dscr("G3", [56, 384])
    send2 = dscr("send2", [8 * R2D, 256]); G2 = dscr("G2", [64 * R2D, 256])
    T = dscr("Ttab", [6, 3328]); cbd = dscr("cbd", [128, 12])
    send1v = send1.rearrange("r (h e) -> (r h) e", h=2)
    G1v = G1.rearrange("r (h e) -> (r h) e", h=2).rearrange("r (k d) -> r k d", d=64)
    G2v = G2.rearrange("r (j d) -> r j d", d=64)

    if dbg == 2:
        dsend1 = nc.dram_tensor("dsend1", [8 * RPD, 2048], BF16, kind="ExternalOutput").ap()
        dqT = nc.dram_tensor("dqT", [128, 16384], BF16, kind="ExternalOutput").ap()
        dkT = nc.dram_tensor("dkT", [128, 16384], BF16, kind="ExternalOutput").ap()
        dV = nc.dram_tensor("dV", [128, 128 * 65], BF16, kind="ExternalOutput").ap()
    if dbg:
        dsend2 = nc.dram_tensor("dsend2", [8 * R2D, 256], F32, kind="ExternalOutput").ap()
    with contextlib.ExitStack() as es:
        arena_t = es.enter_context(nc.sbuf_tensor("arena", [128, 106200], BF16))
        idx = es.enter_context(nc.sbuf_tensor("idxs", [128, 4], I32))
        banks = [es.enter_context(nc.psum_tensor("bk%d" % i, [128, 512], F32)) for i in range(6)]
        pstb = [es.enter_context(nc.psum_tensor("pstb%d" % i, [128, 1024], BF16)) for i in range(2)]
        A = Arena(arena_t[:, :])
        P.dma(idx[:, :], idxd[:, :], [], ["idx"], "idx")
        c.ps = banks
        c.pst = pstb
        cnt = [0]
        wcnt = [0]
        uq = [0]

        grp = ["cP"]

        def ukey(prefix):
            if prefix == "c":
                return grp[0]
            return prefix

        for k in ("w_gu1", "w_dn1", "w_in", "w_out", "w_gu2", "w_dn2"):
            for l in range(2):
                kk = wshapes[k][0] // 8
                hk = kk // 2
                for q in range(2):
                    P.dma(wloc[k][l][q * hk:(q + 1) * hk, :], wsl[k][l, q * hk:(q + 1) * hk, :], [],
                          ["wloc"], "wc", eng="pool")
        for k in ("w_gu1", "w_dn1", "w_in", "w_out", "w_gu2", "w_dn2"):
            for l in range(2):
                P.op("pool", lambda e, k=k, l=l: e.collective_compute(
                    "AllGather", ALU.bypass, replica_groups=[list(range(8))],
                    ins=[wloc[k][l].tensor.ap().opt()], outs=[wful[k][l].tensor.ap().opt()]),
                     ["wloc"], ["wful"], dkey="agw", dinc=1)
        tab = A.take([32, 6], F32); oh = A.take([32, 3328], F32)
        e15s = A.take([32, 128], F32); e31s = A.take([32, 128], F32)
        tsb = A.take([6, 3328], F32); cb = A.take([128, 12], F32); zrow = A.take([1, 384], F32)
        P.dma(tab[:, :], tabc[:, :], [], ["tab"], ukey("c"))
        P.dma(oh[:, :], ohrev[:, :], [], ["oh"], ukey("c"))
        P.dma(e15s[:, :], e15[:, :], [], ["e15"], ukey("c"))
        P.dma(e31s[:, :], e31[:, :], [], ["e31"], ukey("c"))
        ptb = banks[0]
        for i in range(7):
            n = 512 if i < 6 else 256
            P.op("pe", lambda e, i=i, n=n: e.matmul(ptb[0:6, 0:n], lhsT=tab[:, 0:6], rhs=oh[:, i * 512:i * 512 + n],
                                                   start=True, stop=True), ["tab", "oh"], ["ptb"])
            P.op("act", lambda e, i=i, n=n: e.activation(out=tsb[0:6, i * 512:i * 512 + n], in_=ptb[0:6, 0:n],
                                                         func=AF.Copy), ["ptb"], ["tsb"])
        P.dma(T[:, :], tsb[:, :], ["tsb"], ["T"], "T")
        P.op("pe", lambda e: e.matmul(ptb[:, 0:6], lhsT=e15s[:, :], rhs=tab[:, 0:6], start=True, stop=True),
             ["e15", "tab"], ["ptb"])
        P.op("act", lambda e: e.activation(out=cb[:, 0:6], in_=ptb[:, 0:6], func=AF.Copy), ["ptb"], ["cb"])
        P.op("pe", lambda e: e.matmul(ptb[:, 0:6], lhsT=e31s[:, :], rhs=tab[:, 0:6], start=True, stop=True),
             ["e31", "tab"], ["ptb"])
        P.op("act", lambda e: e.activation(out=cb[:, 6:12], in_=ptb[:, 0:6], func=AF.Copy), ["ptb"], ["cb"])
        P.dma(cbd[:, :], cb[:, :], ["cb"], ["cbd"], "cbd")
        P.op("dve", lambda e: e.memset(zrow[:, :], 0.0), [], ["zrow"])
        P.dma(zb[6:7, :], zrow[:, :], ["zrow"], ["zb"], "zb6")

        def carve_AC():
            A.reset()
            c.xt = A.take([128, 4, 1024], F32)
            c.h = A.take([128, 4, 1024], BF16)
            c.hT = A.take([128, 8, 512], BF16)
            c.actT = A.take([128, 22, 512], BF16)
            c.junk = A.take([128, 1024], BF16)
            c.ss = A.take([128, 8], F32); c.rs = A.take([128, 8], F32); c.epsc = A.take([128, 2], F32)
            c.ident = A.take([128, 128], BF16)
            c.wup = [A.take([128, 8, 256], BF16) for _ in range(2)]
            c.wdn = A.take([128, 22, 1024], BF16)
            c.sg = [A.take([128, 512], F32) for _ in range(2)]
            c.gcols = A.take([128, 3, 8], F32)
            c.yT = A.take([128, 8, 512], BF16); c.ytm = A.take([128, 4, 640], BF16)
            odf = A.take([128, 2048], F32)
            c.od = odf.rearrange("p (u n) -> p u n", n=256)
            c.usb_view = odf[:, 0:1536].rearrange("p (a b) -> p a b", b=512)
            olf = A.take([128, 1536], F32)
            c.ol = olf.rearrange("p (u n) -> p u n", n=256)
            c.ob = A.take([128, 256], F32); c.wout = A.take([128, 8, 1024], BF16)
            c.zh = A.take([128, 3, 514], F32); c.bg = A.take([128, 3, 512], F32)
            c.cw = A.take([128, 3, 3], F32); c.ca = A.take([128, 512], F32)
            c.nlam = A.take([128, 2], F32); c.gsub = A.take([128, 64], F32); c.nf = A.take([128, 1024], F32)
            c.lamv = A.take([128, 128], F32); c.lt = A.take([128, 4], F32)
            c.hal = A.take([128, 384], F32); c.halT = A.take([128, 3, 8], F32); c.identf = A.take([128, 128], F32)
            c.usb = c.usb_view; c.zt = c.zh[:, :, 0:512]; c.bgt = c.bg
            c.k = {"usb": "od", "zt": "zh", "bgt": "bg"}
            c.wch = [A.take([128, 8, 128], BF16) for _ in range(2)]
            c.wv = A.take([128, 8, 640], BF16)
            c.sq = [A.take([128, 512], BF16) for _ in range(2)]
            c.rq = [A.take([128, 512], F32) for _ in range(2)]
            c.qk = [A.take([128, 512], BF16) for _ in range(2)]
            c.vt = A.take([128, 4, 640], BF16)
            c.gq = A.take([128, 4], F32); c.bd32 = A.take([128, 128], BF16); c.bd64 = A.take([128, 128], BF16)

        def consts_A(l):
            grp[0] = "cA"
            P.dma(c.ident[:, :], ident_d[:, :], [], ["ident"], ukey("c"))
            P.dma(c.gcols[:, 0, :], n1[l], [], ["gcols"], ukey("c"))
            P.dma(c.gcols[:, 1, :], nm[l], [], ["gcols"], ukey("c"))
            P.dma(c.gq[:, :], gqd[l], [], ["gq"], ukey("c"))
            P.dma(c.bd32[:, :], bd32[:, :], [], ["bd"], ukey("c"))
            P.dma(c.bd64[:, :], bd64[:, :], [], ["bd"], ukey("c"))
            P.op("dve", lambda e: e.memset(c.epsc[:, :], EPS), [], ["epsc"])
            P.op("dve", lambda e: e.tensor_scalar(out=c.gq[:, 0:1], in0=c.gq[:, 0:1], scalar1=32 ** -0.5, scalar2=None,
                                                  op0=ALU.mult), ["gq"], ["gq"])
            P.op("dve", lambda e: e.tensor_scalar(out=c.gq[:, 2:3], in0=c.gq[:, 2:3], scalar1=64 ** -0.5, scalar2=None,
                                                  op0=ALU.mult), ["gq"], ["gq"])
            wv_v = win[l].rearrange("(kc p) n -> p kc n", p=128)
            P.dma(c.wv[:, :, 0:256], wv_v[:, :, 1664:1920], [], ["wv"], "wv", eng="pool")
            P.dma(c.wv[:, :, 256:640], wv_v[:, :, 2688:3072], [], ["wv"], "wv", eng="pool")

        def consts_C(l):
            lam_init = 0.8 - 0.6 * math.exp(-0.3 * l)
            grp[0] = "cC"
            P.dma(c.ident[:, :], ident_d[:, :], [], ["ident"], ukey("c"))
            P.dma(c.identf[:, :], idf[:, :], [], ["identf"], ukey("c"))
            P.dma(c.gcols[:, 2, :], n2[l], [], ["gcols"], ukey("c"))
            P.dma(c.cw[:, :, :], convw[l].rearrange("p (c k) -> p c k", k=3), [], ["cw"], ukey("c"))
            P.dma(c.lamv[:, :], lamv[l].partition_broadcast(128), [], ["lamv"], ukey("c"))
            P.dma(c.gsub[:, :], gsub[l].partition_broadcast(128), [], ["gsub"], ukey("c"))
            P.dma(c.nf[:, :], nf[l].partition_broadcast(128), [], ["nf"], ukey("c"))
            P.op("dve", lambda e: e.memset(c.epsc[:, :], EPS), [], ["epsc"])
            for q in range(2):
                P.op("dve", lambda e, q=q: e.tensor_tensor(out=c.lamv[:, q * 64:q * 64 + 32],
                                                           in0=c.lamv[:, q * 64:q * 64 + 32],
                                                           in1=c.lamv[:, q * 64 + 32:q * 64 + 64], op=ALU.mult),
                     ["lamv"], ["lamv"])
                P.op("dve", lambda e, q=q: e.tensor_reduce(out=c.lt[:, q:q + 1], in_=c.lamv[:, q * 64:q * 64 + 32],
                                                           axis=mybir.AxisListType.X, op=ALU.add), ["lamv"], ["lt"])
            P.op("act", lambda e: e.activation(out=c.lt[:, 2:4], in_=c.lt[:, 0:2], func=AF.Exp), ["lt"], ["lt"])
            P.op("dve", lambda e: e.tensor_tensor(out=c.nlam[:, 0:1], in0=c.lt[:, 3:4], in1=c.lt[:, 2:3],
                                                  op=ALU.subtract), ["lt"], ["nlam"])
            P.op("dve", lambda e: e.tensor_scalar(out=c.nlam[:, 0:1], in0=c.nlam[:, 0:1], scalar1=-lam_init,
                                                  scalar2=None, op0=ALU.add), ["nlam"], ["nlam"])
            P.op("dve", lambda e: e.tensor_scalar(out=c.gsub[:, :], in0=c.gsub[:, :], scalar1=1.0 - lam_init,
                                                  scalar2=None, op0=ALU.mult), ["gsub"], ["gsub"])
            wo_v = wout[l].rearrange("(kc p) n -> p kc n", p=128)
            for q in range(2):
                P.dma(c.wout[:, q * 4:(q + 1) * 4, :], wo_v[:, q * 4:(q + 1) * 4, :], [], ["wout"], "wout", eng="pool")
            P.op("pool", lambda e: e.indirect_dma_start(
                out=c.hal[:, :], out_offset=None, in_=G3[:, :],
                in_offset=bass.IndirectOffsetOnAxis(ap=idx[:, 3:4], axis=0)), ["G3", "idx"], ["hal"], dkey="hal")
            for ch in range(3):
                P.op("pe", lambda e, ch=ch: e.transpose(out=banks[0][:, ch * 8:(ch + 1) * 8],
                                                        in_=c.hal[0:8, ch * 128:(ch + 1) * 128],
                                                        identity=c.identf[0:8, 0:8]), ["hal", "identf"], ["ps0"])
            P.op("act", lambda e: e.activation(out=c.halT[:, :, :],
                                               in_=banks[0][:, 0:24].rearrange("p (c k) -> p c k", k=8),
                                               func=AF.Copy), ["ps0"], ["halT"])

        def phaseA_tile(l, ti, wdn_reload_first, wdn_reload_after):
            x1s, zs, bgs = x1s_l[l], zs_l[l], bgs_l[l]
            g0 = ti * 512
            s, tq = ti // 4, ti % 4
            if wdn_reload_first is not None:
                load_wdn(P, c, wdn_reload_first)
            norm_T(P, c, c.xt, c.gcols[:, 0, :], "n1")
            ffn(P, c, c.xt, wgu1[l], c.wdn, cnt)
            if wdn_reload_after is not None:
                load_wdn(P, c, wdn_reload_after)
            P.dma(x1s[g0:g0 + 512, :].rearrange("(j p) n -> p j n", p=128), c.xt[:, :, :], ["xt"], ["x1s"], "xo")
            norm_T(P, c, c.xt, c.gcols[:, 1, :], "nm")
            win_v = win[l].rearrange("(kc p) n -> p kc n", p=128)
            chunks = [(0, "u", 0), (128, "u", 1), (256, "u", 2), (768, "cg", 0), (896, "cg", 1), (1024, "cg", 2),
                      (384, "bg", 0), (512, "bg", 1), (640, "bg", 2),
                      (1152, "dq", 0), (1280, "dq", 1), (1408, "dk", 0), (1536, "dk", 1),
                      (1920, "lq", 0), (2048, "lq", 1), (2176, "lq", 2), (2304, "lk", 0), (2432, "lk", 1),
                      (2560, "lk", 2)]
            n = 0
            for (col, kind, ci) in chunks:
                slot = wcnt[0] % 2
                wcnt[0] += 1
                wk = "wch%d" % slot
                wt = c.wch[slot]
                P.dma(wt[:, :, :], win_v[:, :, col:col + 128], [], [wk], wk, eng="pool")
                pp = c.ps[n % 2]
                pk = "ps%d" % (n % 2)
                n += 1
                for kc in range(8):
                    P.op("pe", lambda e, kc=kc, wt=wt, pp=pp: e.matmul(pp[:, :], lhsT=wt[:, kc, :], rhs=c.hT[:, kc, :],
                                                                       start=(kc == 0), stop=(kc == 7)),
                         [wk, "hT"], [pk])
                if kind == "u":
                    P.op("act", lambda e, ci=ci, pp=pp: e.activation(out=c.usb[:, ci, :], in_=pp[:, :], func=AF.Copy),
                         [pk], [c.k["usb"]])
                elif kind == "cg":
                    P.op("dve", lambda e, ci=ci, pp=pp: e.tensor_tensor(out=c.zt[:, ci, :], in0=pp[:, :],
                                                                        in1=c.usb[:, ci, :], op=ALU.mult),
                         [pk, c.k["usb"]], [c.k["zt"]])
                elif kind == "bg":
                    P.op("act", lambda e, ci=ci, pp=pp: e.activation(out=c.bgt[:, ci, :], in_=pp[:, :], func=AF.Copy),
                         [pk], [c.k["bgt"]])
                else:
                    is32 = kind in ("dq", "dk")
                    gi = {"dq": 0, "dk": 1, "lq": 2, "lk": 3}[kind]
                    b = n % 2
                    sq, rq, qk = c.sq[b], c.rq[b], c.qk[b]
                    pq = c.ps[2 + b]
                    pqk = "ps%d" % (2 + b)
                    P.op("act", lambda e, sq=sq, pp=pp: e.activation(out=sq[:, :], in_=pp[:, :], func=AF.Square),
                         [pk], ["sq%d" % b])
                    bd = c.bd32 if is32 else c.bd64
                    P.op("pe", lambda e, sq=sq, pq=pq, bd=bd: e.matmul(pq[:, :], lhsT=bd[:, :], rhs=sq[:, :], start=True,
                                                                       stop=True), ["sq%d" % b, "bd"], [pqk])
                    P.op("act", lambda e, rq=rq, pq=pq, is32=is32: e.activation(
                        out=rq[:, :], in_=pq[:, :], func=AF.Sqrt, bias=c.epsc[:, 0:1],
                        scale=1.0 / (32 if is32 else 64)), [pqk, "epsc"], ["rq%d" % b])
                    P.op("dve", lambda e, rq=rq: e.reciprocal(out=rq[:, :], in_=rq[:, :]), ["rq%d" % b], ["rq%d" % b])
                    P.op("dve", lambda e, rq=rq, qk=qk, pp=pp, gi=gi: e.scalar_tensor_tensor(
                        out=qk[:, :], in0=pp[:, :], scalar=c.gq[:, gi:gi + 1], in1=rq[:, :], op0=ALU.mult,
                        op1=ALU.mult), [pk, "gq", "rq%d" % b], ["qk%d" % b])
                    if is32:
                        sec = QD0 if kind == "dq" else KD0
                        for g in range(4):
                            u = s * 8 + ci * 4 + g
                            dest, sl = u // 3, u % 3
                            r0 = dest * RPD + sec + sl * 32
                            P.dma(send1[r0:r0 + 32, tq * 512:(tq + 1) * 512], qk[32 * g:32 * g + 32, :],
                                  ["qk%d" % b], ["send1"], "qko%d" % b)
                    else:
                        sec = QL0 if kind == "lq" else KL0
                        for g in range(2):
                            v = s * 6 + ci * 2 + g
                            dest, sl = v // 3, v % 3
                            r0 = dest * RPD + sec + sl * 64
                            P.dma(send1[r0:r0 + 64, tq * 512:(tq + 1) * 512], qk[64 * g:64 * g + 64, :],
                                  ["qk%d" % b], ["send1"], "qko%d" % b)
            P.dma(zs[:, g0:g0 + 512].rearrange("(c p) n -> p c n", p=128), c.zt[:, :, :], [c.k["zt"]], ["zs"], "zo")
            P.dma(bgs[:, g0:g0 + 512].rearrange("(c p) n -> p c n", p=128), c.bgt[:, :, :], [c.k["bgt"]], ["bgs"],
                  "bgo")
            if tq == 0 or tq == 3:
                row = s * 2 + (0 if tq == 0 else 1)
                col = 0 if tq == 0 else 511
                dst = bass.AP(tensor=zb.tensor, offset=row * 384, ap=[[1, 128], [128, 3], [1, 1]])
                P.op("sp", lambda e, dst=dst, col=col: e.dma_start(out=dst, in_=c.zt[:, :, col:col + 1],
                                                                   allow_slow_non_contiguous=True),
                     [c.k["zt"]], ["zb"], dkey="zbo")
            for j in range(4):
                pa, pb = c.ps[4], c.ps[5]
                for kc in range(8):
                    P.op("pe", lambda e, j=j, kc=kc: e.matmul(pa[:, 0:256], lhsT=c.hT[:, kc, j * 128:(j + 1) * 128],
                                                              rhs=c.wv[:, kc, 0:256], start=(kc == 0), stop=(kc == 7)),
                         ["hT", "wv"], ["ps4"])
                for kc in range(8):
                    P.op("pe", lambda e, j=j, kc=kc: e.matmul(pb[:, 0:384], lhsT=c.hT[:, kc, j * 128:(j + 1) * 128],
                                                              rhs=c.wv[:, kc, 256:640], start=(kc == 0),
                                                              stop=(kc == 7)), ["hT", "wv"], ["ps5"])
                P.op("act", lambda e, j=j: e.activation(out=c.vt[:, j, 0:256], in_=pa[:, 0:256], func=AF.Copy),
                     ["ps4"], ["vt"])
                P.op("dve", lambda e, j=j: e.tensor_copy(out=c.vt[:, j, 256:640], in_=pb[:, 0:384]), ["ps5"], ["vt"])
            for hc in range(8):
                u = s * 8 + hc
                dest, sl = u // 3, u % 3
                h = hc // 2
                r0 = dest * 2 * RPD + VD0 + sl * 128
                P.dma(send1v[r0:r0 + 128, tq * 256:(tq + 1) * 256].rearrange("p (j d) -> p j d", d=64),
                      c.vt[:, :, h * 64:(h + 1) * 64], ["vt"], ["send1"], "vo")
            for h6 in range(6):
                v = s * 6 + h6
                dest, sl = v // 3, v % 3
                r0 = dest * 2 * RPD + VL0 + sl * 128
                P.dma(send1v[r0:r0 + 128, tq * 256:(tq + 1) * 256].rearrange("p (j d) -> p j d", d=64),
                      c.vt[:, :, 256 + h6 * 64:256 + (h6 + 1) * 64], ["vt"], ["send1"], "vo")

        def phaseC_tile(l, ti, last):
            x1s, zs, bgs = x1s_l[l], zs_l[l], bgs_l[l]
            g0 = ti * 512
            s, tq = ti // 4, ti % 4
            P.dma(c.xt[:, :, :], x1s[g0:g0 + 512, :].rearrange("(j p) n -> p j n", p=128), ["x1s"], ["xt"], "xt")
            for hc in range(8):
                u = s * 8 + hc
                j, k = u // 3, u % 3
                eo = j * SRC2 + ((k * 4 + tq) * 128) * 256
                P.op("pool", lambda e, hc=hc, eo=eo: e.indirect_dma_start(
                    out=c.od[:, hc, :], out_offset=None, in_=G2[:, :],
                    in_offset=bass.IndirectOffsetOnAxis(ap=idx[:, 2:3], axis=0), element_offset=eo),
                     ["G2", "idx"], ["od"], dkey="od")
            for h6 in range(6):
                v = s * 6 + h6
                j, k = v // 3, v % 3
                eo = j * SRC2 + (((3 + k) * 4 + tq) * 128) * 256
                P.op("pool", lambda e, h6=h6, eo=eo: e.indirect_dma_start(
                    out=c.ol[:, h6, :], out_offset=None, in_=G2[:, :],
                    in_offset=bass.IndirectOffsetOnAxis(ap=idx[:, 2:3], axis=0), element_offset=eo),
                     ["G2", "idx"], ["ol"], dkey="ol")
            zsv = zs.rearrange("(c p) n -> p c n", p=128)
            if tq == 0:
                P.dma(c.zh[:, :, 1:514], zsv[:, :, g0:g0 + 513], ["zs"], ["zh"], "zh")
                P.op("act", lambda e, s=s: e.activation(out=c.zh[:, :, 0:1], in_=c.halT[:, :, s:s + 1], func=AF.Copy),
                     ["halT"], ["zh"])
            elif tq == 3:
                P.dma(c.zh[:, :, 0:513], zsv[:, :, g0 - 1:g0 + 512], ["zs"], ["zh"], "zh")
                P.op("act", lambda e, s=s: e.activation(out=c.zh[:, :, 513:514], in_=c.halT[:, :, 3 + s:4 + s],
                                                        func=AF.Copy), ["halT"], ["zh"])
            else:
                P.dma(c.zh[:, :, :], zsv[:, :, g0 - 1:g0 + 513], ["zs"], ["zh"], "zh")
            P.dma(c.bg[:, :, :], bgs[:, g0:g0 + 512].rearrange("(c p) n -> p c n", p=128), ["bgs"], ["bg"], "bg")
            for ch in range(3):
                P.op("dve", lambda e, ch=ch: e.tensor_scalar(out=c.ca[:, :], in0=c.zh[:, ch, 0:512],
                                                             scalar1=c.cw[:, ch, 0:1], scalar2=None, op0=ALU.mult),
                     ["zh", "cw"], ["ca"])
                P.op("dve", lambda e, ch=ch: e.scalar_tensor_tensor(out=c.ca[:, :], in0=c.zh[:, ch, 1:513],
                                                                    scalar=c.cw[:, ch, 1:2], in1=c.ca[:, :],
                                                                    op0=ALU.mult, op1=ALU.add),
                     ["zh", "cw", "ca"], ["ca"])
                P.op("dve", lambda e, ch=ch: e.scalar_tensor_tensor(out=c.ca[:, :], in0=c.zh[:, ch, 2:514],
                                                                    scalar=c.cw[:, ch, 2:3], in1=c.ca[:, :],
                                                                    op0=ALU.mult, op1=ALU.add),
                     ["zh", "cw", "ca"], ["ca"])
                P.op("dve", lambda e, ch=ch: e.tensor_tensor(out=c.yT[:, ch, :], in0=c.ca[:, :], in1=c.bg[:, ch, :],
                                                             op=ALU.mult), ["ca", "bg"], ["yT"])
            for j in range(4):
                for hh in range(4):
                    o1 = c.od[:, 2 * hh, j * 64:(j + 1) * 64]
                    o2 = c.od[:, 2 * hh + 1, j * 64:(j + 1) * 64]
                    obv = c.ob[:, hh * 64:(hh + 1) * 64]
                    P.op("dve", lambda e, o1=o1, o2=o2, obv=obv: e.scalar_tensor_tensor(
                        out=obv, in0=o2, scalar=c.nlam[:, 0:1], in1=o1, op0=ALU.mult, op1=ALU.add),
                         ["od", "nlam"], ["ob"])
                    P.op("act", lambda e, obv=obv, hh=hh: e.activation(out=c.junk[:, 0:64], in_=obv, func=AF.Square,
                                                                       accum_out=c.ss[:, 4 + hh:5 + hh]),
                         ["ob"], ["junk", "ss"])
                P.op("act", lambda e: e.activation(out=c.rs[:, 4:8], in_=c.ss[:, 4:8], func=AF.Sqrt,
                                                   bias=c.epsc[:, 0:1], scale=1.0 / 64), ["ss", "epsc"], ["rs"])
                P.op("dve", lambda e: e.reciprocal(out=c.rs[:, 4:8], in_=c.rs[:, 4:8]), ["rs"], ["rs"])
                for hh in range(4):
                    obv = c.ob[:, hh * 64:(hh + 1) * 64]
                    P.op("dve", lambda e, obv=obv, hh=hh, j=j: e.scalar_tensor_tensor(
                        out=c.ytm[:, j, hh * 64:(hh + 1) * 64], in0=obv, scalar=c.rs[:, 4 + hh:5 + hh],
                        in1=c.gsub[:, :], op0=ALU.mult, op1=ALU.mult), ["ob", "rs", "gsub"], ["ytm"])
                P.op("act", lambda e, j=j: e.activation(
                    out=c.ytm[:, j, 256:640].rearrange("p (h d) -> p h d", d=64), in_=c.ol[:, :, j * 64:(j + 1) * 64],
                    func=AF.Copy), ["ol"], ["ytm"])
            for kc in range(5):
                pst = c.pst[kc % 2]
                pk = "pst%d" % (kc % 2)
                for j in range(4):
                    P.op("pe", lambda e, j=j, kc=kc, pst=pst: e.transpose(
                        out=pst[:, j * 128:(j + 1) * 128], in_=c.ytm[:, j, kc * 128:(kc + 1) * 128],
                        identity=c.ident[:, :]), ["ytm", "ident"], [pk])
                if kc % 2 == 0:
                    P.op("act", lambda e, kc=kc, pst=pst: e.activation(out=c.yT[:, 3 + kc, :], in_=pst[:, 0:512],
                                                                       func=AF.Copy), [pk], ["yT"])
                else:
                    P.op("dve", lambda e, kc=kc, pst=pst: e.tensor_copy(out=c.yT[:, 3 + kc, :], in_=pst[:, 0:512]),
                         [pk], ["yT"])
            n = 0
            for j in range(4):
                for hf in range(2):
                    pd = c.ps[4 + n % 2]
                    pk = "ps%d" % (4 + n % 2)
                    n += 1
                    for kc in range(8):
                        P.op("pe", lambda e, j=j, hf=hf, kc=kc, pd=pd: e.matmul(
                            pd[:, :], lhsT=c.yT[:, kc, j * 128:(j + 1) * 128],
                            rhs=c.wout[:, kc, hf * 512:(hf + 1) * 512], start=(kc == 0), stop=(kc == 7)),
                             ["yT", "wout"], [pk])
                    P.op("dve", lambda e, j=j, hf=hf, pd=pd: e.tensor_tensor(
                        out=c.xt[:, j, hf * 512:(hf + 1) * 512], in0=pd[:, :],
                        in1=c.xt[:, j, hf * 512:(hf + 1) * 512], op=ALU.add), [pk, "xt"], ["xt"])
            norm_T(P, c, c.xt, c.gcols[:, 2, :], "n2")
            ffn(P, c, c.xt, wgu2[l], c.wdn, cnt)
            for j in range(4):
                P.op("act", lambda e, j=j: e.activation(out=c.junk[:, :], in_=c.xt[:, j, :], func=AF.Square,
                                                        accum_out=c.ss[:, j:j + 1]), ["xt"], ["junk", "ss"])
            P.op("act", lambda e: e.activation(out=c.rs[:, 0:4], in_=c.ss[:, 0:4], func=AF.Sqrt,
                                               bias=c.epsc[:, 0:1], scale=1.0 / D), ["ss", "epsc"], ["rs"])
            P.op("dve", lambda e: e.reciprocal(out=c.rs[:, 0:4], in_=c.rs[:, 0:4]), ["rs"], ["rs"])
            for j in range(4):
                P.op("dve", lambda e, j=j: e.scalar_tensor_tensor(out=c.xt[:, j, :], in0=c.xt[:, j, :],
                                                                  scalar=c.rs[:, j:j + 1], in1=c.nf[:, :],
                                                                  op0=ALU.mult, op1=ALU.mult),
                     ["xt", "rs", "nf"], ["xt"])
            if last:
                P.dma(yo[g0:g0 + 512, :].rearrange("(j p) n -> p j n", p=128), c.xt[:, :, :], ["xt"], ["yo"], "yo")

        def phaseB():
            A.reset()
            qT = A.take([128, 8, 2048], BF16); kT = A.take([128, 8, 2048], BF16)
            V = A.take([128, 128, 65], BF16); H = A.take([128, HW_], BF16)
            Vc = A.take([128, 128, 64], BF16)
            Vcf = Vc.rearrange("p k d -> p (k d)")
            G1h = G1.rearrange("r (h e) -> (r h) e", h=2)
            J = A.take([128, 128], BF16); identf = A.take([128, 128], F32)
            ms = [A.take([128, w], BF16) for (r, _, _, w) in DIL]
            pT = [A.take([128, 512], BF16) for _ in range(3)]
            osb = A.take([65, 512], F32); rc = A.take([128, 4], F32); on = A.take([128, 4, 64], F32)
            cbs = A.take([128, 12], F32)
            qTf = qT.rearrange("p r t -> p (r t)")
            kTf = kT.rearrange("p r t -> p (r t)")
            pss = [banks[0], banks[1]]
            pso, ptr = banks[2], banks[3]
            grp[0] = "cB"
            P.dma(J[:, :], Jd[:, :], [], ["J"], ukey("c"))
            P.dma(identf[:, :], idf[:, :], [], ["identfB"], ukey("c"))
            P.dma(cbs[:, :], cbd[:, :], ["cbd"], ["cbs"], ukey("c"))
            for i in range(3):
                P.dma(ms[i][:, :], mds[i][:, :], [], ["ms"], ukey("c"))
            P.op("dve", lambda e: e.memset(V[:, :, 64:65], 1.0), [], ["Vones"])
            bcnt = [0]

            def load_qk(sec_q, sec_k):
                for r in range(8):
                    P.op("pool", lambda e, r=r: e.indirect_dma_start(
                        out=qT[:, r, :], out_offset=None, in_=G1[:, :],
                        in_offset=bass.IndirectOffsetOnAxis(ap=idx[:, 0:1], axis=0),
                        element_offset=r * SRC1 + sec_q * 2048), ["G1", "idx"], ["qT"], dkey="qT")
                    P.op("pool", lambda e, r=r: e.indirect_dma_start(
                        out=kT[:, r, :], out_offset=None, in_=G1[:, :],
                        in_offset=bass.IndirectOffsetOnAxis(ap=idx[:, 0:1], axis=0),
                        element_offset=r * SRC1 + sec_k * 2048), ["G1", "idx"], ["kT"], dkey="kT")

            def unit(slot, is_dil, pb):
                K = 64 if is_dil else 32
                k = slot - 3 if is_dil else slot
                vsec = (VL0 if is_dil else VD0) + k * 128
                for r in range(8):
                    P.op("pool", lambda e, r=r: e.indirect_dma_start(
                        out=Vcf[:, r * 1024:(r + 1) * 1024], out_offset=None, in_=G1h[:, :],
                        in_offset=bass.IndirectOffsetOnAxis(ap=idx[:, 1:2], axis=0),
                        element_offset=r * SRC1 + vsec * 1024), ["G1", "idx"], ["Vc"], dkey="V")
                P.op("dve", lambda e: e.tensor_copy(out=V[:, :, 0:64], in_=Vc[:, :, :]), ["Vc"], ["V"])
                hsrc = bass.AP(tensor=T.tensor, offset=slot * 3328, ap=[[1, 128], [1, HW_]])
                P.dma(H[:, :], hsrc, ["T"], ["H"], "H", eng="pool")
                for qt in ([5] if dbg == 2 else [0, 5, 31] if dbg else range(32)):
                    qb = qt * 512
                    tiles = []
                    if is_dil:
                        for bi, (r, dmin, dmax, w) in enumerate(DIL):
                            for kt in range(max(0, (qb + dmin) // 128), min(127, (qb + dmax) // 128) + 1):
                                tiles.append((kt, bi))
                    else:
                        tiles = [(kt, None) for kt in range(128)]
                    nt = len(tiles)
                    staged = []

                    def stage1(ti, kt, bi):
                        d = kt * 128 - qb
                        near = is_dil or (-640 <= d <= 1024)
                        i = bcnt[0]
                        bcnt[0] += 1
                        ps = pss[i % 2]
                        pk = "pss%d" % (i % 2)
                        pt = pT[i % 3]
                        ptk = "pT%d" % (i % 3)
                        P.op("pe", lambda e, kt=kt, ps=ps, near=near, qb=qb: e.matmul(
                            ps[:, :], lhsT=kTf[pb:pb + K, kt * 128:(kt + 1) * 128], rhs=qTf[pb:pb + K, qb:qb + 512],
                            start=True, stop=(not near)), ["kT", "qT"], [pk])
                        if near:
                            off = 1536 - d
                            P.op("pe", lambda e, ps=ps, off=off: e.matmul(ps[:, :], lhsT=J[:, :],
                                                                          rhs=H[:, off:off + 512], start=False,
                                                                          stop=True), ["J", "H"], [pk])
                            P.op("act", lambda e, ps=ps, pt=pt: e.activation(out=pt[:, :], in_=ps[:, :], func=AF.Exp),
                                 [pk], [ptk])
                        else:
                            col = slot if d < 0 else 6 + slot
                            P.op("act", lambda e, ps=ps, pt=pt, col=col: e.activation(
                                out=pt[:, :], in_=ps[:, :], func=AF.Exp, bias=cbs[:, col:col + 1], scale=1.0),
                                 [pk, "cbs"], [ptk])
                        if is_dil:
                            r, dmin, dmax, w = DIL[bi]
                            o = dmax - d
                            P.op("dve", lambda e, pt=pt, bi=bi, o=o: e.tensor_tensor(out=pt[:, :], in0=pt[:, :],
                                                                                     in1=ms[bi][:, o:o + 512],
                                                                                     op=ALU.mult), [ptk, "ms"], [ptk])
                        staged.append((ti, kt, pt, ptk))

                    def stage2():
                        ti, kt, pt, ptk = staged.pop(0)
                        P.op("pe", lambda e, kt=kt, pt=pt, ti=ti, nt=nt: e.matmul(
                            pso[0:65, :], lhsT=V[:, kt, 0:65], rhs=pt[:, :], start=(ti == 0), stop=(ti == nt - 1)),
                             ["V", "Vones", ptk], ["pso"])

                    for ti, (kt, bi) in enumerate(tiles):
                        stage1(ti, kt, bi)
                        if ti >= 1:
                            stage2()
                    stage2()
                    P.op("act", lambda e: e.activation(out=osb[:, :], in_=pso[0:65, :], func=AF.Copy), ["pso"],
                         ["osb"])
                    for j in range(4):
                        P.op("pe", lambda e, j=j: e.transpose(out=ptr[:, j * 65:(j + 1) * 65],
                                                              in_=osb[:, j * 128:(j + 1) * 128],
                                                              identity=identf[0:65, 0:65]), ["osb", "identfB"],
                             ["ptr"])
                    ptv = ptr[:, 0:260].rearrange("p (j d) -> p j d", d=65)
                    P.op("dve", lambda e, ptv=ptv: e.reciprocal(out=rc[:, :], in_=ptv[:, :, 64]), ["ptr"], ["rc"])
                    for j in range(4):
                        P.op("dve", lambda e, j=j, ptv=ptv: e.tensor_scalar(out=on[:, j, :], in0=ptv[:, j, 0:64],
                                                                            scalar1=rc[:, j:j + 1], scalar2=None,
                                                                            op0=ALU.mult), ["ptr", "rc"], ["on"])
                    dest, tq = qt // 4, qt % 4
                    r0 = dest * R2D + (slot * 4 + tq) * 128
                    P.dma(send2[r0:r0 + 128, :].rearrange("p (j d) -> p j d", d=64), on[:, :, :], ["on"], ["send2"],
                          "on")

            load_qk(QD0, KD0)
            if dbg == 2:
                unit(0, False, 0)
                P.dma(dqT[:, :], qTf[:, :], ["qT"], ["dqT"], "dbg5")
                P.dma(dkT[:, :], kTf[:, :], ["kT"], ["dkT"], "dbg6")
                P.dma(dV[:, :], V.rearrange("p k d -> p (k d)"), ["V", "Vones"], ["dV"], "dbg7")
                P.dma(dsend1[:, :], send1[:, :], ["send1"], ["dsend1"], "dbg8")
                return
            for slot in range(3):
                unit(slot, False, 32 * slot)
            load_qk(QL0, KL0)
            unit(3, True, 0)
            unit(4, True, 64)
            load_qk(QL0 + 128, KL0 + 128)
            unit(5, True, 0)

        def allgather(src, dst, rkey, wkey):
            P.op("pool", lambda e: e.collective_compute("AllGather", ALU.bypass, replica_groups=[list(range(8))],
                                                        ins=[src.tensor.ap().opt()], outs=[dst.tensor.ap().opt()]),
                 [rkey], [wkey], dkey="ag", dinc=1)

        P.barrier()
        carve_AC()
        consts_A(0)
        load_wdn(P, c, wdn1[0])
        for ti in range(12):
            g0 = ti * 512
            P.dma(c.xt[:, :, :], xin[g0:g0 + 512, :].rearrange("(j p) n -> p j n", p=128), [], ["xt"], "xt")
            phaseA_tile(0, ti, None, None)
        for l in ([0] if dbg else range(2)):
            P.barrier()
            allgather(send1, G1, "send1", "G1")
            allgather(zb, G3, "zb", "G3")
            P.barrier()
            phaseB()
            if dbg == 2:
                break
            P.barrier()
            allgather(send2, G2, "send2", "G2")
            P.barrier()
            carve_AC()
            consts_C(l)
            if l == 0:
                consts_A(1)
            load_wdn(P, c, wdn2[l])
            for ti in range(12):
                if dbg and (ti % 4) == 2:
                    continue
                phaseC_tile(l, ti, last=(l == 1 or dbg))
                if l == 0 and not dbg:
                    phaseA_tile(1, ti, wdn1[1], (wdn2[0] if ti < 11 else None))
            if dbg:
                P.dma(dsend2[:, :], send2[:, :], ["send2"], ["dsend2"], "dbg4")
        P.final_wait()
        P.emit()
    return nc


def _rel_bucket(rel):
    nb, me = 16, 8
    rel = np.asarray(rel, np.int64)
    ret = np.where(rel > 0, nb, 0)
    n = np.abs(rel)
    nf = np.maximum(n, 1).astype(np.float32)
    large = me + (np.log(nf / np.float32(me)) / np.float32(math.log(1024 / me)) * np.float32(nb - me)).astype(np.int32)
    large = np.minimum(large, nb - 1)
    return ret + np.where(n < me, n, large)


def _consts():
    cst = {}
    cst["ident"] = np.eye(128, dtype=np.float32).astype(NPBF)
    cst["identf"] = np.eye(128, dtype=np.float32)
    cst["J"] = np.eye(128, dtype=np.float32)[::-1].copy().astype(NPBF)
    cst["bd32"] = np.kron(np.eye(4, dtype=np.float32), np.ones((32, 32), np.float32)).astype(NPBF)
    cst["bd64"] = np.kron(np.eye(2, dtype=np.float32), np.ones((64, 64), np.float32)).astype(NPBF)
    e15 = np.zeros((32, 128), np.float32); e15[15] = 1
    e31 = np.zeros((32, 128), np.float32); e31[31] = 1
    cst["e15"], cst["e31"] = e15, e31
    m = np.arange(3328)
    bk = _rel_bucket(CC - m)
    oh = (bk[None, :] == np.arange(32)[:, None]).astype(np.float32)
    oh[:, 3327:] = 0
    cst["ohrev"] = oh
    for (r, dmin, dmax, w) in DIL:
        rel = np.arange(128)[:, None] - np.arange(w)[None, :] + dmax
        cst["m%d" % r] = ((rel % r == 0) & (np.abs(rel) <= 64 * r)).astype(np.float32).astype(NPBF)
    return cst


def _col(v):
    v = np.asarray(v, np.float32)
    return np.ascontiguousarray(v.reshape(v.shape[0], -1, 128).transpose(0, 2, 1))


_PROGS = {}


def kernel(x_prompt, x_sample, ffn1_norm, ffn1_w_gu, ffn1_w_down, mix_norm, w_in, conv_w, diff_q_norm, diff_k_norm,
           lambda_q1, lambda_k1, lambda_q2, lambda_k2, diff_sub_norm, dil_q_norm, dil_k_norm, w_out, ffn2_norm,
           ffn2_w_gu, ffn2_w_down, final_norm, rel_bias):
    f = lambda a: np.ascontiguousarray(np.asarray(a, np.float32))
    cst = _consts()
    xs = [f(x_prompt)[0], f(x_sample)[0], f(x_sample)[1]]
    cores = list(range(NCORE))
    rel_bias = f(rel_bias)
    gq = np.stack([np.tile(f(diff_q_norm), (1, 4)), np.tile(f(diff_k_norm), (1, 4)), np.tile(f(dil_q_norm), (1, 2)),
                   np.tile(f(dil_k_norm), (1, 2))], 2)
    cw = f(conv_w)
    cwl = np.ascontiguousarray(cw.reshape(2, 3, 3, 128).transpose(0, 3, 2, 1).reshape(2, 128, 9))
    lamv = np.concatenate([f(lambda_q1), f(lambda_k1), f(lambda_q2), f(lambda_k2)], 1)[:, None, :]
    shared = {
        "n1": _col(ffn1_norm), "nm": _col(mix_norm), "n2": _col(ffn2_norm), "nf": f(final_norm)[:, None, :],
        "gq": np.ascontiguousarray(gq.astype(np.float32)), "conv_w": cwl,
        "lamv": np.ascontiguousarray(lamv), "gsub": f(diff_sub_norm)[:, None, :],
    }
    shared.update(cst)
    maps = []
    p = np.arange(128)
    for c in cores:
        xc = np.concatenate([xs[s][c * SL:(c + 1) * SL] for s in range(3)], 0)
        idx = np.zeros((128, 4), np.int32)
        idx[:, 0] = c * RPD + p
        idx[:, 1] = c * 2 * RPD + p
        idx[:, 2] = c * R2D + p
        idx[:, 3] = c * 7 + 6
        for s in range(3):
            if c > 0:
                idx[s, 3] = (c - 1) * 7 + s * 2 + 1
            if c < NCORE - 1:
                idx[3 + s, 3] = (c + 1) * 7 + s * 2
        cols = []
        for k in range(3):
            u = 3 * c + k
            cols.append((u % 8) // 2)
        for k in range(3):
            v = 3 * c + k
            cols.append(4 + (v % 6))
        m = dict(shared)
        for nm_, arr in (("w_gu1", ffn1_w_gu), ("w_dn1", ffn1_w_down), ("w_gu2", ffn2_w_gu), ("w_dn2", ffn2_w_down),
                         ("w_in", w_in), ("w_out", w_out)):
            a = np.asarray(arr, np.float32)
            kk = a.shape[1] // 8
            m[nm_] = np.ascontiguousarray(a[:, c * kk:(c + 1) * kk, :])
        m.update({"x": xc, "idx": idx, "tabc": np.ascontiguousarray(rel_bias[:, cols])})
        maps.append(m)
    if "F" not in _PROGS:
        _PROGS["F"] = build_fused()
    res = run_bass_kernel_spmd(_PROGS["F"], maps, core_ids=cores).results
    y = [np.asarray(r["y"]) for r in res]
    ys = [np.concatenate([y[c][s * SL:(s + 1) * SL] for c in cores], 0) for s in range(3)]
    y_prompt = ys[0][None].astype(np.float32)
    y_sample = np.stack([ys[1], ys[2]], 0).astype(np.float32)
    return (y_prompt, y_sample)
```
